# Optimizing a Trainium2 kernel written in Bass

```python
import math
import jax, jax.numpy as jnp
from jax import lax
import numpy as np

D_MODEL = 1024
BATCH = 16
SEQ = 4096
DEPTH = 1

DA_HEADS = 4
DA_HEAD_DIM = 64
DA_V_DIM = 2 * DA_HEAD_DIM
DA_QK_COLS = DA_HEADS * 2 * DA_HEAD_DIM
DA_V_COLS = DA_HEADS * DA_V_DIM
GDN_HEADS = 4
GDN_K_DIM = 128
GDN_V_DIM = 128
GDN_QK_COLS = GDN_HEADS * GDN_K_DIM
GDN_V_COLS = GDN_HEADS * GDN_V_DIM
GDN_CONV_CH = 2 * GDN_QK_COLS + GDN_V_COLS
SHORT_CONV = 4
CHUNK = 64
IN_SIZES = (DA_QK_COLS, DA_QK_COLS, DA_V_COLS, GDN_QK_COLS, GDN_QK_COLS, GDN_V_COLS, GDN_V_COLS, GDN_HEADS, GDN_HEADS)
IN_COLS = DA_QK_COLS * 2 + DA_V_COLS + GDN_QK_COLS * 2 + GDN_V_COLS * 2 + GDN_HEADS * 2
MIX_WIDTH = DA_V_COLS + GDN_V_COLS
D_FF = 2816
FFN_CONV = 3
Q_BLOCK = 128
ROPE_THETA = 10000.0
EPS = 1e-6

kernel_name = "hymba_diffattn_gdn_convffn"


def rms_norm(x, w):
    xf = x.astype(jnp.float32)
    y = xf * lax.rsqrt(jnp.mean(xf * xf, axis=-1, keepdims=True) + EPS)
    return (y * w.astype(jnp.float32)).astype(x.dtype)


def causal_depthwise_conv(x, w):
    k, c = w.shape
    return lax.conv_general_dilated(
        x, w[:, None, :].astype(x.dtype), window_strides=(1,), padding=[(k - 1, 0)],
        dimension_numbers=("NWC", "WIO", "NWC"), feature_group_count=c)


def apply_rope(x):
    t, d = x.shape[1], x.shape[-1]
    inv_freq = ROPE_THETA ** (-jnp.arange(0, d, 2, dtype=jnp.float32) / d)
    ang = jnp.arange(t, dtype=jnp.float32)[:, None] * inv_freq[None, :]
    shape = (t,) + (1,) * (x.ndim - 3) + (d // 2,)
    cos, sin = jnp.cos(ang).reshape(shape), jnp.sin(ang).reshape(shape)
    xf = x.astype(jnp.float32)
    x1, x2 = xf[..., : d // 2], xf[..., d // 2:]
    return jnp.concatenate([x1 * cos - x2 * sin, x2 * cos + x1 * sin], axis=-1).astype(x.dtype)


def l2_normalize(x):
    return x * lax.rsqrt(jnp.sum(x * x, axis=-1, keepdims=True) + EPS)


def diff_attention(q, k, v, lam_q1, lam_k1, lam_q2, lam_k2, subln_w, lambda_init):
    b, t, h = q.shape[0], q.shape[1], q.shape[2]
    q = apply_rope(q) * (DA_HEAD_DIM ** -0.5)
    k = apply_rope(k)
    lam = (jnp.exp(jnp.sum(lam_q1.astype(jnp.float32) * lam_k1.astype(jnp.float32)))
           - jnp.exp(jnp.sum(lam_q2.astype(jnp.float32) * lam_k2.astype(jnp.float32))) + lambda_init)
    nb = t // Q_BLOCK
    qb = q.reshape(b, nb, Q_BLOCK, h, 2, DA_HEAD_DIM).transpose(1, 0, 3, 4, 2, 5)
    kt = k.transpose(0, 2, 3, 1, 4)
    vt = v.transpose(0, 2, 1, 3)
    k_pos = jnp.arange(t)

    def one_block(args):
        q_blk, blk = args
        s = jnp.einsum("bhcqd,bhckd->bhcqk", q_blk, kt).astype(jnp.float32)
        q_pos = blk * Q_BLOCK + jnp.arange(Q_BLOCK)
        s = jnp.where(k_pos[None, :] <= q_pos[:, None], s, -jnp.inf)
        p = jax.nn.softmax(s, axis=-1)
        a = p[:, :, 0] - lam * p[:, :, 1]
        return jnp.einsum("bhqk,bhkd->bqhd", a.astype(vt.dtype), vt)

    o = lax.map(one_block, (qb, jnp.arange(nb)))
    o = o.transpose(1, 0, 2, 3, 4).reshape(b, t, h, DA_V_DIM)
    o = rms_norm(o, subln_w) * (1.0 - lambda_init)
    return o.reshape(b, t, h * DA_V_DIM)


def gated_deltanet(q, k, v, z, beta_logit, a, conv_w, a_log, dt_bias, norm_w):
    b, t = q.shape[0], q.shape[1]
    h, dk, dv, c = GDN_HEADS, GDN_K_DIM, GDN_V_DIM, CHUNK
    f32 = jnp.float32
    qkv = jax.nn.silu(causal_depthwise_conv(jnp.concatenate([q, k, v], axis=-1), conv_w))
    q, k, v = jnp.split(qkv, [GDN_QK_COLS, 2 * GDN_QK_COLS], axis=-1)
    q = l2_normalize(q.reshape(b, t, h, dk).astype(f32)) * (dk ** -0.5)
    k = l2_normalize(k.reshape(b, t, h, dk).astype(f32))
    v = v.reshape(b, t, h, dv).astype(f32)
    beta = jax.nn.sigmoid(beta_logit.astype(f32))
    g = -jnp.exp(a_log.astype(f32)) * jax.nn.softplus(a.astype(f32) + dt_bias.astype(f32))
    n = t // c

    def chunks(x):
        return x.reshape(b, n, c, h, -1).transpose(0, 3, 1, 2, 4)

    qc, kc, vc = chunks(q), chunks(k), chunks(v)
    bc = beta.reshape(b, n, c, h).transpose(0, 3, 1, 2)
    gcum = jnp.cumsum(g.reshape(b, n, c, h).transpose(0, 3, 1, 2), axis=-1)
    tril = jnp.tril(jnp.ones((c, c), dtype=bool))
    strict = jnp.tril(jnp.ones((c, c), dtype=bool), -1)
    decay_mat = jnp.exp(jnp.where(tril, gcum[..., :, None] - gcum[..., None, :], -jnp.inf))
    kb = kc * bc[..., None]
    kkt = jnp.einsum("bhncd,bhnsd->bhncs", kb, kc) * decay_mat
    a_mat = jnp.eye(c, dtype=f32) + jnp.where(strict, kkt, 0.0)
    rhs = jnp.concatenate([vc * bc[..., None], kb * jnp.exp(gcum)[..., None]], axis=-1)
    sol = lax.linalg.triangular_solve(a_mat, rhs, left_side=True, lower=True, unit_diagonal=True)
    u, w = sol[..., :dv], sol[..., dv:]
    qk = jnp.einsum("bhncd,bhnsd->bhncs", qc, kc) * decay_mat
    q_dec = qc * jnp.exp(gcum)[..., None]
    k_dec = kc * jnp.exp(gcum[..., -1:] - gcum)[..., None]
    g_last = jnp.exp(gcum[..., -1])

    def step(state, xs):
        u_i, w_i, qk_i, qd_i, kd_i, gl_i = xs
        v_new = u_i - jnp.einsum("bhcd,bhdv->bhcv", w_i, state)
        o_i = jnp.einsum("bhcd,bhdv->bhcv", qd_i, state) + jnp.einsum("bhcs,bhsv->bhcv", qk_i, v_new)
        state = state * gl_i[..., None, None] + jnp.einsum("bhcd,bhcv->bhdv", kd_i, v_new)
        return state, o_i

    xs = tuple(jnp.moveaxis(x_, 2, 0) for x_ in (u, w, qk, q_dec, k_dec, g_last))
    s0 = jnp.zeros((b, h, dk, dv), f32)
    _, o = lax.scan(step, s0, xs)
    o = o.transpose(1, 0, 3, 2, 4).reshape(b, t, h, dv).astype(z.dtype)
    o = rms_norm(o, norm_w) * jax.nn.silu(z.reshape(b, t, h, dv))
    return o.reshape(b, t, h * dv)


def conv_glu_ffn(x, w_up, conv_w, w_down):
    hid = jnp.einsum("btd,df->btf", x, w_up)
    hid = causal_depthwise_conv(hid, conv_w)
    gate, up = jnp.split(hid, 2, axis=-1)
    return jnp.einsum("btf,fd->btd", jax.nn.silu(gate) * up, w_down)


def setup_inputs(seed: int = 0) -> dict:
    key = jax.random.key(seed)
    ks = jax.random.split(key, 18)
    L = DEPTH
    f32 = jnp.float32

    def nrm(k, shape, scale):
        return jax.random.normal(k, shape, f32) * scale

    dt = jnp.exp(jax.random.uniform(ks[10], (L, GDN_HEADS), f32, math.log(1e-3), math.log(1e-1)))
    return {
        "x": nrm(ks[0], (BATCH, SEQ, D_MODEL), 1.0),
        "attn_norm_w": 1.0 + nrm(ks[1], (L, D_MODEL), 0.02),
        "w_in": nrm(ks[2], (L, D_MODEL, IN_COLS), D_MODEL ** -0.5),
        "da_lambda_q1": nrm(ks[3], (L, DA_HEAD_DIM), 0.1),
        "da_lambda_k1": nrm(ks[4], (L, DA_HEAD_DIM), 0.1),
        "da_lambda_q2": nrm(ks[5], (L, DA_HEAD_DIM), 0.1),
        "da_lambda_k2": nrm(ks[6], (L, DA_HEAD_DIM), 0.1),
        "da_subln_w": 1.0 + nrm(ks[7], (L, DA_V_DIM), 0.02),
        "gdn_conv_w": nrm(ks[8], (L, SHORT_CONV, GDN_CONV_CH), SHORT_CONV ** -0.5),
        "gdn_a_log": jnp.log(jax.random.uniform(ks[9], (L, GDN_HEADS), f32, 1.0, 16.0)),
        "gdn_dt_bias": dt + jnp.log(-jnp.expm1(-dt)),
        "gdn_norm_w": 1.0 + nrm(ks[11], (L, GDN_V_DIM), 0.02),
        "w_out": nrm(ks[12], (L, MIX_WIDTH, D_MODEL), MIX_WIDTH ** -0.5),
        "ffn_norm_w": 1.0 + nrm(ks[13], (L, D_MODEL), 0.02),
        "ffn_w_up": nrm(ks[14], (L, D_MODEL, 2 * D_FF), D_MODEL ** -0.5),
        "ffn_conv_w": nrm(ks[15], (L, FFN_CONV, 2 * D_FF), FFN_CONV ** -0.5),
        "ffn_w_down": nrm(ks[16], (L, D_FF, D_MODEL), D_FF ** -0.5),
        "final_norm_w": 1.0 + nrm(ks[17], (D_MODEL,), 0.02),
    }


def reference(x, attn_norm_w, w_in, da_lambda_q1, da_lambda_k1, da_lambda_q2, da_lambda_k2,
              da_subln_w, gdn_conv_w, gdn_a_log, gdn_dt_bias, gdn_norm_w, w_out,
              ffn_norm_w, ffn_w_up, ffn_conv_w, ffn_w_down, final_norm_w):
    b, t = x.shape[0], x.shape[1]
    split_idx = [int(i) for i in np.cumsum(IN_SIZES)[:-1]]
    h = x
    for l in range(DEPTH):
        lambda_init = 0.8 - 0.6 * math.exp(-0.3 * l)
        xn = rms_norm(h, attn_norm_w[l])
        proj = jnp.einsum("btd,dc->btc", xn, w_in[l])
        da_q, da_k, da_v, g_q, g_k, g_v, g_z, g_b, g_a = jnp.split(proj, split_idx, axis=-1)
        y_da = diff_attention(
            da_q.reshape(b, t, DA_HEADS, 2, DA_HEAD_DIM), da_k.reshape(b, t, DA_HEADS, 2, DA_HEAD_DIM),
            da_v.reshape(b, t, DA_HEADS, DA_V_DIM), da_lambda_q1[l], da_lambda_k1[l],
            da_lambda_q2[l], da_lambda_k2[l], da_subln_w[l], lambda_init)
        y_gdn = gated_deltanet(g_q, g_k, g_v, g_z, g_b, g_a, gdn_conv_w[l], gdn_a_log[l],
                               gdn_dt_bias[l], gdn_norm_w[l])
        mix = jnp.concatenate([y_da, y_gdn.astype(y_da.dtype)], axis=-1)
        h = h + jnp.einsum("btc,cd->btd", mix, w_out[l])
        h = h + conv_glu_ffn(rms_norm(h, ffn_norm_w[l]), ffn_w_up[l], ffn_conv_w[l], ffn_w_down[l])
    return rms_norm(h, final_norm_w)
```

```python
import contextlib
import math
import numpy as np
import ml_dtypes
import concourse.bass as bass
import concourse.mybir as mybir
from concourse.bass_utils import run_bass_kernel_spmd

F32 = mybir.dt.float32
BF16 = mybir.dt.bfloat16
AF = mybir.ActivationFunctionType
ALU = mybir.AluOpType
AX = mybir.AxisListType

ENGS = ("sync", "scalar", "gpsimd", "vector", "tensor")
NDMA = 8
EPS = 1e-6
D = 1024
DFF = 2816
INC = 3592
LAMBDA_INIT = 0.8 - 0.6 * math.exp(-0.3 * 0)


class Buf:
    def __init__(self, name, t):
        self.name = name
        self.t = t
        self.writers = []
        self.readers = []
        self.gen_deps = set()
        self.psum = False

    def __getitem__(self, idx):
        return self.t[idx]


class Op:
    __slots__ = ("eng", "fn", "deps", "is_dma", "signal", "tok", "idx")

    def __init__(self, eng, fn, is_dma):
        self.eng = eng
        self.fn = fn
        self.deps = set()
        self.is_dma = is_dma
        self.signal = False
        self.tok = None


class Prog:
    def __init__(self, nc):
        self.nc = nc
        self.ops = []

    def sb(self, name, shape, dt=F32):
        return Buf(name, self.nc.alloc_sbuf_tensor(name, list(shape), dt))

    def ps(self, name, shape, dt=F32):
        b = Buf(name, self.nc.alloc_psum_tensor(name, list(shape), dt))
        b.psum = True
        return b

    def dram(self, name, shape, dt=F32, kind="Internal"):
        return Buf(name, self.nc.dram_tensor(name, list(shape), dt, kind=kind))

    def op(self, eng, fn, reads=(), writes=(), dma=False, nowaw=False):
        o = Op(eng, fn, dma)
        o.idx = len(self.ops)
        deps = o.deps
        for r in reads:
            deps.update(r.writers)
            if r.psum:
                for ri in r.readers:
                    if self.ops[ri].eng != eng:
                        deps.add(ri)
        for w in writes:
            if nowaw and w.writers:
                deps.update(w.gen_deps)
                deps.update(w.readers)
                w.gen_deps.update(w.readers)
                w.writers.append(o.idx)
                w.readers = []
            else:
                g = set(w.writers) | set(w.readers)
                deps.update(g)
                w.gen_deps = g
                w.writers = [o.idx]
                w.readers = []
        for r in reads:
            r.readers.append(o.idx)
        deps.discard(o.idx)
        self.ops.append(o)
        return o

    def dma(self, eng, out_ap, in_ap, reads, writes, nowaw=False):
        return self.op(eng, lambda e: e.dma_start(out=out_ap, in_=in_ap), reads, writes, dma=True, nowaw=nowaw)

    def mm(self, out_ap, lhsT, rhs, start, stop, reads, writes, nowaw=False):
        return self.op("tensor", lambda e: e.matmul(out_ap, lhsT=lhsT, rhs=rhs, start=start, stop=stop),
                       reads, writes, nowaw=nowaw)

    def tr(self, out_ap, in_ap, ident, reads, writes, nowaw=False):
        return self.op("tensor", lambda e: e.transpose(out_ap, in_ap, ident), reads, writes, nowaw=nowaw)

    def act(self, out_ap, in_ap, func, reads, writes, scale=1.0, bias=None, accum=None, eng="scalar", nowaw=False):
        def f(e):
            kw = dict(out=out_ap, in_=in_ap, func=func, scale=scale)
            if bias is not None:
                kw["bias"] = bias
            if accum is not None:
                kw["accum_out"] = accum
            return e.activation(**kw)
        return self.op(eng, f, reads, writes, nowaw=nowaw)

    def cp(self, eng, out_ap, in_ap, reads, writes, nowaw=False):
        if eng == "scalar":
            return self.op(eng, lambda e: e.copy(out=out_ap, in_=in_ap), reads, writes, nowaw=nowaw)
        return self.op(eng, lambda e: e.tensor_copy(out=out_ap, in_=in_ap), reads, writes, nowaw=nowaw)

    def tt(self, eng, out_ap, in0, in1, op, reads, writes, nowaw=False):
        return self.op(eng, lambda e: e.tensor_tensor(out=out_ap, in0=in0, in1=in1, op=op), reads, writes, nowaw=nowaw)

    def ts(self, eng, out_ap, in0, s1, s2, op0, op1, reads, writes, nowaw=False):
        if s2 is None:
            return self.op(eng, lambda e: e.tensor_scalar(out=out_ap, in0=in0, scalar1=s1, scalar2=None, op0=op0),
                           reads, writes, nowaw=nowaw)
        return self.op(eng, lambda e: e.tensor_scalar(out=out_ap, in0=in0, scalar1=s1, scalar2=s2, op0=op0, op1=op1),
                       reads, writes, nowaw=nowaw)

    def stt(self, eng, out_ap, in0, scalar, in1, op0, op1, reads, writes, nowaw=False):
        return self.op(eng, lambda e: e.scalar_tensor_tensor(out=out_ap, in0=in0, scalar=scalar, in1=in1, op0=op0, op1=op1),
                       reads, writes, nowaw=nowaw)

    def memset(self, eng, ap, val, writes, nowaw=False):
        return self.op(eng, lambda e: e.memset(ap, val), [], writes, nowaw=nowaw)

    def emit(self, final_wait_eng="sync"):
        nc = self.nc
        import os
        kcut = int(os.environ.get("KCUT", "0"))
        if kcut:
            self.ops = self.ops[:kcut]
        ops = self.ops
        print("emit: n_ops =", len(ops), flush=True)
        for o in ops:
            for d in o.deps:
                od = ops[d]
                if od.eng == "tensor" and o.eng == "tensor" and not od.is_dma and not o.is_dma:
                    continue
                od.signal = True
            if o.is_dma:
                o.signal = True
        with contextlib.ExitStack() as st:
            esem = {e: st.enter_context(nc.semaphore("s_" + e)) for e in ENGS}
            dsem = {e: [st.enter_context(nc.semaphore("d_%s%d" % (e, i))) for i in range(NDMA)]
                    for e in ("sync", "scalar", "gpsimd")}
            ecount = {e: 0 for e in ENGS}
            dcount = {e: [0] * NDMA for e in dsem}
            drot = {e: 0 for e in dsem}
            prewait = {}
            for o in ops:
                if not o.signal:
                    continue
                if o.is_dma:
                    i = drot[o.eng]
                    drot[o.eng] = (i + 1) % NDMA
                    prewait[o.idx] = (dsem[o.eng][i], dcount[o.eng][i])
                    dcount[o.eng][i] += 16
                    o.tok = (dsem[o.eng][i], dcount[o.eng][i], 16)
                else:
                    ecount[o.eng] += 1
                    o.tok = (esem[o.eng], ecount[o.eng], 1)
            by_eng = {e: [o for o in ops if o.eng == e] for e in ENGS}
            block = st.enter_context(nc.Block())

            def run(e, eng):
                known = {}
                for o in by_eng[e]:
                    waits = {}
                    for d in o.deps:
                        od = ops[d]
                        if od.tok is None:
                            continue
                        if od.eng == "tensor" and e == "tensor" and not od.is_dma and not o.is_dma:
                            continue
                        s, v, _ = od.tok
                        k = id(s)
                        if known.get(k, 0) >= v:
                            continue
                        if k not in waits or waits[k][1] < v:
                            waits[k] = (s, v)
                    if o.idx in prewait:
                        s, v = prewait[o.idx]
                        k = id(s)
                        if v > 0 and known.get(k, 0) < v and (k not in waits or waits[k][1] < v):
                            waits[k] = (s, v)
                    for k, (s, v) in waits.items():
                        eng.wait_ge(s, v)
                        known[k] = v
                    ins = o.fn(eng)
                    if o.tok is not None:
                        ins.then_inc(o.tok[0], o.tok[2])
                if e == final_wait_eng:
                    for q in dsem:
                        for i in range(NDMA):
                            if dcount[q][i] > 0:
                                eng.wait_ge(dsem[q][i], dcount[q][i])

            @block.sync
            def _(eng):
                run("sync", eng)

            @block.scalar
            def _(eng):
                run("scalar", eng)

            @block.gpsimd
            def _(eng):
                run("gpsimd", eng)

            @block.vector
            def _(eng):
                run("vector", eng)

            @block.tensor
            def _(eng):
                run("tensor", eng)


def build(T, NSEQ, debug=False, do_c=True, stop_after=None):
    nc = bass.Bass("TRN2", target_bir_lowering=False)
    P = Prog(nc)
    NT = T // 512
    NKT = T // 128
    dk = "ExternalOutput"

    def din(name, shape, dt=F32):
        return P.dram(name, shape, dt, kind="ExternalInput")

    x_d = din("x", [NSEQ, T, D])
    anw_d = din("attn_norm_w", [D])
    win_d = din("w_in", [D, INC])
    lq1_d = din("da_lambda_q1", [64]); lk1_d = din("da_lambda_k1", [64])
    lq2_d = din("da_lambda_q2", [64]); lk2_d = din("da_lambda_k2", [64])
    subln_d = din("da_subln_w", [128])
    gcw_d = din("gdn_conv_w", [128, 12, 4])
    alog_d = din("gdn_a_log", [4]); dtb_d = din("gdn_dt_bias", [4])
    gnw_d = din("gdn_norm_w", [128])
    wout_d = din("w_out", [D, D])
    fnw_d = din("ffn_norm_w", [D])
    wup_d = din("ffn_w_up", [D, 2 * DFF])
    fcw_d = din("ffn_conv_w", [128, 44, 3])
    wdn_d = din("ffn_w_down", [DFF, D])
    finw_d = din("final_norm_w", [D])
    identb_d = din("c_identb", [128, 128], BF16)
    identf_d = din("c_identf", [128, 128])
    rot_d = din("c_rot", [128, 128], BF16)
    rope_d = din("c_rope", [128, 2, T])
    trim_d = din("c_trimask", [128, 128], BF16)
    cm_d = din("c_masks", [64, 5, 64])
    out_d = P.dram("out", [NSEQ, T, D], F32, kind="ExternalOutput")

    qT_d = P.dram("s_qT", [NSEQ, 4, 128, T], BF16, kind=dk)
    kT_d = P.dram("s_kT", [NSEQ, 4, 128, T], BF16, kind=dk)
    vda_d = P.dram("s_vda", [NSEQ, T, 512], BF16, kind=dk)
    gqT_d = P.dram("s_gqT", [NSEQ, 4, 128, T], BF16, kind=dk)
    gkT_d = P.dram("s_gkT", [NSEQ, 4, 128, T], BF16, kind=dk)
    gkn_d = P.dram("s_gkn", [NSEQ, T, 512], BF16, kind=dk)
    gv_d = P.dram("s_gv", [NSEQ, T, 512], BF16, kind=dk)
    gz_d = P.dram("s_gz", [NSEQ, T, 512], BF16, kind=dk)
    gsc_d = P.dram("s_gsc", [NSEQ, T, 16], F32, kind=dk)
    mix_d = P.dram("s_mix", [NSEQ, T, D], BF16, kind=dk)
    wupb_d = P.dram("s_wupb", [D, 2 * DFF], BF16)
    wdnb_d = P.dram("s_wdnb", [DFF, D], BF16)
    def units(n):
        return [Buf("u", None) for _ in range(n)]
    u_qT = units(NSEQ * NT); u_kT = units(NSEQ * NT); u_vda = units(NSEQ * NT)
    u_gqT = units(NSEQ * NT); u_gkT = units(NSEQ * NT); u_gkn = units(NSEQ * NT)
    u_gv = units(NSEQ * NT); u_gz = units(NSEQ * NT); u_gsc = units(NSEQ * NT)
    u_mixa = units(NSEQ * NT); u_mixg = units(NSEQ * NT)
    u_wupb = Buf("u", None); u_wdnb = Buf("u", None)
    CONST = Buf("const_in", None)
    CONST_OUT = Buf("const_out", None)

    PB = [P.ps("pb%d" % i, [128, 512]) for i in range(8)]

    def pbf(i):
        return PB[i].t[:, :].bitcast(BF16)

    ARENA_BYTES = 196 * 1024
    arena = nc.alloc_sbuf_tensor("arena", [128, ARENA_BYTES // 4], F32)
    live = []
    cur = [0]

    def aalloc(name, free_shape, dt):
        n = int(np.prod(free_shape))
        nb = n * (2 if dt == BF16 else 4)
        nb = (nb + 63) // 64 * 64
        s = cur[0]
        e = s + nb
        assert e <= ARENA_BYTES, (name, e)
        cur[0] = e
        ap = arena[:, s // 4:e // 4]
        if dt == BF16:
            ap = ap.bitcast(BF16)
        ap = ap[:, 0:n]
        if len(free_shape) == 2:
            ap = ap.rearrange("p (a b) -> p a b", a=free_shape[0])
        elif len(free_shape) == 3:
            ap = ap.rearrange("p (a b c) -> p a b c", a=free_shape[0], b=free_shape[1])
        elif len(free_shape) == 4:
            ap = ap.rearrange("p (a b c d) -> p a b c d", a=free_shape[0], b=free_shape[1], c=free_shape[2])
        b = Buf(name, ap)
        for (s0, e0, ob) in live:
            if s0 < e and s < e0:
                b.readers.extend(ob.writers)
                b.readers.extend(ob.readers)
        live.append((s, e, b))
        return b

    def areset(mark=0):
        cur[0] = mark

    identb = P.sb("identb", [128, 128], BF16)
    identf = P.sb("identf", [128, 128])
    rotm = P.sb("rotm", [128, 128], BF16)
    trim = P.sb("trim", [128, 128], BF16)
    cm = P.sb("cm", [64, 5, 64])
    negh = P.sb("negh", [128, 64])
    ones_f = P.sb("ones_f", [64, 128])
    P.dma("sync", identb[:, :], identb_d[:, :], [CONST], [identb])
    P.dma("sync", identf[:, :], identf_d[:, :], [CONST], [identf])
    P.dma("sync", rotm[:, :], rot_d[:, :], [CONST], [rotm])
    P.dma("sync", trim[:, :], trim_d[:, :], [CONST], [trim])
    P.dma("sync", cm[:, :, :], cm_d[:, :, :], [CONST], [cm])
    P.memset("vector", negh[:, :], -0.5, [negh])
    P.memset("vector", ones_f[:, :], 1.0, [ones_f])

    def bc(d_buf, n):
        return d_buf.t.ap().partition_broadcast(n)

    def rsqrt_(out_ap, in_ap, scale, nh_ap, reads, writes):
        P.ts("vector", out_ap, in_ap, scale, EPS, ALU.mult, ALU.add, reads, writes)
        P.tt("gpsimd", out_ap, out_ap, nh_ap, ALU.pow, list(writes) + [negh], writes)

    areset(0)
    Win = aalloc("Win", [8, INC], BF16)
    markW = cur[0]
    stg = [aalloc("stg%d" % i, [3592], F32) for i in range(2)]
    stgb = [aalloc("stgb%d" % i, [3592], BF16) for i in range(2)]
    ci = [0]
    cast_engs = ["vector", "gpsimd"]

    def cast_to_dram(src_ap, dst_ap, ncols, unit):
        i = ci[0] % 2
        ci[0] += 1
        P.dma("sync", stg[i][:, 0:ncols], src_ap, [CONST], [stg[i]])
        P.cp(cast_engs[i], stgb[i][:, 0:ncols], stg[i][:, 0:ncols], [stg[i]], [stgb[i]])
        P.dma("sync", dst_ap, stgb[i][:, 0:ncols], [stgb[i]], [unit], nowaw=True)

    for k in range(8):
        for hf in range(2):
            cast_to_dram(wup_d[k * 128:(k + 1) * 128, hf * DFF:(hf + 1) * DFF],
                         wupb_d[k * 128:(k + 1) * 128, hf * DFF:(hf + 1) * DFF], DFF, u_wupb)
    for k in range(22):
        cast_to_dram(wdn_d[k * 128:(k + 1) * 128, :], wdnb_d[k * 128:(k + 1) * 128, :], D, u_wdnb)

    for k in range(8):
        i = ci[0] % 2
        ci[0] += 1
        P.dma("sync", stg[i][:, 0:INC], win_d[k * 128:(k + 1) * 128, :], [CONST], [stg[i]])
        P.cp(cast_engs[i], Win[:, k, :], stg[i][:, 0:INC], [stg[i]], [Win], nowaw=True)
    areset(markW)
    xbuf = [aalloc("xbuf%d" % i, [4, D], F32) for i in range(2)]
    rpbuf = [aalloc("rp%d" % i, [2, 512], F32) for i in range(2)]
    xn = aalloc("xn", [4, D], BF16)
    xnT = aalloc("xnT", [8, 512], BF16)
    nwA = aalloc("nwA", [D], F32)
    junk = aalloc("junk", [D], BF16)
    ssq = aalloc("ssq", [4], F32)
    rstd = aalloc("rstd", [4], F32)
    xb = [aalloc("xb%d" % i, [512], BF16) for i in range(2)]
    t1 = [aalloc("t1_%d" % i, [512], F32) for i in range(2)]
    t2 = [aalloc("t2_%d" % i, [512], F32) for i in range(2)]
    ro = [aalloc("ro%d" % i, [512], BF16) for i in range(3)]
    gh = aalloc("gh", [12, 515], BF16)
    gdiag = aalloc("gdiag", [12, 4, 128], BF16)
    gcw = aalloc("gcw", [12, 4], F32)
    gs = [aalloc("gs%d" % i, [512], BF16) for i in range(3)]
    knt = [aalloc("knt%d" % i, [128], BF16) for i in range(2)]
    knTt = [aalloc("knTt%d" % i, [512], BF16) for i in range(2)]
    kn_t = aalloc("kn_t", [4, 512], BF16)
    v_t = aalloc("v_t", [4, 512], BF16)
    va_t = aalloc("va_t", [4, 512], BF16)
    z_t = aalloc("z_t", [4, 512], BF16)
    ssqk = aalloc("ssqk", [4, 8], F32)
    rk1 = aalloc("rk1", [4, 8], F32)
    gsc_t = aalloc("gsc_t", [4, 16], F32)
    ba_t = aalloc("ba_t", [4, 8], F32)
    tmp4 = aalloc("tmp4", [4, 4], F32)
    dtbB = aalloc("dtbB", [4], F32)
    negA = aalloc("negA", [4], F32)
    markA = cur[0]

    P.dma("sync", nwA[:, :], bc(anw_d, 128), [CONST], [nwA])
    P.dma("sync", gcw[:, :, :], gcw_d[:, :, :], [CONST], [gcw])
    P.dma("sync", dtbB[:, :], bc(dtb_d, 128), [CONST], [dtbB])
    P.dma("sync", negA[:, :], bc(alog_d, 128), [CONST], [negA])
    P.act(negA[:, :], negA[:, :], AF.Exp, [negA], [negA])
    P.ts("vector", negA[:, :], negA[:, :], -1.0, None, ALU.mult, None, [negA], [negA])
    for c in range(12):
        for j in range(4):
            P.ts("gpsimd", gdiag[:, c, j, :], identb[:, :], gcw[:, c, j:j + 1], None, ALU.mult, None,
                 [identb, gcw], [gdiag], nowaw=True)

    tiles = [(s, tt) for s in range(NSEQ) for tt in range(NT)]

    def loadA(g):
        s, tt = tiles[g]
        t0 = tt * 512
        P.dma("sync", xbuf[g % 2][:, :, :], x_d[s, t0:t0 + 512, :].rearrange("(j p) d -> p j d", p=128),
              [CONST], [xbuf[g % 2]])
        P.dma("sync", rpbuf[g % 2][:, :, :], rope_d[:, :, t0:t0 + 512], [CONST], [rpbuf[g % 2]])

    evi = [0]

    def ev_eng():
        evi[0] += 1
        return "vector" if evi[0] % 2 else "scalar"

    loadA(0)
    for g, (s, tt) in enumerate(tiles):
        t0 = tt * 512
        ug = s * NT + tt
        if g + 1 < len(tiles):
            loadA(g + 1)
        xt = xbuf[g % 2]
        rp = rpbuf[g % 2]
        P.memset("vector", ssq[:, :], 0.0, [ssq])
        for j in range(4):
            P.act(junk[:, :], xt[:, j, :], AF.Square, [xt, ssq], [junk, ssq], accum=ssq[:, j:j + 1], nowaw=True)
        rsqrt_(rstd[:, :], ssq[:, :], 1.0 / D, negh[:, 0:4], [ssq], [rstd])
        for j in range(4):
            P.stt("vector", xn[:, j, :], xt[:, j, :], rstd[:, j:j + 1], nwA[:, :], ALU.mult, ALU.mult,
                  [xt, rstd, nwA], [xn], nowaw=True)
        for k in range(8):
            bk = k % 2
            for j in range(4):
                P.tr(pbf(bk)[:, j * 128:(j + 1) * 128], xn[:, j, k * 128:(k + 1) * 128], identb[:, :],
                     [xn, identb], [PB[bk]], nowaw=(j > 0))
            P.cp(ev_eng(), xnT[:, k, :], pbf(bk)[:, 0:512], [PB[bk]], [xnT], nowaw=True)

        def proj_fm(c0, bank):
            for k in range(8):
                P.mm(PB[bank][:, :], Win[:, k, c0:c0 + 128], xnT[:, k, :], k == 0, k == 7, [Win, xnT], [PB[bank]],
                     nowaw=(k > 0))

        for c in range(8):
            bank = 2 + (c % 2)
            proj_fm(c * 128, bank)
            i2 = c % 2
            P.cp("scalar", xb[i2][:, :], PB[bank][:, :], [PB[bank]], [xb[i2]])
            import os
            hk = os.environ.get("HACK", "")
            if hk == "a":
                P.tt("vector", t1[i2][:, :], PB[bank][:, :], nwA[:, 0:512], ALU.mult, [PB[bank], nwA], [t1[i2]])
            elif hk == "c":
                P.tt("vector", t1[i2][:, :], xn[:, 0, 0:512], nwA[:, 0:512], ALU.mult, [PB[bank], nwA, xn], [t1[i2]])
            elif hk == "d":
                P.tt("vector", t1[i2][:, :], PB[0][:, :], nwA[:, 0:512], ALU.mult, [PB[bank], PB[0], nwA], [t1[i2]])
            elif hk == "e":
                P.dma("gpsimd", qT_d[s, c % 4, :, t0:t0 + 512], xb[i2][:, :], [xb[i2]], [u_qT[ug]], nowaw=True)
            elif hk == "f":
                P.tt("vector", t1[i2][:, :], PB[bank][:, :], rp[:, 0, :], ALU.mult, [PB[bank], rp, xb[i2]], [t1[i2]])
            elif hk == "b":
                P.tt("vector", xnT[:, 0, :], PB[bank][:, :], rp[:, 0, :], ALU.mult, [PB[bank], rp], [xnT])
            else:
                P.tt("vector", t1[i2][:, :], PB[bank][:, :], rp[:, 0, :], ALU.mult, [PB[bank], rp], [t1[i2]])
            rb = 4 + (c % 2)
            P.mm(PB[rb][:, :], rotm[:, :], xb[i2][:, :], True, True, [rotm, xb[i2]], [PB[rb]])
            P.tt("vector", t2[i2][:, :], PB[rb][:, :], rp[:, 1, :], ALU.mult, [PB[rb], rp], [t2[i2]])
            r3 = ro[c % 3]
            P.tt("gpsimd", r3[:, :], t1[i2][:, :], t2[i2][:, :], ALU.add, [t1[i2], t2[i2]], [r3])
            if c < 4:
                P.dma("gpsimd", qT_d[s, c, :, t0:t0 + 512], r3[:, :], [r3], [u_qT[ug]], nowaw=True)
            else:
                P.dma("gpsimd", kT_d[s, c - 4, :, t0:t0 + 512], r3[:, :], [r3], [u_kT[ug]], nowaw=True)
        if tt == 0:
            P.memset("vector", gh[:, :, 0:3], 0.0, [gh])
        P.memset("vector", ssqk[:, :, :], 0.0, [ssqk])
        for c in range(12):
            bank = 2 + (c % 2)
            proj_fm(1536 + c * 128, bank)
            P.cp("scalar", gh[:, c, 3:515], PB[bank][:, :], [PB[bank]], [gh], nowaw=True)
        for c in range(12):
            bank = 4 + (c % 2)
            for j in range(4):
                P.mm(PB[bank][:, :], gdiag[:, c, j, :], gh[:, c, j:j + 512], j == 0, j == 3, [gdiag, gh], [PB[bank]],
                     nowaw=(j > 0))
            g3 = gs[c % 3]
            P.act(g3[:, :], PB[bank][:, :], AF.Silu, [PB[bank]], [g3])
            h = c % 4
            if c < 4:
                P.dma("gpsimd", gqT_d[s, h, :, t0:t0 + 512], g3[:, :], [g3], [u_gqT[ug]], nowaw=True)
                tb = 6 + (c % 2)
                for i in range(4):
                    P.tr(pbf(tb)[:, i * 128:(i + 1) * 128], g3[:, i * 128:(i + 1) * 128], identb[:, :],
                         [g3, identb], [PB[tb]], nowaw=(i > 0))
                for i in range(4):
                    P.act(junk[:, 0:128], pbf(tb)[:, i * 128:(i + 1) * 128], AF.Square, [PB[tb], ssqk], [junk, ssqk],
                          accum=ssqk[:, i, 4 + h:5 + h], nowaw=True)
            elif c < 8:
                tb = 6 + (c % 2)
                for i in range(4):
                    P.tr(pbf(tb)[:, i * 128:(i + 1) * 128], g3[:, i * 128:(i + 1) * 128], identb[:, :],
                         [g3, identb], [PB[tb]], nowaw=(i > 0))
                for i in range(4):
                    P.act(junk[:, 0:128], pbf(tb)[:, i * 128:(i + 1) * 128], AF.Square, [PB[tb], ssqk], [junk, ssqk],
                          accum=ssqk[:, i, h:h + 1], nowaw=True)
                rsqrt_(rk1[:, :, h:h + 1], ssqk[:, :, h:h + 1], 1.0, negh[:, 0:4].unsqueeze(2), [ssqk], [rk1])
                for i in range(4):
                    P.ts("vector", kn_t[:, i, h * 128:(h + 1) * 128], pbf(tb)[:, i * 128:(i + 1) * 128],
                         rk1[:, i, h:h + 1], None, ALU.mult, None, [PB[tb], rk1], [kn_t], nowaw=True)
                kT_ = knTt[c % 2]
                tb2 = 2 + (c % 2)
                for i in range(4):
                    P.tr(pbf(tb2)[:, i * 128:(i + 1) * 128], kn_t[:, i, h * 128:(h + 1) * 128], identb[:, :],
                         [kn_t, identb], [PB[tb2]], nowaw=(i > 0))
                P.cp("vector", kT_[:, :], pbf(tb2)[:, 0:512], [PB[tb2]], [kT_])
                P.dma("gpsimd", gkT_d[s, h, :, t0:t0 + 512], kT_[:, :], [kT_], [u_gkT[ug]], nowaw=True)
            else:
                tb = 6 + (c % 2)
                for i in range(4):
                    P.tr(pbf(tb)[:, i * 128:(i + 1) * 128], g3[:, i * 128:(i + 1) * 128], identb[:, :],
                         [g3, identb], [PB[tb]], nowaw=(i > 0))
                P.cp("vector", v_t[:, :, h * 128:(h + 1) * 128],
                     pbf(tb)[:, 0:512].rearrange("p (i d) -> p i d", i=4), [PB[tb]], [v_t], nowaw=True)
        P.cp("gpsimd", gh[:, :, 0:3], gh[:, :, 512:515], [gh], [gh])
        P.dma("gpsimd", gkn_d[s, t0:t0 + 512, :].rearrange("(j p) f -> p j f", p=128), kn_t[:, :, :], [kn_t], [u_gkn[ug]])
        P.dma("gpsimd", gv_d[s, t0:t0 + 512, :].rearrange("(j p) f -> p j f", p=128), v_t[:, :, :], [v_t], [u_gv[ug]])
        for i in range(4):
            bank = 2 + (i % 2)
            for k in range(8):
                P.mm(PB[bank][:, :], xnT[:, k, i * 128:(i + 1) * 128], Win[:, k, 1024:1536], k == 0, k == 7,
                     [Win, xnT], [PB[bank]], nowaw=(k > 0))
            P.cp(ev_eng(), va_t[:, i, :], PB[bank][:, :], [PB[bank]], [va_t], nowaw=True)
            bank = 4 + (i % 2)
            for k in range(8):
                P.mm(PB[bank][:, :], xnT[:, k, i * 128:(i + 1) * 128], Win[:, k, 3072:3584], k == 0, k == 7,
                     [Win, xnT], [PB[bank]], nowaw=(k > 0))
            P.act(z_t[:, i, :], PB[bank][:, :], AF.Silu, [PB[bank]], [z_t], nowaw=True)
        for i in range(4):
            for k in range(8):
                P.mm(PB[6][:, i * 8:(i + 1) * 8], xnT[:, k, i * 128:(i + 1) * 128], Win[:, k, 3584:3592], k == 0, k == 7,
                     [Win, xnT], [PB[6]], nowaw=not (i == 0 and k == 0))
        P.cp("vector", ba_t[:, :, :], PB[6][:, 0:32].rearrange("p (i e) -> p i e", i=4), [PB[6]], [ba_t])
        P.dma("gpsimd", vda_d[s, t0:t0 + 512, :].rearrange("(j p) f -> p j f", p=128), va_t[:, :, :], [va_t], [u_vda[ug]])
        P.dma("gpsimd", gz_d[s, t0:t0 + 512, :].rearrange("(j p) f -> p j f", p=128), z_t[:, :, :], [z_t], [u_gz[ug]])
        rsqrt_(gsc_t[:, :, 4:8], ssqk[:, :, 4:8], 1.0, negh[:, 0:16].rearrange("p (a b) -> p a b", a=4), [ssqk], [gsc_t])
        P.ts("vector", gsc_t[:, :, 4:8], gsc_t[:, :, 4:8], 128.0 ** -0.5, None, ALU.mult, None, [gsc_t], [gsc_t])
        P.cp("vector", gsc_t[:, :, 0:4], rk1[:, :, 0:4], [rk1], [gsc_t])
        P.act(gsc_t[:, :, 8:12], ba_t[:, :, 0:4], AF.Sigmoid, [ba_t], [gsc_t])
        P.tt("vector", tmp4[:, :, :], ba_t[:, :, 4:8], dtbB[:, :].unsqueeze(1).to_broadcast([128, 4, 4]), ALU.add,
             [ba_t, dtbB], [tmp4])
        P.act(tmp4[:, :, :], tmp4[:, :, :], AF.Exp, [tmp4], [tmp4])
        P.act(tmp4[:, :, :], tmp4[:, :, :], AF.Ln, [tmp4], [tmp4], bias=1.0)
        P.tt("vector", gsc_t[:, :, 12:16], tmp4[:, :, :], negA[:, :].unsqueeze(1).to_broadcast([128, 4, 4]), ALU.mult,
             [tmp4, negA], [gsc_t])
        P.dma("gpsimd", gsc_d[s, t0:t0 + 512, :].rearrange("(j p) f -> p j f", p=128), gsc_t[:, :, :], [gsc_t], [u_gsc[ug]])


    if stop_after == "A":
        P.emit()
        return nc

    def phase_c():
        areset(0)
        knT_c = aalloc("c_knT", [4, 512], BF16)
        qT_c = aalloc("c_qT", [4, 512], BF16)
        kn_c = aalloc("c_kn", [8, 512], BF16)
        v_c = aalloc("c_v", [8, 512], BF16)
        z_c = aalloc("c_z", [8, 512], BF16)
        sc_c = aalloc("c_sc", [8, 16], F32)
        g_c = aalloc("c_g", [32], F32)
        gcum = aalloc("c_gcum", [32], F32)
        egc = aalloc("c_egc", [32], F32)
        egl = aalloc("c_egl", [32], F32)
        kdc = aalloc("c_kdc", [32], F32)
        nbeta = aalloc("c_nbeta", [32], F32)
        beta_c = aalloc("c_beta", [32], F32)
        rq_c = aalloc("c_rq", [32], F32)
        rqe = aalloc("c_rqe", [32], F32)
        gU = aalloc("c_gU", [8, 64], F32)
        Gt = aalloc("c_Gt", [8, 64], F32)
        tSU = aalloc("c_tSU", [8, 64], F32)
        tU = aalloc("c_tU", [8, 64], F32)
        Pm = [aalloc("c_P%d" % i, [8, 64], F32) for i in range(2)]
        PTm = [aalloc("c_PT%d" % i, [8, 64], F32) for i in range(2)]
        Am = aalloc("c_A", [8, 64], F32)
        Wb = aalloc("c_Wb", [8, 64], BF16)
        kg_b = aalloc("c_kg", [8, 128], BF16)
        wT_all = aalloc("c_wT", [32, 64], BF16)
        ub_all = aalloc("c_ub", [32, 128], F32)
        Aq_all = aalloc("c_Aq", [32, 64], BF16)
        kdec_all = aalloc("c_kdec", [32, 128], BF16)
        S = [aalloc("c_S%d" % i, [128], F32) for i in range(4)]
        Sb = [aalloc("c_Sb%d" % i, [128], BF16) for i in range(4)]
        vnew = [aalloc("c_vn%d" % i, [128], BF16) for i in range(4)]
        t1c = [aalloc("c_t1%d" % i, [128], F32) for i in range(4)]
        obuf = aalloc("c_o", [8, 512], F32)
        osq = aalloc("c_osq", [8, 512], F32)
        ssqg = aalloc("c_ssqg", [32], F32)
        rstdg = aalloc("c_rstdg", [32], F32)
        mixg = aalloc("c_mixg", [8, 512], BF16)
        gnwB = aalloc("c_gnwB", [128], F32)
        P.dma("sync", gnwB[:, :], bc(gnw_d, 128), [CONST], [gnwB])
        H = slice(0, 64)

        def b3(ap2, n):
            return ap2.unsqueeze(2).to_broadcast([64, 8, n])

        for g, (s, tt) in enumerate(tiles):
            t0 = tt * 512
            ug = s * NT + tt
            P.dma("sync", knT_c[:, :, :], gkT_d[s, :, :, t0:t0 + 512].rearrange("h p t -> p h t"), [u_gkT[ug]], [knT_c])
            P.dma("sync", qT_c[:, :, :], gqT_d[s, :, :, t0:t0 + 512].rearrange("h p t -> p h t"), [u_gqT[ug]], [qT_c])
            P.dma("sync", kn_c[H, :, :], gkn_d[s, t0:t0 + 512, :].rearrange("(n c) f -> c n f", c=64), [u_gkn[ug]], [kn_c])
            P.dma("sync", v_c[H, :, :], gv_d[s, t0:t0 + 512, :].rearrange("(n c) f -> c n f", c=64), [u_gv[ug]], [v_c])
            P.dma("sync", z_c[H, :, :], gz_d[s, t0:t0 + 512, :].rearrange("(n c) f -> c n f", c=64), [u_gz[ug]], [z_c])
            P.dma("sync", sc_c[H, :, :], gsc_d[s, t0:t0 + 512, :].rearrange("(n c) f -> c n f", c=64), [u_gsc[ug]], [sc_c])
            if tt == 0:
                for h in range(4):
                    P.memset("vector", S[h][:, :], 0.0, [S[h]])
                    P.memset("vector", Sb[h][:, :], 0.0, [Sb[h]])
            g3 = g_c[H, :].rearrange("p (n h) -> p n h", n=8)
            P.cp("vector", g3, sc_c[H, :, 12:16], [sc_c], [g_c])
            P.cp("vector", beta_c[H, :].rearrange("p (n h) -> p n h", n=8), sc_c[H, :, 8:12], [sc_c], [beta_c])
            P.cp("vector", rq_c[H, :].rearrange("p (n h) -> p n h", n=8), sc_c[H, :, 4:8], [sc_c], [rq_c])
            P.ts("vector", nbeta[H, :], beta_c[H, :], -1.0, None, ALU.mult, None, [beta_c], [nbeta])
            P.mm(PB[0][H, 0:32], cm[:, 0, :], g_c[H, :], True, True, [cm, g_c], [PB[0]])
            P.mm(PB[1][:, 0:32], ones_f[:, :], g_c[H, :], True, True, [ones_f, g_c], [PB[1]])
            P.cp("vector", gcum[H, :], PB[0][H, 0:32], [PB[0]], [gcum])
            P.act(egc[H, :], PB[0][H, 0:32], AF.Exp, [PB[0]], [egc])
            P.act(egl[:, :], PB[1][:, 0:32], AF.Exp, [PB[1]], [egl])
            P.tt("vector", kdc[H, :], PB[1][H, 0:32], gcum[H, :], ALU.subtract, [PB[1], gcum], [kdc])
            P.act(kdc[H, :], kdc[H, :], AF.Exp, [kdc], [kdc])
            P.tt("vector", rqe[H, :], rq_c[H, :], egc[H, :], ALU.mult, [rq_c, egc], [rqe])
            for bb in range(4):
                ps = slice(bb * 8, bb * 8 + 8)
                pairs = [(2 * bb + q // 4, q % 4) for q in range(8)]
                P.tt("vector", gU[H, :, :], cm[:, 0, :].unsqueeze(1).to_broadcast([64, 8, 64]), b3(g_c[H, ps], 64), ALU.mult,
                     [cm, g_c], [gU])
                for q, (n, h) in enumerate(pairs):
                    P.mm(PB[0][H, q * 64:(q + 1) * 64], cm[:, 4, :], gU[H, q, :], True, True, [cm, gU], [PB[0]], nowaw=(q > 0))
                P.act(Gt[H, :, :], PB[0][H, :].rearrange("p (q i) -> p q i", q=8), AF.Exp, [PB[0]], [Gt])
                for q, (n, h) in enumerate(pairs):
                    ks = knT_c[:, h, n * 64:(n + 1) * 64]
                    P.mm(PB[1][H, q * 64:(q + 1) * 64], ks, ks, True, True, [knT_c], [PB[1]], nowaw=(q > 0))
                for q, (n, h) in enumerate(pairs):
                    ks = knT_c[:, h, n * 64:(n + 1) * 64]
                    P.mm(PB[2][H, q * 64:(q + 1) * 64], ks, qT_c[:, h, n * 64:(n + 1) * 64], True, True, [knT_c, qT_c], [PB[2]],
                         nowaw=(q > 0))
                m8 = lambda i: cm[:, i, :].unsqueeze(1).to_broadcast([64, 8, 64])
                P.tt("gpsimd", tSU[H, :, :], Gt[H, :, :], m8(1), ALU.mult, [Gt, cm], [tSU])
                P.stt("vector", tSU[H, :, :], tSU[H, :, :], -1.0, b3(beta_c[H, ps], 64), ALU.mult, ALU.mult, [tSU, beta_c], [tSU])
                P.tt("gpsimd", tU[H, :, :], Gt[H, :, :], m8(2), ALU.mult, [Gt, cm], [tU])
                P0 = Pm[0]
                P.tt("vector", P0[H, :, :], PB[1][H, :].rearrange("p (q i) -> p q i", q=8), tSU[H, :, :], ALU.mult,
                     [PB[1], tSU], [P0])
                P.tt("vector", Aq_all[H, ps, :], PB[2][H, :].rearrange("p (q i) -> p q i", q=8), tU[H, :, :], ALU.mult,
                     [PB[2], tU], [Aq_all], nowaw=(bb > 0))
                for q in range(8):
                    P.tr(PB[3][H, q * 64:(q + 1) * 64], P0[H, q, :], identf[H, H], [P0, identf], [PB[3]], nowaw=(q > 0))
                P.cp("scalar", PTm[0][H, :, :], PB[3][H, :].rearrange("p (q i) -> p q i", q=8), [PB[3]], [PTm[0]])
                P.tt("gpsimd", Am[H, :, :], P0[H, :, :], m8(3), ALU.add, [P0, cm], [Am])
                for m in range(5):
                    Pc, PTc = Pm[m % 2], PTm[m % 2]
                    Pn, PTn = Pm[(m + 1) % 2], PTm[(m + 1) % 2]
                    if m < 4:
                        for q in range(8):
                            P.mm(PB[1][H, q * 64:(q + 1) * 64], PTc[H, q, :], Pc[H, q, :], True, True, [PTc, Pc], [PB[1]],
                                 nowaw=(q > 0))
                    for q in range(8):
                        P.mm(PB[2][H, q * 64:(q + 1) * 64], Pc[H, q, :], PTc[H, q, :], True, True, [PTc, Pc], [PB[2]],
                             nowaw=(q > 0))
                    if m < 4:
                        P.cp("vector", Pn[H, :, :], PB[1][H, :].rearrange("p (q i) -> p q i", q=8), [PB[1]], [Pn])
                    P.cp("scalar", PTn[H, :, :], PB[2][H, :].rearrange("p (q i) -> p q i", q=8), [PB[2]], [PTn])
                    for q in range(8):
                        P.mm(PB[3][H, q * 64:(q + 1) * 64], PTn[H, q, :], Am[H, q, :], True, True, [PTn, Am], [PB[3]],
                             nowaw=(q > 0))
                    P.tt("vector", Am[H, :, :], PB[3][H, :].rearrange("p (q i) -> p q i", q=8), Am[H, :, :], ALU.add,
                         [PB[3], Am], [Am])
                P.cp("gpsimd", Wb[H, :, :], Am[H, :, :], [Am], [Wb])
                knv = kn_c[H, 2 * bb:2 * bb + 2, :].rearrange("p n (h d) -> p (n h) d", h=4)
                vv = v_c[H, 2 * bb:2 * bb + 2, :].rearrange("p n (h d) -> p (n h) d", h=4)
                P.tt("gpsimd", kg_b[H, :, :], knv, b3(egc[H, ps], 128), ALU.mult, [kn_c, egc], [kg_b])
                P.tt("gpsimd", kdec_all[H, ps, :], knv, b3(kdc[H, ps], 128), ALU.mult, [kn_c, kdc], [kdec_all], nowaw=(bb > 0))
                for q in range(8):
                    P.mm(PB[0][:, q * 64:(q + 1) * 64], kg_b[H, q, :], Wb[H, q, :], True, True, [kg_b, Wb], [PB[0]], nowaw=(q > 0))
                P.cp("scalar", wT_all[:, ps, :], PB[0][:, :].rearrange("p (q i) -> p q i", q=8), [PB[0]], [wT_all], nowaw=(bb > 0))
                for hf in range(2):
                    bank = 1 + hf
                    for q4 in range(4):
                        q = hf * 4 + q4
                        P.mm(PB[bank][H, q4 * 128:(q4 + 1) * 128], Wb[H, q, :], vv[:, q, :], True, True, [Wb, v_c], [PB[bank]],
                             nowaw=(q4 > 0))
                    pq = slice(bb * 8 + hf * 4, bb * 8 + hf * 4 + 4)
                    P.tt("vector", ub_all[H, pq, :], PB[bank][H, :].rearrange("p (q d) -> p q d", q=4),
                         beta_c[H, pq].unsqueeze(2).to_broadcast([64, 4, 128]), ALU.mult, [PB[bank], beta_c], [ub_all],
                         nowaw=not (bb == 0 and hf == 0))
            for n in range(8):
                for h in range(4):
                    p = n * 4 + h
                    bank = 4 + h
                    B_ = PB[bank]
                    P.mm(B_[H, 0:128], wT_all[:, p, :], Sb[h][:, :], True, True, [wT_all, Sb[h]], [B_])
                    P.mm(B_[H, 128:256], qT_c[:, h, n * 64:(n + 1) * 64], Sb[h][:, :], True, True, [qT_c, Sb[h]], [B_], nowaw=True)
                    P.stt("vector", vnew[h][H, :], B_[H, 0:128], nbeta[H, p:p + 1], ub_all[H, p, :], ALU.mult, ALU.add,
                          [B_, nbeta, ub_all], [vnew[h]])
                    P.ts("vector", t1c[h][H, :], B_[H, 128:256], rqe[H, p:p + 1], None, ALU.mult, None, [B_, rqe], [t1c[h]])
                    P.mm(B_[H, 256:384], Aq_all[H, p, :], vnew[h][H, :], True, True, [Aq_all, vnew[h]], [B_])
                    P.mm(B_[:, 384:512], kdec_all[H, p, :], vnew[h][H, :], True, True, [kdec_all, vnew[h]], [B_], nowaw=True)
                    P.stt("vector", obuf[H, n, h * 128:(h + 1) * 128], B_[H, 256:384], rq_c[H, p:p + 1], t1c[h][H, :],
                          ALU.mult, ALU.add, [B_, rq_c, t1c[h]], [obuf], nowaw=not (n == 0 and h == 0))
                    P.stt("vector", S[h][:, :], S[h][:, :], egl[:, p:p + 1], B_[:, 384:512], ALU.mult, ALU.add,
                          [S[h], egl, B_], [S[h]])
                    P.cp("gpsimd", Sb[h][:, :], S[h][:, :], [S[h]], [Sb[h]])
            o3 = obuf[H, :, :].rearrange("p n (h d) -> p (n h) d", h=4)
            P.tt("gpsimd", osq[H, :, :], obuf[H, :, :], obuf[H, :, :], ALU.mult, [obuf], [osq])
            P.op("vector", lambda e: e.reduce_sum(out=ssqg[H, :], in_=osq[H, :, :].rearrange("p n (h d) -> p (n h) d", h=4), axis=AX.X),
                 [osq], [ssqg])
            rsqrt_(rstdg[H, :], ssqg[H, :], 1.0 / 128, negh[H, 0:32], [ssqg], [rstdg])
            P.tt("vector", o3, o3, rstdg[H, :].unsqueeze(2).to_broadcast([64, 32, 128]), ALU.mult, [obuf, rstdg], [obuf])
            P.tt("gpsimd", o3, o3, gnwB[H, :].unsqueeze(1).to_broadcast([64, 32, 128]), ALU.mult, [obuf, gnwB], [obuf])
            P.tt("vector", mixg[H, :, :], obuf[H, :, :], z_c[H, :, :], ALU.mult, [obuf, z_c], [mixg])
            P.dma("gpsimd", mix_d[s, t0:t0 + 512, 512:1024].rearrange("(n c) f -> c n f", c=64), mixg[H, :, :], [mixg],
                  [u_mixg[ug]])

    areset(0)
    qTb = [aalloc("qTb%d" % i, [T], BF16) for i in range(2)]
    kTb = [aalloc("kTb%d" % i, [T], BF16) for i in range(2)]
    vAb = [aalloc("vAb%d" % i, [NKT, 130], BF16) for i in range(2)]
    pT = [aalloc("pT%d" % i, [512], BF16) for i in range(3)]
    o1 = aalloc("o1", [4, 128], F32)
    o2 = aalloc("o2", [4, 128], F32)
    rinv = aalloc("rinv", [4], F32)
    rinv2 = aalloc("rinv2", [4], F32)
    ssqo = aalloc("ssqo", [4], F32)
    rstdo = aalloc("rstdo", [4], F32)
    mixt = [aalloc("mixt%d" % i, [4, 128], BF16) for i in range(2)]
    lamb = aalloc("lamb", [4, 64], F32)
    lamp = aalloc("lamp", [2, 64], F32)
    lams = aalloc("lams", [2], F32)
    neglam = aalloc("neglam", [1], F32)
    sublnB = aalloc("sublnB", [128], F32)
    junkB = aalloc("junkB", [128], F32)

    for i, dd in enumerate((lq1_d, lk1_d, lq2_d, lk2_d)):
        P.dma("sync", lamb[:, i, :], bc(dd, 128), [CONST], [lamb], nowaw=True)
    P.dma("sync", sublnB[:, :], bc(subln_d, 128), [CONST], [sublnB])
    P.ts("vector", sublnB[:, :], sublnB[:, :], 1.0 - LAMBDA_INIT, None, ALU.mult, None, [sublnB], [sublnB])
    P.tt("vector", lamp[:, 0, :], lamb[:, 0, :], lamb[:, 1, :], ALU.mult, [lamb], [lamp], nowaw=True)
    P.tt("vector", lamp[:, 1, :], lamb[:, 2, :], lamb[:, 3, :], ALU.mult, [lamb], [lamp], nowaw=True)
    P.op("vector", lambda e: e.reduce_sum(out=lams[:, :], in_=lamp[:, :, :], axis=AX.X), [lamp], [lams])
    P.act(lams[:, :], lams[:, :], AF.Exp, [lams], [lams])
    P.tt("vector", neglam[:, :], lams[:, 1:2], lams[:, 0:1], ALU.subtract, [lams], [neglam])
    P.ts("vector", neglam[:, :], neglam[:, :], -LAMBDA_INIT, None, ALU.add, None, [neglam], [neglam])
    for i in range(2):
        P.memset("vector", vAb[i][:, :, 128:130], 1.0, [vAb[i]])

    heads = [(s, h) for s in range(NSEQ) for h in range(4)]

    def loadB(n):
        s, h = heads[n]
        i = n % 2
        rd = [u_qT[s * NT + tt] for tt in range(NT)]
        P.dma("sync", qTb[i][:, :], qT_d[s, h, :, :], rd, [qTb[i]])
        rd = [u_kT[s * NT + tt] for tt in range(NT)]
        P.dma("sync", kTb[i][:, :], kT_d[s, h, :, :], rd, [kTb[i]])
        rd = [u_vda[s * NT + tt] for tt in range(NT)]
        nq = max(1, NKT // 8)
        for a in range(0, NKT, nq):
            P.dma("sync", vAb[i][:, a:a + nq, 0:128],
                  vda_d[s, a * 128:(a + nq) * 128, h * 128:(h + 1) * 128].rearrange("(n p) d -> p n d", p=128),
                  rd, [vAb[i]], nowaw=(a > 0))

    loadB(0)
    kidx = 0
    ucnt = 0
    for n, (s, h) in enumerate(heads):
        if n + 1 < len(heads):
            loadB(n + 1)
        qb, kb, vb = qTb[n % 2], kTb[n % 2], vAb[n % 2]
        for qt in range(NT):
            for c in range(2):
                pob = (4, 5) if ucnt % 2 == 0 else (6, 7)
                ucnt += 1
                fresh = {pob[0]: True, pob[1]: True}
                nk = 4 * qt + 4
                cs = slice(c * 64, (c + 1) * 64)
                for kt in range(nk):
                    j = kt - 4 * qt
                    sbk = PB[kidx % 3]
                    pt = pT[kidx % 3]
                    kidx += 1
                    ksl = kb[cs, kt * 128:(kt + 1) * 128]
                    q0 = qt * 512
                    if j < 0:
                        lo = 0
                        P.mm(sbk[:, 0:512], ksl, qb[cs, q0:q0 + 512], True, True, [kb, qb], [sbk])
                    else:
                        lo = j * 128
                        P.mm(sbk[:, lo:lo + 128], ksl, qb[cs, q0 + lo:q0 + lo + 128], True, False, [kb, qb], [sbk])
                        P.mm(sbk[:, lo:lo + 128], identb[:, :], trim[:, :], False, True, [identb, trim], [sbk], nowaw=True)
                        if lo + 128 < 512:
                            P.mm(sbk[:, lo + 128:512], ksl, qb[cs, q0 + lo + 128:q0 + 512], True, True, [kb, qb], [sbk],
                                 nowaw=True)
                    P.act(pt[:, lo:512], sbk[:, lo:512], AF.Exp, [sbk], [pt], scale=0.125)
                    for i in range(max(j, 0), 4):
                        bank = pob[i // 2]
                        off = (i % 2) * 130
                        P.mm(PB[bank][:, off:off + 129], pt[:, i * 128:(i + 1) * 128], vb[:, kt, 0:129],
                             fresh[bank], (kt == 4 * qt + i and i % 2 == 1), [pt, vb], [PB[bank]], nowaw=not fresh[bank])
                        fresh[bank] = False
                for i in range(4):
                    bank = pob[i // 2]
                    off = (i % 2) * 130
                    P.op("vector", (lambda bank=bank, off=off, i=i: (lambda e: e.reciprocal(out=rinv[:, i:i + 1], in_=PB[bank][:, off + 128:off + 129])))(),
                         [PB[bank]], [rinv], nowaw=(i > 0))
                    if c == 0:
                        P.ts("vector", o1[:, i, :], PB[bank][:, off:off + 128], rinv[:, i:i + 1], None, ALU.mult, None,
                             [PB[bank], rinv], [o1], nowaw=(i > 0))
                    else:
                        P.ts("vector", rinv2[:, i:i + 1], rinv[:, i:i + 1], neglam[:, 0:1], None, ALU.mult, None,
                             [rinv, neglam], [rinv2], nowaw=(i > 0))
                        P.stt("vector", o2[:, i, :], PB[bank][:, off:off + 128], rinv2[:, i:i + 1], o1[:, i, :],
                              ALU.mult, ALU.add, [PB[bank], rinv2, o1], [o2], nowaw=(i > 0))
            mt = mixt[(n * NT + qt) % 2]
            P.memset("vector", ssqo[:, :], 0.0, [ssqo])
            for i in range(4):
                P.act(junkB[:, :], o2[:, i, :], AF.Square, [o2, ssqo], [junkB, ssqo], accum=ssqo[:, i:i + 1], nowaw=True)
            rsqrt_(rstdo[:, :], ssqo[:, :], 1.0 / 128, negh[:, 0:4], [ssqo], [rstdo])
            for i in range(4):
                P.stt("vector", mt[:, i, :], o2[:, i, :], rstdo[:, i:i + 1], sublnB[:, :], ALU.mult, ALU.mult,
                      [o2, rstdo, sublnB], [mt], nowaw=(i > 0))
            P.dma("gpsimd", mix_d[s, qt * 512:(qt + 1) * 512, h * 128:(h + 1) * 128].rearrange("(i p) d -> p i d", p=128),
                  mt[:, :, :], [mt], [u_mixa[s * NT + qt]], nowaw=True)

    if stop_after == "B":
        P.emit()
        return nc
    if do_c:
        phase_c()
    if stop_after == "C":
        P.emit()
        return nc

    areset(0)
    Wout = aalloc("Wout", [8, D], BF16)
    wst = [aalloc("wst%d" % i, [D], F32) for i in range(2)]
    xh = [aalloc("xh%d" % i, [4, D], F32) for i in range(2)]
    mixb = aalloc("mixb", [4, D], BF16)
    mixT = aalloc("mixTD", [8, 512], BF16)
    wup = [aalloc("wup%d" % i, [2, 8, 256], BF16) for i in range(3)]
    wdn = [aalloc("wdn%d" % i, [11, 512], BF16) for i in range(2)]
    aT = aalloc("aT", [22, 512], BF16)
    hp = [aalloc("hp%d" % i, [4, 514], BF16) for i in range(2)]
    halo = aalloc("halo", [11, 4, 2], BF16)
    fdiag = [aalloc("fdiag%d" % i, [4, 3, 128], BF16) for i in range(2)]
    fcw = aalloc("fcw", [44, 3], F32)
    gt = [aalloc("gt%d" % i, [512], BF16) for i in range(4)]
    nwF = aalloc("nwF", [D], F32)
    nwO = aalloc("nwO", [D], F32)
    junkD = aalloc("junkD", [D], BF16)
    ssq2 = aalloc("ssq2", [4], F32)
    rstd2 = aalloc("rstd2", [4], F32)

    for k in range(8):
        P.dma("sync", wst[k % 2][:, :], wout_d[k * 128:(k + 1) * 128, :], [CONST], [wst[k % 2]])
        P.cp(cast_engs[k % 2], Wout[:, k, :], wst[k % 2][:, :], [wst[k % 2]], [Wout], nowaw=True)
    P.dma("sync", fcw[:, :, :], fcw_d[:, :, :], [CONST], [fcw])
    P.dma("sync", nwF[:, :], bc(fnw_d, 128), [CONST], [nwF])
    P.dma("sync", nwO[:, :], bc(finw_d, 128), [CONST], [nwO])

    def loadD(g):
        s, tt = tiles[g]
        t0 = tt * 512
        P.dma("sync", xh[g % 2][:, :, :], x_d[s, t0:t0 + 512, :].rearrange("(j p) d -> p j d", p=128), [CONST], [xh[g % 2]])

    def load_wup(grp):
        sl = wup[grp % 3]
        P.dma("sync", sl[:, 0, :, :], wupb_d[:, grp * 256:(grp + 1) * 256].rearrange("(k p) c -> p k c", p=128),
              [u_wupb], [sl])
        P.dma("sync", sl[:, 1, :, :], wupb_d[:, DFF + grp * 256:DFF + (grp + 1) * 256].rearrange("(k p) c -> p k c", p=128),
              [u_wupb], [sl], nowaw=True)

    def load_wdn(idx):
        half, piece = idx // 2, idx % 2
        sl = wdn[idx % 2]
        P.dma("sync", sl[:, :, :],
              wdnb_d[piece * 11 * 128:(piece + 1) * 11 * 128, half * 512:(half + 1) * 512].rearrange("(f p) c -> p f c", p=128),
              [u_wdnb], [sl])

    def rms_to(src, dst, nw):
        P.memset("vector", ssq2[:, :], 0.0, [ssq2])
        for j in range(4):
            P.act(junkD[:, :], src[:, j, :], AF.Square, [src, ssq2], [junkD, ssq2], accum=ssq2[:, j:j + 1], nowaw=True)
        rsqrt_(rstd2[:, :], ssq2[:, :], 1.0 / D, negh[:, 0:4], [ssq2], [rstd2])
        for j in range(4):
            P.stt("vector", dst[:, j, :], src[:, j, :], rstd2[:, j:j + 1], nw[:, :], ALU.mult, ALU.mult,
                  [src, rstd2, nw], [dst], nowaw=(j > 0))

    def transpose_to(src, dst):
        for k in range(8):
            bk = k % 2
            for j in range(4):
                P.tr(pbf(bk)[:, j * 128:(j + 1) * 128], src[:, j, k * 128:(k + 1) * 128], identb[:, :],
                     [src, identb], [PB[bk]], nowaw=(j > 0))
            P.cp(ev_eng(), dst[:, k, :], pbf(bk)[:, 0:512], [PB[bk]], [dst], nowaw=(k > 0))

    loadD(0)
    for g, (s, tt) in enumerate(tiles):
        t0 = tt * 512
        ug = s * NT + tt
        if g + 1 < len(tiles):
            loadD(g + 1)
        xt = xh[g % 2]
        rdm = [u_mixa[ug]] + ([u_mixg[ug]] if do_c else [])
        P.dma("sync", mixb[:, :, :], mix_d[s, t0:t0 + 512, :].rearrange("(j p) d -> p j d", p=128), rdm, [mixb])
        load_wup(0)
        load_wup(1)
        transpose_to(mixb, mixT)
        bi = 0
        for j in range(4):
            for hf in range(2):
                bank = 2 + (bi % 2)
                bi += 1
                for k in range(8):
                    P.mm(PB[bank][:, :], mixT[:, k, j * 128:(j + 1) * 128], Wout[:, k, hf * 512:(hf + 1) * 512],
                         k == 0, k == 7, [mixT, Wout], [PB[bank]], nowaw=(k > 0))
                P.tt("vector", xt[:, j, hf * 512:(hf + 1) * 512], PB[bank][:, :], xt[:, j, hf * 512:(hf + 1) * 512], ALU.add,
                     [PB[bank], xt], [xt])
        rms_to(xt, mixb, nwF)
        transpose_to(mixb, mixT)
        if tt == 0:
            P.memset("vector", halo[:, :, :, :], 0.0, [halo])
        for grp in range(11):
            if grp + 2 < 11:
                load_wup(grp + 2)
            if grp == 9:
                load_wdn(0)
            if grp == 10:
                load_wdn(1)
            sl = wup[grp % 3]
            hpb = hp[grp % 2]
            fd = fdiag[grp % 2]
            chunks = [2 * grp, 2 * grp + 1, 22 + 2 * grp, 23 + 2 * grp]
            for q in range(4):
                for j in range(3):
                    P.ts("gpsimd", fd[:, q, j, :], identb[:, :], fcw[:, chunks[q], j:j + 1], None, ALU.mult, None,
                         [identb, fcw], [fd], nowaw=not (q == 0 and j == 0))
            P.cp("gpsimd", hpb[:, :, 0:2], halo[:, grp, :, :], [halo], [hpb])
            for q in range(4):
                bank = 2 + (q % 2)
                for k in range(8):
                    P.mm(PB[bank][:, :], sl[:, q // 2, k, (q % 2) * 128:(q % 2) * 128 + 128], mixT[:, k, :], k == 0, k == 7,
                         [sl, mixT], [PB[bank]], nowaw=(k > 0))
                P.cp(ev_eng(), hpb[:, q, 2:514], PB[bank][:, :], [PB[bank]], [hpb], nowaw=True)
            for q in range(4):
                bank = 4 + (q % 2)
                for j in range(3):
                    P.mm(PB[bank][:, :], fd[:, q, j, :], hpb[:, q, j:j + 512], j == 0, j == 2, [fd, hpb], [PB[bank]],
                         nowaw=(j > 0))
                if q < 2:
                    P.act(gt[(grp % 2) * 2 + q][:, :], PB[bank][:, :], AF.Silu, [PB[bank]], [gt[(grp % 2) * 2 + q]])
                else:
                    P.tt("vector", aT[:, 2 * grp + q - 2, :], PB[bank][:, :], gt[(grp % 2) * 2 + q - 2][:, :], ALU.mult,
                         [PB[bank], gt[(grp % 2) * 2 + q - 2]], [aT], nowaw=True)
            P.cp("gpsimd", halo[:, grp, :, :], hpb[:, :, 512:514], [hpb], [halo], nowaw=True)
        for idx in range(4):
            half, piece = idx // 2, idx % 2
            sl = wdn[idx % 2]
            for f in range(11):
                fc = piece * 11 + f
                for j in range(4):
                    P.mm(PB[4 + j][:, :], aT[:, fc, j * 128:(j + 1) * 128], sl[:, f, :], fc == 0, fc == 21,
                         [aT, sl], [PB[4 + j]], nowaw=(fc > 0))
            if idx + 2 < 4:
                load_wdn(idx + 2)
            if piece == 1:
                for j in range(4):
                    P.tt("vector", xt[:, j, half * 512:(half + 1) * 512], PB[4 + j][:, :],
                         xt[:, j, half * 512:(half + 1) * 512], ALU.add, [PB[4 + j], xt], [xt])
        rms_to(xt, xt, nwO)
        P.dma("gpsimd", out_d[s, t0:t0 + 512, :].rearrange("(j p) d -> p j d", p=128), xt[:, :, :], [xt], [CONST_OUT])

    P.emit()
    return nc


def host_consts(T):
    identf = np.eye(128, dtype=np.float32)
    identb = identf.astype(ml_dtypes.bfloat16)
    rot = np.zeros((128, 128), np.float32)
    for g in range(2):
        for d in range(32):
            rot[g * 64 + d + 32, g * 64 + d] = -1.0
            rot[g * 64 + d, g * 64 + d + 32] = 1.0
    inv_freq = (10000.0 ** (-np.arange(0, 64, 2, dtype=np.float32) / 64)).astype(np.float32)
    ang = np.arange(T, dtype=np.float32)[None, :] * inv_freq[:, None]
    cos = np.cos(ang).astype(np.float32); sin = np.sin(ang).astype(np.float32)
    rope = np.zeros((128, 2, T), np.float32)
    for r in range(128):
        rope[r, 0] = cos[r % 32]
        rope[r, 1] = sin[r % 32]
    kk = np.arange(128)[:, None]; qq = np.arange(128)[None, :]
    trim = np.where(kk <= qq, 0.0, -30000.0).astype(np.float32).astype(ml_dtypes.bfloat16)
    s = np.arange(64)[:, None]; i = np.arange(64)[None, :]
    cm = np.zeros((64, 5, 64), np.float32)
    cm[:, 0] = (s <= i); cm[:, 1] = (s < i); cm[:, 2] = (s <= i); cm[:, 3] = (s == i); cm[:, 4] = (s > i)
    return {"c_identb": identb, "c_identf": identf, "c_rot": rot.astype(ml_dtypes.bfloat16), "c_rope": rope,
            "c_trimask": trim, "c_masks": cm}


_W1 = ["attn_norm_w", "w_in", "da_lambda_q1", "da_lambda_k1", "da_lambda_q2", "da_lambda_k2", "da_subln_w",
       "gdn_conv_w", "gdn_a_log", "gdn_dt_bias", "gdn_norm_w", "w_out", "ffn_norm_w", "ffn_w_up", "ffn_conv_w",
       "ffn_w_down"]


def make_in_maps(inputs, n_cores, nseq, T):
    consts = host_consts(T)
    base = {k: np.ascontiguousarray(np.asarray(inputs[k], np.float32)[0]) for k in _W1}
    base["final_norm_w"] = np.ascontiguousarray(np.asarray(inputs["final_norm_w"], np.float32))
    base["gdn_conv_w"] = np.ascontiguousarray(base["gdn_conv_w"].reshape(4, 12, 128).transpose(2, 1, 0))
    base["ffn_conv_w"] = np.ascontiguousarray(base["ffn_conv_w"].reshape(3, 44, 128).transpose(2, 1, 0))
    base.update(consts)
    x = np.asarray(inputs["x"], np.float32)
    maps = []
    for c in range(n_cores):
        m = dict(base)
        m["x"] = np.ascontiguousarray(x[c * nseq:(c + 1) * nseq])
        maps.append(m)
    return maps


def kernel(**inputs):
    x = inputs["x"]
    B, T, _ = x.shape
    n = 8
    nseq = B // n
    nc = build(T, nseq)
    maps = make_in_maps(inputs, n, nseq, T)
    res = run_bass_kernel_spmd(nc, maps, core_ids=list(range(n)))
    return np.concatenate([r["out"] for r in res.results], axis=0)
```

```python
import contextlib
import math
import numpy as np
import ml_dtypes
import concourse.bass as bass
import concourse.mybir as mybir
from concourse.bass_utils import run_bass_kernel_spmd

F32 = mybir.dt.float32
BF16 = mybir.dt.bfloat16
AF = mybir.ActivationFunctionType
ALU = mybir.AluOpType
AX = mybir.AxisListType

ENGS = ("sync", "scalar", "gpsimd", "vector", "tensor")
NDMA = 8
EPS = 1e-6
D = 1024
DFF = 2816
INC = 3592
LAMBDA_INIT = 0.8 - 0.6 * math.exp(-0.3 * 0)


class Buf:
    def __init__(self, name, t):
        self.name = name
        self.t = t
        self.writers = []
        self.readers = []
        self.gen_deps = set()
        self.psum = False

    def __getitem__(self, idx):
        return self.t[idx]


class Op:
    __slots__ = ("eng", "fn", "deps", "is_dma", "signal", "tok", "idx")

    def __init__(self, eng, fn, is_dma):
        self.eng = eng
        self.fn = fn
        self.deps = set()
        self.is_dma = is_dma
        self.signal = False
        self.tok = None


class Prog:
    def __init__(self, nc):
        self.nc = nc
        self.ops = []

    def sb(self, name, shape, dt=F32):
        return Buf(name, self.nc.alloc_sbuf_tensor(name, list(shape), dt))

    def ps(self, name, shape, dt=F32):
        b = Buf(name, self.nc.alloc_psum_tensor(name, list(shape), dt))
        b.psum = True
        return b

    def dram(self, name, shape, dt=F32, kind="Internal"):
        return Buf(name, self.nc.dram_tensor(name, list(shape), dt, kind=kind))

    def op(self, eng, fn, reads=(), writes=(), dma=False, nowaw=False):
        o = Op(eng, fn, dma)
        o.idx = len(self.ops)
        deps = o.deps
        for r in reads:
            deps.update(r.writers)
            if r.psum:
                for ri in r.readers:
                    if self.ops[ri].eng != eng:
                        deps.add(ri)
        for w in writes:
            if nowaw and w.writers:
                deps.update(w.gen_deps)
                deps.update(w.readers)
                w.gen_deps.update(w.readers)
                w.writers.append(o.idx)
                w.readers = []
            else:
                g = set(w.writers) | set(w.readers)
                deps.update(g)
                w.gen_deps = g
                w.writers = [o.idx]
                w.readers = []
        for r in reads:
            r.readers.append(o.idx)
        deps.discard(o.idx)
        self.ops.append(o)
        return o

    def dma(self, eng, out_ap, in_ap, reads, writes, nowaw=False):
        return self.op(eng, lambda e: e.dma_start(out=out_ap, in_=in_ap), reads, writes, dma=True, nowaw=nowaw)

    def mm(self, out_ap, lhsT, rhs, start, stop, reads, writes, nowaw=False):
        return self.op("tensor", lambda e: e.matmul(out_ap, lhsT=lhsT, rhs=rhs, start=start, stop=stop),
                       reads, writes, nowaw=nowaw)

    def tr(self, out_ap, in_ap, ident, reads, writes, nowaw=False):
        return self.op("tensor", lambda e: e.transpose(out_ap, in_ap, ident), reads, writes, nowaw=nowaw)

    def act(self, out_ap, in_ap, func, reads, writes, scale=1.0, bias=None, accum=None, eng="scalar", nowaw=False):
        def f(e):
            kw = dict(out=out_ap, in_=in_ap, func=func, scale=scale)
            if bias is not None:
                kw["bias"] = bias
            if accum is not None:
                kw["accum_out"] = accum
            return e.activation(**kw)
        return self.op(eng, f, reads, writes, nowaw=nowaw)

    def cp(self, eng, out_ap, in_ap, reads, writes, nowaw=False):
        if eng == "scalar":
            return self.op(eng, lambda e: e.copy(out=out_ap, in_=in_ap), reads, writes, nowaw=nowaw)
        return self.op(eng, lambda e: e.tensor_copy(out=out_ap, in_=in_ap), reads, writes, nowaw=nowaw)

    def tt(self, eng, out_ap, in0, in1, op, reads, writes, nowaw=False):
        return self.op(eng, lambda e: e.tensor_tensor(out=out_ap, in0=in0, in1=in1, op=op), reads, writes, nowaw=nowaw)

    def ts(self, eng, out_ap, in0, s1, s2, op0, op1, reads, writes, nowaw=False):
        if s2 is None:
            return self.op(eng, lambda e: e.tensor_scalar(out=out_ap, in0=in0, scalar1=s1, scalar2=None, op0=op0),
                           reads, writes, nowaw=nowaw)
        return self.op(eng, lambda e: e.tensor_scalar(out=out_ap, in0=in0, scalar1=s1, scalar2=s2, op0=op0, op1=op1),
                       reads, writes, nowaw=nowaw)

    def stt(self, eng, out_ap, in0, scalar, in1, op0, op1, reads, writes, nowaw=False):
        return self.op(eng, lambda e: e.scalar_tensor_tensor(out=out_ap, in0=in0, scalar=scalar, in1=in1, op0=op0, op1=op1),
                       reads, writes, nowaw=nowaw)

    def memset(self, eng, ap, val, writes, nowaw=False):
        return self.op(eng, lambda e: e.memset(ap, val), [], writes, nowaw=nowaw)

    def emit(self, final_wait_eng="sync"):
        nc = self.nc
        import os
        kcut = int(os.environ.get("KCUT", "0"))
        if kcut:
            self.ops = self.ops[:kcut]
        ops = self.ops
        print("emit: n_ops =", len(ops), flush=True)
        for o in ops:
            for d in o.deps:
                od = ops[d]
                if od.eng == "tensor" and o.eng == "tensor" and not od.is_dma and not o.is_dma:
                    continue
                od.signal = True
            if o.is_dma:
                o.signal = True
        with contextlib.ExitStack() as st:
            esem = {e: st.enter_context(nc.semaphore("s_" + e)) for e in ENGS}
            dsem = {e: [st.enter_context(nc.semaphore("d_%s%d" % (e, i))) for i in range(NDMA)]
                    for e in ("sync", "scalar", "gpsimd")}
            ecount = {e: 0 for e in ENGS}
            dcount = {e: [0] * NDMA for e in dsem}
            drot = {e: 0 for e in dsem}
            prewait = {}
            for o in ops:
                if not o.signal:
                    continue
                if o.is_dma:
                    i = drot[o.eng]
                    drot[o.eng] = (i + 1) % NDMA
                    prewait[o.idx] = (dsem[o.eng][i], dcount[o.eng][i])
                    dcount[o.eng][i] += 16
                    o.tok = (dsem[o.eng][i], dcount[o.eng][i], 16)
                else:
                    ecount[o.eng] += 1
                    o.tok = (esem[o.eng], ecount[o.eng], 1)
            by_eng = {e: [o for o in ops if o.eng == e] for e in ENGS}
            block = st.enter_context(nc.Block())

            def run(e, eng):
                known = {}
                for o in by_eng[e]:
                    waits = {}
                    for d in o.deps:
                        od = ops[d]
                        if od.tok is None:
                            continue
                        if od.eng == "tensor" and e == "tensor" and not od.is_dma and not o.is_dma:
                            continue
                        s, v, _ = od.tok
                        k = id(s)
                        if known.get(k, 0) >= v:
                            continue
                        if k not in waits or waits[k][1] < v:
                            waits[k] = (s, v)
                    if o.idx in prewait:
                        s, v = prewait[o.idx]
                        k = id(s)
                        if v > 0 and known.get(k, 0) < v and (k not in waits or waits[k][1] < v):
                            waits[k] = (s, v)
                    for k, (s, v) in waits.items():
                        eng.wait_ge(s, v)
                        known[k] = v
                    ins = o.fn(eng)
                    if o.tok is not None:
                        ins.then_inc(o.tok[0], o.tok[2])
                if e == final_wait_eng:
                    for q in dsem:
                        for i in range(NDMA):
                            if dcount[q][i] > 0:
                                eng.wait_ge(dsem[q][i], dcount[q][i])

            @block.sync
            def _(eng):
                run("sync", eng)

            @block.scalar
            def _(eng):
                run("scalar", eng)

            @block.gpsimd
            def _(eng):
                run("gpsimd", eng)

            @block.vector
            def _(eng):
                run("vector", eng)

            @block.tensor
            def _(eng):
                run("tensor", eng)


def build(T, NSEQ, debug=False, do_c=True, stop_after=None):
    nc = bass.Bass("TRN2", target_bir_lowering=False)
    P = Prog(nc)
    NT = T // 512
    NKT = T // 128
    dk = "ExternalOutput"

    def din(name, shape, dt=F32):
        return P.dram(name, shape, dt, kind="ExternalInput")

    x_d = din("x", [NSEQ, T, D])
    anw_d = din("attn_norm_w", [D])
    win_d = din("w_in", [D, INC])
    lq1_d = din("da_lambda_q1", [64]); lk1_d = din("da_lambda_k1", [64])
    lq2_d = din("da_lambda_q2", [64]); lk2_d = din("da_lambda_k2", [64])
    subln_d = din("da_subln_w", [128])
    gcw_d = din("gdn_conv_w", [128, 12, 4])
    alog_d = din("gdn_a_log", [4]); dtb_d = din("gdn_dt_bias", [4])
    gnw_d = din("gdn_norm_w", [128])
    wout_d = din("w_out", [D, D])
    fnw_d = din("ffn_norm_w", [D])
    wup_d = din("ffn_w_up", [D, 2 * DFF])
    fcw_d = din("ffn_conv_w", [128, 44, 3])
    wdn_d = din("ffn_w_down", [DFF, D])
    finw_d = din("final_norm_w", [D])
    identb_d = din("c_identb", [128, 128], BF16)
    identf_d = din("c_identf", [128, 128])
    rot_d = din("c_rot", [128, 128], BF16)
    rope_d = din("c_rope", [128, 2, T])
    trim_d = din("c_trimask", [128, 128], BF16)
    cm_d = din("c_masks", [64, 5, 64])
    out_d = P.dram("out", [NSEQ, T, D], F32, kind="ExternalOutput")

    qT_d = P.dram("s_qT", [NSEQ, 4, 128, T], BF16, kind=dk)
    kT_d = P.dram("s_kT", [NSEQ, 4, 128, T], BF16, kind=dk)
    vda_d = P.dram("s_vda", [NSEQ, T, 512], BF16, kind=dk)
    gqT_d = P.dram("s_gqT", [NSEQ, 4, 128, T], BF16, kind=dk)
    gkT_d = P.dram("s_gkT", [NSEQ, 4, 128, T], BF16, kind=dk)
    gkn_d = P.dram("s_gkn", [NSEQ, T, 512], BF16, kind=dk)
    gv_d = P.dram("s_gv", [NSEQ, T, 512], BF16, kind=dk)
    gz_d = P.dram("s_gz", [NSEQ, T, 512], BF16, kind=dk)
    gsc_d = P.dram("s_gsc", [NSEQ, T, 16], F32, kind=dk)
    mix_d = P.dram("s_mix", [NSEQ, T, D], BF16, kind=dk)
    wupb_d = P.dram("s_wupb", [D, 2 * DFF], BF16)
    wdnb_d = P.dram("s_wdnb", [DFF, D], BF16)
    def units(n):
        return [Buf("u", None) for _ in range(n)]
    u_qT = units(NSEQ * NT); u_kT = units(NSEQ * NT); u_vda = units(NSEQ * NT)
    u_gqT = units(NSEQ * NT); u_gkT = units(NSEQ * NT); u_gkn = units(NSEQ * NT)
    u_gv = units(NSEQ * NT); u_gz = units(NSEQ * NT); u_gsc = units(NSEQ * NT)
    u_mixa = units(NSEQ * NT); u_mixg = units(NSEQ * NT)
    u_wupb = Buf("u", None); u_wdnb = Buf("u", None)
    CONST = Buf("const_in", None)
    CONST_OUT = Buf("const_out", None)

    PB = [P.ps("pb%d" % i, [128, 512]) for i in range(8)]

    def pbf(i):
        return PB[i].t[:, :].bitcast(BF16)

    ARENA_BYTES = 196 * 1024
    arena = nc.alloc_sbuf_tensor("arena", [128, ARENA_BYTES // 4], F32)
    live = []
    cur = [0]

    def aalloc(name, free_shape, dt):
        n = int(np.prod(free_shape))
        nb = n * (2 if dt == BF16 else 4)
        nb = (nb + 63) // 64 * 64
        s = cur[0]
        e = s + nb
        assert e <= ARENA_BYTES, (name, e)
        cur[0] = e
        ap = arena[:, s // 4:e // 4]
        if dt == BF16:
            ap = ap.bitcast(BF16)
        ap = ap[:, 0:n]
        if len(free_shape) == 2:
            ap = ap.rearrange("p (a b) -> p a b", a=free_shape[0])
        elif len(free_shape) == 3:
            ap = ap.rearrange("p (a b c) -> p a b c", a=free_shape[0], b=free_shape[1])
        elif len(free_shape) == 4:
            ap = ap.rearrange("p (a b c d) -> p a b c d", a=free_shape[0], b=free_shape[1], c=free_shape[2])
        b = Buf(name, ap)
        for (s0, e0, ob) in live:
            if s0 < e and s < e0:
                b.readers.extend(ob.writers)
                b.readers.extend(ob.readers)
        live.append((s, e, b))
        return b

    def areset(mark=0):
        cur[0] = mark

    identb = P.sb("identb", [128, 128], BF16)
    identf = P.sb("identf", [128, 128])
    rotm = P.sb("rotm", [128, 128], BF16)
    trim = P.sb("trim", [128, 128], BF16)
    cm = P.sb("cm", [64, 5, 64])
    negh = P.sb("negh", [128, 64])
    ones_f = P.sb("ones_f", [64, 128])
    P.dma("sync", identb[:, :], identb_d[:, :], [CONST], [identb])
    P.dma("sync", identf[:, :], identf_d[:, :], [CONST], [identf])
    P.dma("sync", rotm[:, :], rot_d[:, :], [CONST], [rotm])
    P.dma("sync", trim[:, :], trim_d[:, :], [CONST], [trim])
    P.dma("sync", cm[:, :, :], cm_d[:, :, :], [CONST], [cm])
    P.memset("vector", negh[:, :], -0.5, [negh])
    P.memset("vector", ones_f[:, :], 1.0, [ones_f])

    def bc(d_buf, n):
        return d_buf.t.ap().partition_broadcast(n)

    def rsqrt_(out_ap, in_ap, scale, nh_ap, reads, writes):
        P.ts("vector", out_ap, in_ap, scale, EPS, ALU.mult, ALU.add, reads, writes)
        P.tt("gpsimd", out_ap, out_ap, nh_ap, ALU.pow, list(writes) + [negh], writes)

    areset(0)
    Win = aalloc("Win", [8, INC], BF16)
    markW = cur[0]
    stg = [aalloc("stg%d" % i, [3592], F32) for i in range(2)]
    stgb = [aalloc("stgb%d" % i, [3592], BF16) for i in range(2)]
    ci = [0]
    cast_engs = ["vector", "gpsimd"]

    def cast_to_dram(src_ap, dst_ap, ncols, unit):
        i = ci[0] % 2
        ci[0] += 1
        P.dma("sync", stg[i][:, 0:ncols], src_ap, [CONST], [stg[i]])
        P.cp(cast_engs[i], stgb[i][:, 0:ncols], stg[i][:, 0:ncols], [stg[i]], [stgb[i]])
        P.dma("sync", dst_ap, stgb[i][:, 0:ncols], [stgb[i]], [unit], nowaw=True)

    for k in range(8):
        for hf in range(2):
            cast_to_dram(wup_d[k * 128:(k + 1) * 128, hf * DFF:(hf + 1) * DFF],
                         wupb_d[k * 128:(k + 1) * 128, hf * DFF:(hf + 1) * DFF], DFF, u_wupb)
    for k in range(22):
        cast_to_dram(wdn_d[k * 128:(k + 1) * 128, :], wdnb_d[k * 128:(k + 1) * 128, :], D, u_wdnb)

    for k in range(8):
        i = ci[0] % 2
        ci[0] += 1
        P.dma("sync", stg[i][:, 0:INC], win_d[k * 128:(k + 1) * 128, :], [CONST], [stg[i]])
        P.cp(cast_engs[i], Win[:, k, :], stg[i][:, 0:INC], [stg[i]], [Win], nowaw=True)
    areset(markW)
    xbuf = [aalloc("xbuf%d" % i, [4, D], F32) for i in range(2)]
    rpbuf = [aalloc("rp%d" % i, [2, 512], F32) for i in range(2)]
    xn = aalloc("xn", [4, D], BF16)
    xnT = aalloc("xnT", [8, 512], BF16)
    nwA = aalloc("nwA", [D], F32)
    junk = aalloc("junk", [D], BF16)
    ssq = aalloc("ssq", [4], F32)
    rstd = aalloc("rstd", [4], F32)
    xb = [aalloc("xb%d" % i, [512], BF16) for i in range(2)]
    t1 = [aalloc("t1_%d" % i, [512], F32) for i in range(2)]
    t2 = [aalloc("t2_%d" % i, [512], F32) for i in range(2)]
    ro = [aalloc("ro%d" % i, [512], BF16) for i in range(3)]
    gh = aalloc("gh", [12, 515], BF16)
    gdiag = aalloc("gdiag", [12, 4, 128], BF16)
    gcw = aalloc("gcw", [12, 4], F32)
    gs = [aalloc("gs%d" % i, [512], BF16) for i in range(3)]
    knt = [aalloc("knt%d" % i, [128], BF16) for i in range(2)]
    knTt = [aalloc("knTt%d" % i, [512], BF16) for i in range(2)]
    kn_t = aalloc("kn_t", [4, 512], BF16)
    v_t = aalloc("v_t", [4, 512], BF16)
    va_t = aalloc("va_t", [4, 512], BF16)
    z_t = aalloc("z_t", [4, 512], BF16)
    ssqk = aalloc("ssqk", [4, 8], F32)
    rk1 = aalloc("rk1", [4, 8], F32)
    gsc_t = aalloc("gsc_t", [4, 16], F32)
    ba_t = aalloc("ba_t", [4, 8], F32)
    tmp4 = aalloc("tmp4", [4, 4], F32)
    dtbB = aalloc("dtbB", [4], F32)
    negA = aalloc("negA", [4], F32)
    markA = cur[0]

    P.dma("sync", nwA[:, :], bc(anw_d, 128), [CONST], [nwA])
    P.dma("sync", gcw[:, :, :], gcw_d[:, :, :], [CONST], [gcw])
    P.dma("sync", dtbB[:, :], bc(dtb_d, 128), [CONST], [dtbB])
    P.dma("sync", negA[:, :], bc(alog_d, 128), [CONST], [negA])
    P.act(negA[:, :], negA[:, :], AF.Exp, [negA], [negA])
    P.ts("vector", negA[:, :], negA[:, :], -1.0, None, ALU.mult, None, [negA], [negA])
    for c in range(12):
        for j in range(4):
            P.ts("gpsimd", gdiag[:, c, j, :], identb[:, :], gcw[:, c, j:j + 1], None, ALU.mult, None,
                 [identb, gcw], [gdiag], nowaw=True)

    tiles = [(s, tt) for s in range(NSEQ) for tt in range(NT)]

    def loadA(g):
        s, tt = tiles[g]
        t0 = tt * 512
        P.dma("sync", xbuf[g % 2][:, :, :], x_d[s, t0:t0 + 512, :].rearrange("(j p) d -> p j d", p=128),
              [CONST], [xbuf[g % 2]])
        P.dma("sync", rpbuf[g % 2][:, :, :], rope_d[:, :, t0:t0 + 512], [CONST], [rpbuf[g % 2]])

    evi = [0]

    def ev_eng():
        evi[0] += 1
        return "vector" if evi[0] % 2 else "scalar"

    loadA(0)
    for g, (s, tt) in enumerate(tiles):
        t0 = tt * 512
        ug = s * NT + tt
        if g + 1 < len(tiles):
            loadA(g + 1)
        xt = xbuf[g % 2]
        rp = rpbuf[g % 2]
        P.memset("vector", ssq[:, :], 0.0, [ssq])
        for j in range(4):
            P.act(junk[:, :], xt[:, j, :], AF.Square, [xt, ssq], [junk, ssq], accum=ssq[:, j:j + 1], nowaw=True)
        rsqrt_(rstd[:, :], ssq[:, :], 1.0 / D, negh[:, 0:4], [ssq], [rstd])
        for j in range(4):
            P.stt("vector", xn[:, j, :], xt[:, j, :], rstd[:, j:j + 1], nwA[:, :], ALU.mult, ALU.mult,
                  [xt, rstd, nwA], [xn], nowaw=True)
        for k in range(8):
            bk = k % 2
            for j in range(4):
                P.tr(pbf(bk)[:, j * 128:(j + 1) * 128], xn[:, j, k * 128:(k + 1) * 128], identb[:, :],
                     [xn, identb], [PB[bk]], nowaw=(j > 0))
            P.cp(ev_eng(), xnT[:, k, :], pbf(bk)[:, 0:512], [PB[bk]], [xnT], nowaw=True)

        def proj_fm(c0, bank):
            for k in range(8):
                P.mm(PB[bank][:, :], Win[:, k, c0:c0 + 128], xnT[:, k, :], k == 0, k == 7, [Win, xnT], [PB[bank]],
                     nowaw=(k > 0))

        for c in range(8):
            bank = 2 + (c % 2)
            proj_fm(c * 128, bank)
            i2 = c % 2
            P.cp("scalar", xb[i2][:, :], PB[bank][:, :], [PB[bank]], [xb[i2]])
            import os
            hk = os.environ.get("HACK", "")
            if hk == "a":
                P.tt("vector", t1[i2][:, :], PB[bank][:, :], nwA[:, 0:512], ALU.mult, [PB[bank], nwA], [t1[i2]])
            elif hk == "c":
                P.tt("vector", t1[i2][:, :], xn[:, 0, 0:512], nwA[:, 0:512], ALU.mult, [PB[bank], nwA, xn], [t1[i2]])
            elif hk == "d":
                P.tt("vector", t1[i2][:, :], PB[0][:, :], nwA[:, 0:512], ALU.mult, [PB[bank], PB[0], nwA], [t1[i2]])
            elif hk == "e":
                P.dma("gpsimd", qT_d[s, c % 4, :, t0:t0 + 512], xb[i2][:, :], [xb[i2]], [u_qT[ug]], nowaw=True)
            elif hk == "f":
                P.tt("vector", t1[i2][:, :], PB[bank][:, :], rp[:, 0, :], ALU.mult, [PB[bank], rp, xb[i2]], [t1[i2]])
            elif hk == "b":
                P.tt("vector", xnT[:, 0, :], PB[bank][:, :], rp[:, 0, :], ALU.mult, [PB[bank], rp], [xnT])
            else:
                P.tt("vector", t1[i2][:, :], PB[bank][:, :], rp[:, 0, :], ALU.mult, [PB[bank], rp], [t1[i2]])
            rb = 4 + (c % 2)
            P.mm(PB[rb][:, :], rotm[:, :], xb[i2][:, :], True, True, [rotm, xb[i2]], [PB[rb]])
            P.tt("vector", t2[i2][:, :], PB[rb][:, :], rp[:, 1, :], ALU.mult, [PB[rb], rp], [t2[i2]])
            r3 = ro[c % 3]
            P.tt("gpsimd", r3[:, :], t1[i2][:, :], t2[i2][:, :], ALU.add, [t1[i2], t2[i2]], [r3])
            if c < 4:
                P.dma("gpsimd", qT_d[s, c, :, t0:t0 + 512], r3[:, :], [r3], [u_qT[ug]], nowaw=True)
            else:
                P.dma("gpsimd", kT_d[s, c - 4, :, t0:t0 + 512], r3[:, :], [r3], [u_kT[ug]], nowaw=True)
        if tt == 0:
            P.memset("vector", gh[:, :, 0:3], 0.0, [gh])
        P.memset("vector", ssqk[:, :, :], 0.0, [ssqk])
        for c in range(12):
            bank = 2 + (c % 2)
            proj_fm(1536 + c * 128, bank)
            P.cp("scalar", gh[:, c, 3:515], PB[bank][:, :], [PB[bank]], [gh], nowaw=True)
        for c in range(12):
            bank = 4 + (c % 2)
            for j in range(4):
                P.mm(PB[bank][:, :], gdiag[:, c, j, :], gh[:, c, j:j + 512], j == 0, j == 3, [gdiag, gh], [PB[bank]],
                     nowaw=(j > 0))
            g3 = gs[c % 3]
            P.act(g3[:, :], PB[bank][:, :], AF.Silu, [PB[bank]], [g3])
            h = c % 4
            if c < 4:
                P.dma("gpsimd", gqT_d[s, h, :, t0:t0 + 512], g3[:, :], [g3], [u_gqT[ug]], nowaw=True)
                tb = 6 + (c % 2)
                for i in range(4):
                    P.tr(pbf(tb)[:, i * 128:(i + 1) * 128], g3[:, i * 128:(i + 1) * 128], identb[:, :],
                         [g3, identb], [PB[tb]], nowaw=(i > 0))
                for i in range(4):
                    P.act(junk[:, 0:128], pbf(tb)[:, i * 128:(i + 1) * 128], AF.Square, [PB[tb], ssqk], [junk, ssqk],
                          accum=ssqk[:, i, 4 + h:5 + h], nowaw=True)
            elif c < 8:
                tb = 6 + (c % 2)
                for i in range(4):
                    P.tr(pbf(tb)[:, i * 128:(i + 1) * 128], g3[:, i * 128:(i + 1) * 128], identb[:, :],
                         [g3, identb], [PB[tb]], nowaw=(i > 0))
                for i in range(4):
                    P.act(junk[:, 0:128], pbf(tb)[:, i * 128:(i + 1) * 128], AF.Square, [PB[tb], ssqk], [junk, ssqk],
                          accum=ssqk[:, i, h:h + 1], nowaw=True)
                rsqrt_(rk1[:, :, h:h + 1], ssqk[:, :, h:h + 1], 1.0, negh[:, 0:4].unsqueeze(2), [ssqk], [rk1])
                for i in range(4):
                    P.ts("vector", kn_t[:, i, h * 128:(h + 1) * 128], pbf(tb)[:, i * 128:(i + 1) * 128],
                         rk1[:, i, h:h + 1], None, ALU.mult, None, [PB[tb], rk1], [kn_t], nowaw=True)
                kT_ = knTt[c % 2]
                tb2 = 2 + (c % 2)
                for i in range(4):
                    P.tr(pbf(tb2)[:, i * 128:(i + 1) * 128], kn_t[:, i, h * 128:(h + 1) * 128], identb[:, :],
                         [kn_t, identb], [PB[tb2]], nowaw=(i > 0))
                P.cp("vector", kT_[:, :], pbf(tb2)[:, 0:512], [PB[tb2]], [kT_])
                P.dma("gpsimd", gkT_d[s, h, :, t0:t0 + 512], kT_[:, :], [kT_], [u_gkT[ug]], nowaw=True)
            else:
                tb = 6 + (c % 2)
                for i in range(4):
                    P.tr(pbf(tb)[:, i * 128:(i + 1) * 128], g3[:, i * 128:(i + 1) * 128], identb[:, :],
                         [g3, identb], [PB[tb]], nowaw=(i > 0))
                P.cp("vector", v_t[:, :, h * 128:(h + 1) * 128],
                     pbf(tb)[:, 0:512].rearrange("p (i d) -> p i d", i=4), [PB[tb]], [v_t], nowaw=True)
        P.cp("gpsimd", gh[:, :, 0:3], gh[:, :, 512:515], [gh], [gh])
        P.dma("gpsimd", gkn_d[s, t0:t0 + 512, :].rearrange("(j p) f -> p j f", p=128), kn_t[:, :, :], [kn_t], [u_gkn[ug]])
        P.dma("gpsimd", gv_d[s, t0:t0 + 512, :].rearrange("(j p) f -> p j f", p=128), v_t[:, :, :], [v_t], [u_gv[ug]])
        for i in range(4):
            bank = 2 + (i % 2)
            for k in range(8):
                P.mm(PB[bank][:, :], xnT[:, k, i * 128:(i + 1) * 128], Win[:, k, 1024:1536], k == 0, k == 7,
                     [Win, xnT], [PB[bank]], nowaw=(k > 0))
            P.cp(ev_eng(), va_t[:, i, :], PB[bank][:, :], [PB[bank]], [va_t], nowaw=True)
            bank = 4 + (i % 2)
            for k in range(8):
                P.mm(PB[bank][:, :], xnT[:, k, i * 128:(i + 1) * 128], Win[:, k, 3072:3584], k == 0, k == 7,
                     [Win, xnT], [PB[bank]], nowaw=(k > 0))
            P.act(z_t[:, i, :], PB[bank][:, :], AF.Silu, [PB[bank]], [z_t], nowaw=True)
        for i in range(4):
            for k in range(8):
                P.mm(PB[6][:, i * 8:(i + 1) * 8], xnT[:, k, i * 128:(i + 1) * 128], Win[:, k, 3584:3592], k == 0, k == 7,
                     [Win, xnT], [PB[6]], nowaw=not (i == 0 and k == 0))
        P.cp("vector", ba_t[:, :, :], PB[6][:, 0:32].rearrange("p (i e) -> p i e", i=4), [PB[6]], [ba_t])
        P.dma("gpsimd", vda_d[s, t0:t0 + 512, :].rearrange("(j p) f -> p j f", p=128), va_t[:, :, :], [va_t], [u_vda[ug]])
        P.dma("gpsimd", gz_d[s, t0:t0 + 512, :].rearrange("(j p) f -> p j f", p=128), z_t[:, :, :], [z_t], [u_gz[ug]])
        rsqrt_(gsc_t[:, :, 4:8], ssqk[:, :, 4:8], 1.0, negh[:, 0:16].rearrange("p (a b) -> p a b", a=4), [ssqk], [gsc_t])
        P.ts("vector", gsc_t[:, :, 4:8], gsc_t[:, :, 4:8], 128.0 ** -0.5, None, ALU.mult, None, [gsc_t], [gsc_t])
        P.cp("vector", gsc_t[:, :, 0:4], rk1[:, :, 0:4], [rk1], [gsc_t])
        P.act(gsc_t[:, :, 8:12], ba_t[:, :, 0:4], AF.Sigmoid, [ba_t], [gsc_t])
        P.tt("vector", tmp4[:, :, :], ba_t[:, :, 4:8], dtbB[:, :].unsqueeze(1).to_broadcast([128, 4, 4]), ALU.add,
             [ba_t, dtbB], [tmp4])
        P.act(tmp4[:, :, :], tmp4[:, :, :], AF.Exp, [tmp4], [tmp4])
        P.act(tmp4[:, :, :], tmp4[:, :, :], AF.Ln, [tmp4], [tmp4], bias=1.0)
        P.tt("vector", gsc_t[:, :, 12:16], tmp4[:, :, :], negA[:, :].unsqueeze(1).to_broadcast([128, 4, 4]), ALU.mult,
             [tmp4, negA], [gsc_t])
        P.dma("gpsimd", gsc_d[s, t0:t0 + 512, :].rearrange("(j p) f -> p j f", p=128), gsc_t[:, :, :], [gsc_t], [u_gsc[ug]])


    if stop_after == "A":
        P.emit()
        return nc

    def phase_c():
        areset(0)
        knT_c = aalloc("c_knT", [4, 512], BF16)
        qT_c = aalloc("c_qT", [4, 512], BF16)
        kn_c = aalloc("c_kn", [8, 512], BF16)
        v_c = aalloc("c_v", [8, 512], BF16)
        z_c = aalloc("c_z", [8, 512], BF16)
        sc_c = aalloc("c_sc", [8, 16], F32)
        g_c = aalloc("c_g", [32], F32)
        gcum = aalloc("c_gcum", [32], F32)
        egc = aalloc("c_egc", [32], F32)
        egl = aalloc("c_egl", [32], F32)
        kdc = aalloc("c_kdc", [32], F32)
        nbeta = aalloc("c_nbeta", [32], F32)
        beta_c = aalloc("c_beta", [32], F32)
        rq_c = aalloc("c_rq", [32], F32)
        rqe = aalloc("c_rqe", [32], F32)
        gU = aalloc("c_gU", [8, 64], F32)
        Gt = aalloc("c_Gt", [8, 64], F32)
        tSU = aalloc("c_tSU", [8, 64], F32)
        tU = aalloc("c_tU", [8, 64], F32)
        Pm = [aalloc("c_P%d" % i, [8, 64], F32) for i in range(2)]
        PTm = [aalloc("c_PT%d" % i, [8, 64], F32) for i in range(2)]
        Am = aalloc("c_A", [8, 64], F32)
        Wb = aalloc("c_Wb", [8, 64], BF16)
        kg_b = aalloc("c_kg", [8, 128], BF16)
        wT_all = aalloc("c_wT", [32, 64], BF16)
        ub_all = aalloc("c_ub", [32, 128], F32)
        Aq_all = aalloc("c_Aq", [32, 64], BF16)
        kdec_all = aalloc("c_kdec", [32, 128], BF16)
        S = [aalloc("c_S%d" % i, [128], F32) for i in range(4)]
        Sb = [aalloc("c_Sb%d" % i, [128], BF16) for i in range(4)]
        vnew = [aalloc("c_vn%d" % i, [128], BF16) for i in range(4)]
        t1c = [aalloc("c_t1%d" % i, [128], F32) for i in range(4)]
        obuf = aalloc("c_o", [8, 512], F32)
        osq = aalloc("c_osq", [8, 512], F32)
        ssqg = aalloc("c_ssqg", [32], F32)
        rstdg = aalloc("c_rstdg", [32], F32)
        mixg = aalloc("c_mixg", [8, 512], BF16)
        gnwB = aalloc("c_gnwB", [128], F32)
        P.dma("sync", gnwB[:, :], bc(gnw_d, 128), [CONST], [gnwB])
        H = slice(0, 64)

        def b3(ap2, n):
            return ap2.unsqueeze(2).to_broadcast([64, 8, n])

        for g, (s, tt) in enumerate(tiles):
            t0 = tt * 512
            ug = s * NT + tt
            P.dma("sync", knT_c[:, :, :], gkT_d[s, :, :, t0:t0 + 512].rearrange("h p t -> p h t"), [u_gkT[ug]], [knT_c])
            P.dma("sync", qT_c[:, :, :], gqT_d[s, :, :, t0:t0 + 512].rearrange("h p t -> p h t"), [u_gqT[ug]], [qT_c])
            P.dma("sync", kn_c[H, :, :], gkn_d[s, t0:t0 + 512, :].rearrange("(n c) f -> c n f", c=64), [u_gkn[ug]], [kn_c])
            P.dma("sync", v_c[H, :, :], gv_d[s, t0:t0 + 512, :].rearrange("(n c) f -> c n f", c=64), [u_gv[ug]], [v_c])
            P.dma("sync", z_c[H, :, :], gz_d[s, t0:t0 + 512, :].rearrange("(n c) f -> c n f", c=64), [u_gz[ug]], [z_c])
            P.dma("sync", sc_c[H, :, :], gsc_d[s, t0:t0 + 512, :].rearrange("(n c) f -> c n f", c=64), [u_gsc[ug]], [sc_c])
            if tt == 0:
                for h in range(4):
                    P.memset("vector", S[h][:, :], 0.0, [S[h]])
                    P.memset("vector", Sb[h][:, :], 0.0, [Sb[h]])
            g3 = g_c[H, :].rearrange("p (n h) -> p n h", n=8)
            P.cp("vector", g3, sc_c[H, :, 12:16], [sc_c], [g_c])
            P.cp("vector", beta_c[H, :].rearrange("p (n h) -> p n h", n=8), sc_c[H, :, 8:12], [sc_c], [beta_c])
            P.cp("vector", rq_c[H, :].rearrange("p (n h) -> p n h", n=8), sc_c[H, :, 4:8], [sc_c], [rq_c])
            P.ts("vector", nbeta[H, :], beta_c[H, :], -1.0, None, ALU.mult, None, [beta_c], [nbeta])
            P.mm(PB[0][H, 0:32], cm[:, 0, :], g_c[H, :], True, True, [cm, g_c], [PB[0]])
            P.mm(PB[1][:, 0:32], ones_f[:, :], g_c[H, :], True, True, [ones_f, g_c], [PB[1]])
            P.cp("vector", gcum[H, :], PB[0][H, 0:32], [PB[0]], [gcum])
            P.act(egc[H, :], PB[0][H, 0:32], AF.Exp, [PB[0]], [egc])
            P.act(egl[:, :], PB[1][:, 0:32], AF.Exp, [PB[1]], [egl])
            P.tt("vector", kdc[H, :], PB[1][H, 0:32], gcum[H, :], ALU.subtract, [PB[1], gcum], [kdc])
            P.act(kdc[H, :], kdc[H, :], AF.Exp, [kdc], [kdc])
            P.tt("vector", rqe[H, :], rq_c[H, :], egc[H, :], ALU.mult, [rq_c, egc], [rqe])
            for bb in range(4):
                ps = slice(bb * 8, bb * 8 + 8)
                pairs = [(2 * bb + q // 4, q % 4) for q in range(8)]
                P.tt("vector", gU[H, :, :], cm[:, 0, :].unsqueeze(1).to_broadcast([64, 8, 64]), b3(g_c[H, ps], 64), ALU.mult,
                     [cm, g_c], [gU])
                for q, (n, h) in enumerate(pairs):
                    P.mm(PB[0][H, q * 64:(q + 1) * 64], cm[:, 4, :], gU[H, q, :], True, True, [cm, gU], [PB[0]], nowaw=(q > 0))
                P.act(Gt[H, :, :], PB[0][H, :].rearrange("p (q i) -> p q i", q=8), AF.Exp, [PB[0]], [Gt])
                for q, (n, h) in enumerate(pairs):
                    ks = knT_c[:, h, n * 64:(n + 1) * 64]
                    P.mm(PB[1][H, q * 64:(q + 1) * 64], ks, ks, True, True, [knT_c], [PB[1]], nowaw=(q > 0))
                for q, (n, h) in enumerate(pairs):
                    ks = knT_c[:, h, n * 64:(n + 1) * 64]
                    P.mm(PB[2][H, q * 64:(q + 1) * 64], ks, qT_c[:, h, n * 64:(n + 1) * 64], True, True, [knT_c, qT_c], [PB[2]],
                         nowaw=(q > 0))
                m8 = lambda i: cm[:, i, :].unsqueeze(1).to_broadcast([64, 8, 64])
                P.tt("gpsimd", tSU[H, :, :], Gt[H, :, :], m8(1), ALU.mult, [Gt, cm], [tSU])
                P.stt("vector", tSU[H, :, :], tSU[H, :, :], -1.0, b3(beta_c[H, ps], 64), ALU.mult, ALU.mult, [tSU, beta_c], [tSU])
                P.tt("gpsimd", tU[H, :, :], Gt[H, :, :], m8(2), ALU.mult, [Gt, cm], [tU])
                P0 = Pm[0]
                P.tt("vector", P0[H, :, :], PB[1][H, :].rearrange("p (q i) -> p q i", q=8), tSU[H, :, :], ALU.mult,
                     [PB[1], tSU], [P0])
                P.tt("vector", Aq_all[H, ps, :], PB[2][H, :].rearrange("p (q i) -> p q i", q=8), tU[H, :, :], ALU.mult,
                     [PB[2], tU], [Aq_all], nowaw=(bb > 0))
                for q in range(8):
                    P.tr(PB[3][H, q * 64:(q + 1) * 64], P0[H, q, :], identf[H, H], [P0, identf], [PB[3]], nowaw=(q > 0))
                P.cp("scalar", PTm[0][H, :, :], PB[3][H, :].rearrange("p (q i) -> p q i", q=8), [PB[3]], [PTm[0]])
                P.tt("gpsimd", Am[H, :, :], P0[H, :, :], m8(3), ALU.add, [P0, cm], [Am])
                for m in range(5):
                    Pc, PTc = Pm[m % 2], PTm[m % 2]
                    Pn, PTn = Pm[(m + 1) % 2], PTm[(m + 1) % 2]
                    if m < 4:
                        for q in range(8):
                            P.mm(PB[1][H, q * 64:(q + 1) * 64], PTc[H, q, :], Pc[H, q, :], True, True, [PTc, Pc], [PB[1]],
                                 nowaw=(q > 0))
                    for q in range(8):
                        P.mm(PB[2][H, q * 64:(q + 1) * 64], Pc[H, q, :], PTc[H, q, :], True, True, [PTc, Pc], [PB[2]],
                             nowaw=(q > 0))
                    if m < 4:
                        P.cp("vector", Pn[H, :, :], PB[1][H, :].rearrange("p (q i) -> p q i", q=8), [PB[1]], [Pn])
                    P.cp("scalar", PTn[H, :, :], PB[2][H, :].rearrange("p (q i) -> p q i", q=8), [PB[2]], [PTn])
                    for q in range(8):
                        P.mm(PB[3][H, q * 64:(q + 1) * 64], PTn[H, q, :], Am[H, q, :], True, True, [PTn, Am], [PB[3]],
                             nowaw=(q > 0))
                    P.tt("vector", Am[H, :, :], PB[3][H, :].rearrange("p (q i) -> p q i", q=8), Am[H, :, :], ALU.add,
                         [PB[3], Am], [Am])
                P.cp("gpsimd", Wb[H, :, :], Am[H, :, :], [Am], [Wb])
                knv = kn_c[H, 2 * bb:2 * bb + 2, :].rearrange("p n (h d) -> p (n h) d", h=4)
                vv = v_c[H, 2 * bb:2 * bb + 2, :].rearrange("p n (h d) -> p (n h) d", h=4)
                P.tt("gpsimd", kg_b[H, :, :], knv, b3(egc[H, ps], 128), ALU.mult, [kn_c, egc], [kg_b])
                P.tt("gpsimd", kdec_all[H, ps, :], knv, b3(kdc[H, ps], 128), ALU.mult, [kn_c, kdc], [kdec_all], nowaw=(bb > 0))
                for q in range(8):
                    P.mm(PB[0][:, q * 64:(q + 1) * 64], kg_b[H, q, :], Wb[H, q, :], True, True, [kg_b, Wb], [PB[0]], nowaw=(q > 0))
                P.cp("scalar", wT_all[:, ps, :], PB[0][:, :].rearrange("p (q i) -> p q i", q=8), [PB[0]], [wT_all], nowaw=(bb > 0))
                for hf in range(2):
                    bank = 1 + hf
                    for q4 in range(4):
                        q = hf * 4 + q4
                        P.mm(PB[bank][H, q4 * 128:(q4 + 1) * 128], Wb[H, q, :], vv[:, q, :], True, True, [Wb, v_c], [PB[bank]],
                             nowaw=(q4 > 0))
                    pq = slice(bb * 8 + hf * 4, bb * 8 + hf * 4 + 4)
                    P.tt("vector", ub_all[H, pq, :], PB[bank][H, :].rearrange("p (q d) -> p q d", q=4),
                         beta_c[H, pq].unsqueeze(2).to_broadcast([64, 4, 128]), ALU.mult, [PB[bank], beta_c], [ub_all],
                         nowaw=not (bb == 0 and hf == 0))
            for n in range(8):
                for h in range(4):
                    p = n * 4 + h
                    bank = 4 + h
                    B_ = PB[bank]
                    P.mm(B_[H, 0:128], wT_all[:, p, :], Sb[h][:, :], True, True, [wT_all, Sb[h]], [B_])
                    P.mm(B_[H, 128:256], qT_c[:, h, n * 64:(n + 1) * 64], Sb[h][:, :], True, True, [qT_c, Sb[h]], [B_], nowaw=True)
                    P.stt("vector", vnew[h][H, :], B_[H, 0:128], nbeta[H, p:p + 1], ub_all[H, p, :], ALU.mult, ALU.add,
                          [B_, nbeta, ub_all], [vnew[h]])
                    P.ts("vector", t1c[h][H, :], B_[H, 128:256], rqe[H, p:p + 1], None, ALU.mult, None, [B_, rqe], [t1c[h]])
                    P.mm(B_[H, 256:384], Aq_all[H, p, :], vnew[h][H, :], True, True, [Aq_all, vnew[h]], [B_])
                    P.mm(B_[:, 384:512], kdec_all[H, p, :], vnew[h][H, :], True, True, [kdec_all, vnew[h]], [B_], nowaw=True)
                    P.stt("vector", obuf[H, n, h * 128:(h + 1) * 128], B_[H, 256:384], rq_c[H, p:p + 1], t1c[h][H, :],
                          ALU.mult, ALU.add, [B_, rq_c, t1c[h]], [obuf], nowaw=not (n == 0 and h == 0))
                    P.stt("vector", S[h][:, :], S[h][:, :], egl[:, p:p + 1], B_[:, 384:512], ALU.mult, ALU.add,
                          [S[h], egl, B_], [S[h]])
                    P.cp("gpsimd", Sb[h][:, :], S[h][:, :], [S[h]], [Sb[h]])
            o3 = obuf[H, :, :].rearrange("p n (h d) -> p (n h) d", h=4)
            P.tt("gpsimd", osq[H, :, :], obuf[H, :, :], obuf[H, :, :], ALU.mult, [obuf], [osq])
            P.op("vector", lambda e: e.reduce_sum(out=ssqg[H, :], in_=osq[H, :, :].rearrange("p n (h d) -> p (n h) d", h=4), axis=AX.X),
                 [osq], [ssqg])
            rsqrt_(rstdg[H, :], ssqg[H, :], 1.0 / 128, negh[H, 0:32], [ssqg], [rstdg])
            P.tt("vector", o3, o3, rstdg[H, :].unsqueeze(2).to_broadcast([64, 32, 128]), ALU.mult, [obuf, rstdg], [obuf])
            P.tt("gpsimd", o3, o3, gnwB[H, :].unsqueeze(1).to_broadcast([64, 32, 128]), ALU.mult, [obuf, gnwB], [obuf])
            P.tt("vector", mixg[H, :, :], obuf[H, :, :], z_c[H, :, :], ALU.mult, [obuf, z_c], [mixg])
            P.dma("gpsimd", mix_d[s, t0:t0 + 512, 512:1024].rearrange("(n c) f -> c n f", c=64), mixg[H, :, :], [mixg],
                  [u_mixg[ug]])

    areset(0)
    qTb = [aalloc("qTb%d" % i, [T], BF16) for i in range(2)]
    kTb = [aalloc("kTb%d" % i, [T], BF16) for i in range(2)]
    vAb = [aalloc("vAb%d" % i, [NKT, 130], BF16) for i in range(2)]
    pT = [aalloc("pT%d" % i, [512], BF16) for i in range(3)]
    o1 = aalloc("o1", [4, 128], F32)
    o2 = aalloc("o2", [4, 128], F32)
    rinv = aalloc("rinv", [4], F32)
    rinv2 = aalloc("rinv2", [4], F32)
    ssqo = aalloc("ssqo", [4], F32)
    rstdo = aalloc("rstdo", [4], F32)
    mixt = [aalloc("mixt%d" % i, [4, 128], BF16) for i in range(2)]
    lamb = aalloc("lamb", [4, 64], F32)
    lamp = aalloc("lamp", [2, 64], F32)
    lams = aalloc("lams", [2], F32)
    neglam = aalloc("neglam", [1], F32)
    sublnB = aalloc("sublnB", [128], F32)
    junkB = aalloc("junkB", [128], F32)

    for i, dd in enumerate((lq1_d, lk1_d, lq2_d, lk2_d)):
        P.dma("sync", lamb[:, i, :], bc(dd, 128), [CONST], [lamb], nowaw=True)
    P.dma("sync", sublnB[:, :], bc(subln_d, 128), [CONST], [sublnB])
    P.ts("vector", sublnB[:, :], sublnB[:, :], 1.0 - LAMBDA_INIT, None, ALU.mult, None, [sublnB], [sublnB])
    P.tt("vector", lamp[:, 0, :], lamb[:, 0, :], lamb[:, 1, :], ALU.mult, [lamb], [lamp], nowaw=True)
    P.tt("vector", lamp[:, 1, :], lamb[:, 2, :], lamb[:, 3, :], ALU.mult, [lamb], [lamp], nowaw=True)
    P.op("vector", lambda e: e.reduce_sum(out=lams[:, :], in_=lamp[:, :, :], axis=AX.X), [lamp], [lams])
    P.act(lams[:, :], lams[:, :], AF.Exp, [lams], [lams])
    P.tt("vector", neglam[:, :], lams[:, 1:2], lams[:, 0:1], ALU.subtract, [lams], [neglam])
    P.ts("vector", neglam[:, :], neglam[:, :], -LAMBDA_INIT, None, ALU.add, None, [neglam], [neglam])
    for i in range(2):
        P.memset("vector", vAb[i][:, :, 128:130], 1.0, [vAb[i]])

    heads = [(s, h) for s in range(NSEQ) for h in range(4)]

    def loadB(n):
        s, h = heads[n]
        i = n % 2
        rd = [u_qT[s * NT + tt] for tt in range(NT)]
        P.dma("sync", qTb[i][:, :], qT_d[s, h, :, :], rd, [qTb[i]])
        rd = [u_kT[s * NT + tt] for tt in range(NT)]
        P.dma("sync", kTb[i][:, :], kT_d[s, h, :, :], rd, [kTb[i]])
        rd = [u_vda[s * NT + tt] for tt in range(NT)]
        nq = max(1, NKT // 8)
        for a in range(0, NKT, nq):
            P.dma("sync", vAb[i][:, a:a + nq, 0:128],
                  vda_d[s, a * 128:(a + nq) * 128, h * 128:(h + 1) * 128].rearrange("(n p) d -> p n d", p=128),
                  rd, [vAb[i]], nowaw=(a > 0))

    loadB(0)
    kidx = 0
    ucnt = 0
    pending = [None]

    def flush():
        if pending[0] is not None:
            f = pending[0]
            pending[0] = None
            f()

    for n, (s, h) in enumerate(heads):
        flush()
        if n + 1 < len(heads):
            loadB(n + 1)
        qb, kb, vb = qTb[n % 2], kTb[n % 2], vAb[n % 2]
        for qt in range(NT):
            for c in range(2):
                pob = (4, 5) if ucnt % 2 == 0 else (6, 7)
                ucnt += 1
                fresh = {pob[0]: True, pob[1]: True}
                nk = 4 * qt + 4
                cs = slice(c * 64, (c + 1) * 64)
                for kt in range(nk):
                    j = kt - 4 * qt
                    sbk = PB[kidx % 3]
                    pt = pT[kidx % 3]
                    kidx += 1
                    ksl = kb[cs, kt * 128:(kt + 1) * 128]
                    q0 = qt * 512
                    if j < 0:
                        lo = 0
                        P.mm(sbk[:, 0:512], ksl, qb[cs, q0:q0 + 512], True, True, [kb, qb], [sbk])
                    else:
                        lo = j * 128
                        P.mm(sbk[:, lo:lo + 128], ksl, qb[cs, q0 + lo:q0 + lo + 128], True, False, [kb, qb], [sbk])
                        P.mm(sbk[:, lo:lo + 128], identb[:, :], trim[:, :], False, True, [identb, trim], [sbk], nowaw=True)
                        if lo + 128 < 512:
                            P.mm(sbk[:, lo + 128:512], ksl, qb[cs, q0 + lo + 128:q0 + 512], True, True, [kb, qb], [sbk],
                                 nowaw=True)
                    P.act(pt[:, lo:512], sbk[:, lo:512], AF.Exp, [sbk], [pt], scale=0.125)
                    flush()

                    def pv(kt=kt, j=j, pt=pt, vb=vb, pob=pob, fresh=fresh, qt=qt, c=c, nk=nk, s=s, h=h, n=n):
                        for i in range(max(j, 0), 4):
                            bank = pob[i // 2]
                            off = (i % 2) * 130
                            P.mm(PB[bank][:, off:off + 129], pt[:, i * 128:(i + 1) * 128], vb[:, kt, 0:129],
                                 fresh[bank], (kt == 4 * qt + i and i % 2 == 1), [pt, vb], [PB[bank]], nowaw=not fresh[bank])
                            fresh[bank] = False
                        if kt != nk - 1:
                            return
                        for i in range(4):
                            bank = pob[i // 2]
                            off = (i % 2) * 130
                            P.op("vector", (lambda bank=bank, off=off, i=i: (lambda e: e.reciprocal(out=rinv[:, i:i + 1], in_=PB[bank][:, off + 128:off + 129])))(),
                                 [PB[bank]], [rinv], nowaw=(i > 0))
                            if c == 0:
                                P.ts("vector", o1[:, i, :], PB[bank][:, off:off + 128], rinv[:, i:i + 1], None, ALU.mult, None,
                                     [PB[bank], rinv], [o1], nowaw=(i > 0))
                            else:
                                P.ts("vector", rinv2[:, i:i + 1], rinv[:, i:i + 1], neglam[:, 0:1], None, ALU.mult, None,
                                     [rinv, neglam], [rinv2], nowaw=(i > 0))
                                P.stt("vector", o2[:, i, :], PB[bank][:, off:off + 128], rinv2[:, i:i + 1], o1[:, i, :],
                                      ALU.mult, ALU.add, [PB[bank], rinv2, o1], [o2], nowaw=(i > 0))
                        if c == 0:
                            return
                        mt = mixt[(n * NT + qt) % 2]
                        P.memset("vector", ssqo[:, :], 0.0, [ssqo])
                        for i in range(4):
                            P.act(junkB[:, :], o2[:, i, :], AF.Square, [o2, ssqo], [junkB, ssqo], accum=ssqo[:, i:i + 1], nowaw=True)
                        rsqrt_(rstdo[:, :], ssqo[:, :], 1.0 / 128, negh[:, 0:4], [ssqo], [rstdo])
                        for i in range(4):
                            P.stt("vector", mt[:, i, :], o2[:, i, :], rstdo[:, i:i + 1], sublnB[:, :], ALU.mult, ALU.mult,
                                  [o2, rstdo, sublnB], [mt], nowaw=(i > 0))
                        P.dma("gpsimd", mix_d[s, qt * 512:(qt + 1) * 512, h * 128:(h + 1) * 128].rearrange("(i p) d -> p i d", p=128),
                              mt[:, :, :], [mt], [u_mixa[s * NT + qt]], nowaw=True)

                    pending[0] = pv
    flush()

    if stop_after == "B":
        P.emit()
        return nc
    if do_c:
        phase_c()
    if stop_after == "C":
        P.emit()
        return nc

    areset(0)
    Wout = aalloc("Wout", [8, D], BF16)
    xh = [aalloc("xh%d" % i, [4, D], F32) for i in range(2)]
    wst = [xh[1][:, 0, :], xh[1][:, 1, :]]
    mixb = aalloc("mixb", [4, D], BF16)
    mixT = aalloc("mixTD", [8, 512], BF16)
    wup = [aalloc("wup%d" % i, [2, 8, 256], BF16) for i in range(3)]
    wdn = [aalloc("wdn%d" % i, [11, 512], BF16) for i in range(2)]
    aT = aalloc("aT", [22, 512], BF16)
    hp = [aalloc("hp%d" % i, [4, 514], BF16) for i in range(2)]
    halo = aalloc("halo", [11, 4, 2], BF16)
    fdiag = aalloc("fdiag", [44, 3, 128], BF16)
    fcw = aalloc("fcw", [44, 3], F32)
    gt = [aalloc("gt%d" % i, [512], BF16) for i in range(4)]
    nwF = aalloc("nwF", [D], F32)
    nwO = aalloc("nwO", [D], F32)
    junkD = aalloc("junkD", [D], BF16)
    ssq2 = aalloc("ssq2", [4], F32)
    rstd2 = aalloc("rstd2", [4], F32)

    for k in range(8):
        P.dma("sync", wst[k % 2], wout_d[k * 128:(k + 1) * 128, :], [CONST], [xh[1]])
        P.cp(cast_engs[k % 2], Wout[:, k, :], wst[k % 2], [xh[1]], [Wout], nowaw=True)
    P.dma("sync", fcw[:, :, :], fcw_d[:, :, :], [CONST], [fcw])
    di = 0
    for cc in range(44):
        for j in range(3):
            eng_ = "vector" if di % 2 == 0 else "gpsimd"
            di += 1
            P.ts(eng_, fdiag[:, cc, j, :], identb[:, :], fcw[:, cc, j:j + 1], 1.0, ALU.mult, ALU.mult,
                 [identb, fcw], [fdiag], nowaw=True)
    P.dma("sync", nwF[:, :], bc(fnw_d, 128), [CONST], [nwF])
    P.dma("sync", nwO[:, :], bc(finw_d, 128), [CONST], [nwO])

    def loadD(g):
        s, tt = tiles[g]
        t0 = tt * 512
        P.dma("sync", xh[g % 2][:, :, :], x_d[s, t0:t0 + 512, :].rearrange("(j p) d -> p j d", p=128), [CONST], [xh[g % 2]])

    def load_wup(grp):
        sl = wup[grp % 3]
        P.dma("sync", sl[:, 0, :, :], wupb_d[:, grp * 256:(grp + 1) * 256].rearrange("(k p) c -> p k c", p=128),
              [u_wupb], [sl])
        P.dma("sync", sl[:, 1, :, :], wupb_d[:, DFF + grp * 256:DFF + (grp + 1) * 256].rearrange("(k p) c -> p k c", p=128),
              [u_wupb], [sl], nowaw=True)

    def load_wdn(idx):
        half, piece = idx // 2, idx % 2
        sl = wdn[idx % 2]
        P.dma("sync", sl[:, :, :],
              wdnb_d[piece * 11 * 128:(piece + 1) * 11 * 128, half * 512:(half + 1) * 512].rearrange("(f p) c -> p f c", p=128),
              [u_wdnb], [sl])

    def rms_to(src, dst, nw):
        P.memset("vector", ssq2[:, :], 0.0, [ssq2])
        for j in range(4):
            P.act(junkD[:, :], src[:, j, :], AF.Square, [src, ssq2], [junkD, ssq2], accum=ssq2[:, j:j + 1], nowaw=True)
        rsqrt_(rstd2[:, :], ssq2[:, :], 1.0 / D, negh[:, 0:4], [ssq2], [rstd2])
        for j in range(4):
            P.stt("vector", dst[:, j, :], src[:, j, :], rstd2[:, j:j + 1], nw[:, :], ALU.mult, ALU.mult,
                  [src, rstd2, nw], [dst], nowaw=(j > 0))

    def transpose_to(src, dst):
        for k in range(8):
            bk = k % 2
            for j in range(4):
                P.tr(pbf(bk)[:, j * 128:(j + 1) * 128], src[:, j, k * 128:(k + 1) * 128], identb[:, :],
                     [src, identb], [PB[bk]], nowaw=(j > 0))
            P.cp(ev_eng(), dst[:, k, :], pbf(bk)[:, 0:512], [PB[bk]], [dst], nowaw=(k > 0))

    loadD(0)
    for g, (s, tt) in enumerate(tiles):
        t0 = tt * 512
        ug = s * NT + tt
        if g + 1 < len(tiles):
            loadD(g + 1)
        xt = xh[g % 2]
        rdm = [u_mixa[ug]] + ([u_mixg[ug]] if do_c else [])
        P.dma("sync", mixb[:, :, :], mix_d[s, t0:t0 + 512, :].rearrange("(j p) d -> p j d", p=128), rdm, [mixb])
        load_wup(0)
        load_wup(1)
        transpose_to(mixb, mixT)
        bi = 0
        for j in range(4):
            for hf in range(2):
                bank = 2 + (bi % 2)
                bi += 1
                for k in range(8):
                    P.mm(PB[bank][:, :], mixT[:, k, j * 128:(j + 1) * 128], Wout[:, k, hf * 512:(hf + 1) * 512],
                         k == 0, k == 7, [mixT, Wout], [PB[bank]], nowaw=(k > 0))
                P.tt("vector", xt[:, j, hf * 512:(hf + 1) * 512], PB[bank][:, :], xt[:, j, hf * 512:(hf + 1) * 512], ALU.add,
                     [PB[bank], xt], [xt])
        rms_to(xt, mixb, nwF)
        transpose_to(mixb, mixT)
        if tt == 0:
            P.memset("vector", halo[:, :, :, :], 0.0, [halo])
        for grp in range(11):
            if grp + 2 < 11:
                load_wup(grp + 2)
            if grp == 9:
                load_wdn(0)
            if grp == 10:
                load_wdn(1)
            sl = wup[grp % 3]
            hpb = hp[grp % 2]
            chunks = [2 * grp, 2 * grp + 1, 22 + 2 * grp, 23 + 2 * grp]
            P.cp("gpsimd", hpb[:, :, 0:2], halo[:, grp, :, :], [halo], [hpb])
            for q in range(4):
                bank = 2 + (q % 2)
                for k in range(8):
                    P.mm(PB[bank][:, :], sl[:, q // 2, k, (q % 2) * 128:(q % 2) * 128 + 128], mixT[:, k, :], k == 0, k == 7,
                         [sl, mixT], [PB[bank]], nowaw=(k > 0))
                P.cp(ev_eng(), hpb[:, q, 2:514], PB[bank][:, :], [PB[bank]], [hpb], nowaw=True)
            for q in range(4):
                bank = 4 + (q % 2)
                for j in range(3):
                    P.mm(PB[bank][:, :], fdiag[:, chunks[q], j, :], hpb[:, q, j:j + 512], j == 0, j == 2, [fdiag, hpb], [PB[bank]],
                         nowaw=(j > 0))
                if q < 2:
                    P.act(gt[(grp % 2) * 2 + q][:, :], PB[bank][:, :], AF.Silu, [PB[bank]], [gt[(grp % 2) * 2 + q]])
                else:
                    P.tt("vector", aT[:, 2 * grp + q - 2, :], PB[bank][:, :], gt[(grp % 2) * 2 + q - 2][:, :], ALU.mult,
                         [PB[bank], gt[(grp % 2) * 2 + q - 2]], [aT], nowaw=True)
            P.cp("gpsimd", halo[:, grp, :, :], hpb[:, :, 512:514], [hpb], [halo], nowaw=True)
        for idx in range(4):
            half, piece = idx // 2, idx % 2
            sl = wdn[idx % 2]
            for f in range(11):
                fc = piece * 11 + f
                for j in range(4):
                    P.mm(PB[4 + j][:, :], aT[:, fc, j * 128:(j + 1) * 128], sl[:, f, :], fc == 0, fc == 21,
                         [aT, sl], [PB[4 + j]], nowaw=(fc > 0))
            if idx + 2 < 4:
                load_wdn(idx + 2)
            if piece == 1:
                for j in range(4):
                    P.tt("vector", xt[:, j, half * 512:(half + 1) * 512], PB[4 + j][:, :],
                         xt[:, j, half * 512:(half + 1) * 512], ALU.add, [PB[4 + j], xt], [xt])
        rms_to(xt, xt, nwO)
        P.dma("gpsimd", out_d[s, t0:t0 + 512, :].rearrange("(j p) d -> p j d", p=128), xt[:, :, :], [xt], [CONST_OUT])

    P.emit()
    return nc


def host_consts(T):
    identf = np.eye(128, dtype=np.float32)
    identb = identf.astype(ml_dtypes.bfloat16)
    rot = np.zeros((128, 128), np.float32)
    for g in range(2):
        for d in range(32):
            rot[g * 64 + d + 32, g * 64 + d] = -1.0
            rot[g * 64 + d, g * 64 + d + 32] = 1.0
    inv_freq = (10000.0 ** (-np.arange(0, 64, 2, dtype=np.float32) / 64)).astype(np.float32)
    ang = np.arange(T, dtype=np.float32)[None, :] * inv_freq[:, None]
    cos = np.cos(ang).astype(np.float32); sin = np.sin(ang).astype(np.float32)
    rope = np.zeros((128, 2, T), np.float32)
    for r in range(128):
        rope[r, 0] = cos[r % 32]
        rope[r, 1] = sin[r % 32]
    kk = np.arange(128)[:, None]; qq = np.arange(128)[None, :]
    trim = np.where(kk <= qq, 0.0, -30000.0).astype(np.float32).astype(ml_dtypes.bfloat16)
    s = np.arange(64)[:, None]; i = np.arange(64)[None, :]
    cm = np.zeros((64, 5, 64), np.float32)
    cm[:, 0] = (s <= i); cm[:, 1] = (s < i); cm[:, 2] = (s <= i); cm[:, 3] = (s == i); cm[:, 4] = (s > i)
    return {"c_identb": identb, "c_identf": identf, "c_rot": rot.astype(ml_dtypes.bfloat16), "c_rope": rope,
            "c_trimask": trim, "c_masks": cm}


_W1 = ["attn_norm_w", "w_in", "da_lambda_q1", "da_lambda_k1", "da_lambda_q2", "da_lambda_k2", "da_subln_w",
       "gdn_conv_w", "gdn_a_log", "gdn_dt_bias", "gdn_norm_w", "w_out", "ffn_norm_w", "ffn_w_up", "ffn_conv_w",
       "ffn_w_down"]


def make_in_maps(inputs, n_cores, nseq, T):
    consts = host_consts(T)
    base = {k: np.ascontiguousarray(np.asarray(inputs[k], np.float32)[0]) for k in _W1}
    base["final_norm_w"] = np.ascontiguousarray(np.asarray(inputs["final_norm_w"], np.float32))
    base["gdn_conv_w"] = np.ascontiguousarray(base["gdn_conv_w"].reshape(4, 12, 128).transpose(2, 1, 0))
    base["ffn_conv_w"] = np.ascontiguousarray(base["ffn_conv_w"].reshape(3, 44, 128).transpose(2, 1, 0))
    base.update(consts)
    x = np.asarray(inputs["x"], np.float32)
    maps = []
    for c in range(n_cores):
        m = dict(base)
        m["x"] = np.ascontiguousarray(x[c * nseq:(c + 1) * nseq])
        maps.append(m)
    return maps


def kernel(**inputs):
    x = inputs["x"]
    B, T, _ = x.shape
    n = 8
    nseq = B // n
    nc = build(T, nseq)
    maps = make_in_maps(inputs, n, nseq, T)
    res = run_bass_kernel_spmd(nc, maps, core_ids=list(range(n)))
    return np.concatenate([r["out"] for r in res.results], axis=0)
```

```python
import contextlib
import math
import numpy as np
import ml_dtypes
import concourse.bass as bass
import concourse.mybir as mybir
from concourse.bass_utils import run_bass_kernel_spmd

F32 = mybir.dt.float32
BF16 = mybir.dt.bfloat16
AF = mybir.ActivationFunctionType
ALU = mybir.AluOpType
AX = mybir.AxisListType

ENGS = ("sync", "scalar", "gpsimd", "vector", "tensor")
NDMA = 8
EPS = 1e-6
D = 1024
DFF = 2816
INC = 3592
LAMBDA_INIT = 0.8 - 0.6 * math.exp(-0.3 * 0)


class Buf:
    def __init__(self, name, t):
        self.name = name
        self.t = t
        self.writers = []
        self.readers = []
        self.gen_deps = set()
        self.psum = False

    def __getitem__(self, idx):
        return self.t[idx]


class Op:
    __slots__ = ("eng", "fn", "deps", "is_dma", "signal", "tok", "idx")

    def __init__(self, eng, fn, is_dma):
        self.eng = eng
        self.fn = fn
        self.deps = set()
        self.is_dma = is_dma
        self.signal = False
        self.tok = None


class Prog:
    def __init__(self, nc):
        self.nc = nc
        self.ops = []

    def sb(self, name, shape, dt=F32):
        return Buf(name, self.nc.alloc_sbuf_tensor(name, list(shape), dt))

    def ps(self, name, shape, dt=F32):
        b = Buf(name, self.nc.alloc_psum_tensor(name, list(shape), dt))
        b.psum = True
        return b

    def dram(self, name, shape, dt=F32, kind="Internal"):
        return Buf(name, self.nc.dram_tensor(name, list(shape), dt, kind=kind))

    def op(self, eng, fn, reads=(), writes=(), dma=False, nowaw=False):
        o = Op(eng, fn, dma)
        o.idx = len(self.ops)
        deps = o.deps
        for r in reads:
            deps.update(r.writers)
            if r.psum:
                for ri in r.readers:
                    if self.ops[ri].eng != eng:
                        deps.add(ri)
        for w in writes:
            if nowaw and w.writers:
                deps.update(w.gen_deps)
                deps.update(w.readers)
                w.gen_deps.update(w.readers)
                w.writers.append(o.idx)
                w.readers = []
            else:
                g = set(w.writers) | set(w.readers)
                deps.update(g)
                w.gen_deps = g
                w.writers = [o.idx]
                w.readers = []
        for r in reads:
            r.readers.append(o.idx)
        deps.discard(o.idx)
        self.ops.append(o)
        return o

    def dma(self, eng, out_ap, in_ap, reads, writes, nowaw=False):
        return self.op(eng, lambda e: e.dma_start(out=out_ap, in_=in_ap), reads, writes, dma=True, nowaw=nowaw)

    def mm(self, out_ap, lhsT, rhs, start, stop, reads, writes, nowaw=False):
        return self.op("tensor", lambda e: e.matmul(out_ap, lhsT=lhsT, rhs=rhs, start=start, stop=stop),
                       reads, writes, nowaw=nowaw)

    def tr(self, out_ap, in_ap, ident, reads, writes, nowaw=False):
        return self.op("tensor", lambda e: e.transpose(out_ap, in_ap, ident), reads, writes, nowaw=nowaw)

    def act(self, out_ap, in_ap, func, reads, writes, scale=1.0, bias=None, accum=None, eng="scalar", nowaw=False):
        def f(e):
            kw = dict(out=out_ap, in_=in_ap, func=func, scale=scale)
            if bias is not None:
                kw["bias"] = bias
            if accum is not None:
                kw["accum_out"] = accum
            return e.activation(**kw)
        return self.op(eng, f, reads, writes, nowaw=nowaw)

    def cp(self, eng, out_ap, in_ap, reads, writes, nowaw=False):
        if eng == "scalar":
            return self.op(eng, lambda e: e.copy(out=out_ap, in_=in_ap), reads, writes, nowaw=nowaw)
        return self.op(eng, lambda e: e.tensor_copy(out=out_ap, in_=in_ap), reads, writes, nowaw=nowaw)

    def tt(self, eng, out_ap, in0, in1, op, reads, writes, nowaw=False):
        return self.op(eng, lambda e: e.tensor_tensor(out=out_ap, in0=in0, in1=in1, op=op), reads, writes, nowaw=nowaw)

    def ts(self, eng, out_ap, in0, s1, s2, op0, op1, reads, writes, nowaw=False):
        if s2 is None:
            return self.op(eng, lambda e: e.tensor_scalar(out=out_ap, in0=in0, scalar1=s1, scalar2=None, op0=op0),
                           reads, writes, nowaw=nowaw)
        return self.op(eng, lambda e: e.tensor_scalar(out=out_ap, in0=in0, scalar1=s1, scalar2=s2, op0=op0, op1=op1),
                       reads, writes, nowaw=nowaw)

    def stt(self, eng, out_ap, in0, scalar, in1, op0, op1, reads, writes, nowaw=False):
        return self.op(eng, lambda e: e.scalar_tensor_tensor(out=out_ap, in0=in0, scalar=scalar, in1=in1, op0=op0, op1=op1),
                       reads, writes, nowaw=nowaw)

    def memset(self, eng, ap, val, writes, nowaw=False):
        return self.op(eng, lambda e: e.memset(ap, val), [], writes, nowaw=nowaw)

    def emit(self, final_wait_eng="sync"):
        nc = self.nc
        import os
        kcut = int(os.environ.get("KCUT", "0"))
        if kcut:
            self.ops = self.ops[:kcut]
        ops = self.ops
        print("emit: n_ops =", len(ops), flush=True)
        for o in ops:
            for d in o.deps:
                od = ops[d]
                if od.eng == "tensor" and o.eng == "tensor" and not od.is_dma and not o.is_dma:
                    continue
                od.signal = True
            if o.is_dma:
                o.signal = True
        with contextlib.ExitStack() as st:
            esem = {e: st.enter_context(nc.semaphore("s_" + e)) for e in ENGS}
            dsem = {e: [st.enter_context(nc.semaphore("d_%s%d" % (e, i))) for i in range(NDMA)]
                    for e in ("sync", "scalar", "gpsimd")}
            ecount = {e: 0 for e in ENGS}
            dcount = {e: [0] * NDMA for e in dsem}
            drot = {e: 0 for e in dsem}
            prewait = {}
            for o in ops:
                if not o.signal:
                    continue
                if o.is_dma:
                    i = drot[o.eng]
                    drot[o.eng] = (i + 1) % NDMA
                    prewait[o.idx] = (dsem[o.eng][i], dcount[o.eng][i])
                    dcount[o.eng][i] += 16
                    o.tok = (dsem[o.eng][i], dcount[o.eng][i], 16)
                else:
                    ecount[o.eng] += 1
                    o.tok = (esem[o.eng], ecount[o.eng], 1)
            by_eng = {e: [o for o in ops if o.eng == e] for e in ENGS}
            block = st.enter_context(nc.Block())

            def run(e, eng):
                known = {}
                for o in by_eng[e]:
                    waits = {}
                    for d in o.deps:
                        od = ops[d]
                        if od.tok is None:
                            continue
                        if od.eng == "tensor" and e == "tensor" and not od.is_dma and not o.is_dma:
                            continue
                        s, v, _ = od.tok
                        k = id(s)
                        if known.get(k, 0) >= v:
                            continue
                        if k not in waits or waits[k][1] < v:
                            waits[k] = (s, v)
                    if o.idx in prewait:
                        s, v = prewait[o.idx]
                        k = id(s)
                        if v > 0 and known.get(k, 0) < v and (k not in waits or waits[k][1] < v):
                            waits[k] = (s, v)
                    for k, (s, v) in waits.items():
                        eng.wait_ge(s, v)
                        known[k] = v
                    ins = o.fn(eng)
                    if o.tok is not None:
                        ins.then_inc(o.tok[0], o.tok[2])
                if e == final_wait_eng:
                    for q in dsem:
                        for i in range(NDMA):
                            if dcount[q][i] > 0:
                                eng.wait_ge(dsem[q][i], dcount[q][i])

            @block.sync
            def _(eng):
                run("sync", eng)

            @block.scalar
            def _(eng):
                run("scalar", eng)

            @block.gpsimd
            def _(eng):
                run("gpsimd", eng)

            @block.vector
            def _(eng):
                run("vector", eng)

            @block.tensor
            def _(eng):
                run("tensor", eng)


def build(T, NSEQ, debug=False, do_c=True, stop_after=None):
    nc = bass.Bass("TRN2", target_bir_lowering=False)
    P = Prog(nc)
    NT = T // 512
    NKT = T // 128
    dk = "ExternalOutput"

    def din(name, shape, dt=F32):
        return P.dram(name, shape, dt, kind="ExternalInput")

    x_d = din("x", [NSEQ, T, D])
    anw_d = din("attn_norm_w", [D])
    win_d = din("w_in", [D, INC])
    lq1_d = din("da_lambda_q1", [64]); lk1_d = din("da_lambda_k1", [64])
    lq2_d = din("da_lambda_q2", [64]); lk2_d = din("da_lambda_k2", [64])
    subln_d = din("da_subln_w", [128])
    gcw_d = din("gdn_conv_w", [128, 12, 4])
    alog_d = din("gdn_a_log", [4]); dtb_d = din("gdn_dt_bias", [4])
    gnw_d = din("gdn_norm_w", [128])
    wout_d = din("w_out", [D, D])
    fnw_d = din("ffn_norm_w", [D])
    wup_d = din("ffn_w_up", [D, 2 * DFF])
    fcw_d = din("ffn_conv_w", [128, 44, 3])
    wdn_d = din("ffn_w_down", [DFF, D])
    finw_d = din("final_norm_w", [D])
    identb_d = din("c_identb", [128, 128], BF16)
    identf_d = din("c_identf", [128, 128])
    rot_d = din("c_rot", [128, 128], BF16)
    rope_d = din("c_rope", [128, 2, T])
    trim_d = din("c_trimask", [128, 128], BF16)
    cm_d = din("c_masks", [64, 5, 64])
    out_d = P.dram("out", [NSEQ, T, D], F32, kind="ExternalOutput")

    qT_d = P.dram("s_qT", [NSEQ, 4, 128, T], BF16, kind=dk)
    kT_d = P.dram("s_kT", [NSEQ, 4, 128, T], BF16, kind=dk)
    vda_d = P.dram("s_vda", [NSEQ, T, 512], BF16, kind=dk)
    gqT_d = P.dram("s_gqT", [NSEQ, 4, 128, T], BF16, kind=dk)
    gkT_d = P.dram("s_gkT", [NSEQ, 4, 128, T], BF16, kind=dk)
    gkn_d = P.dram("s_gkn", [NSEQ, T, 512], BF16, kind=dk)
    gv_d = P.dram("s_gv", [NSEQ, T, 512], BF16, kind=dk)
    gz_d = P.dram("s_gz", [NSEQ, T, 512], BF16, kind=dk)
    gsc_d = P.dram("s_gsc", [NSEQ, T, 16], F32, kind=dk)
    mix_d = P.dram("s_mix", [NSEQ, T, D], BF16, kind=dk)
    wupb_d = P.dram("s_wupb", [D, 2 * DFF], BF16)
    wdnb_d = P.dram("s_wdnb", [DFF, D], BF16)
    def units(n):
        return [Buf("u", None) for _ in range(n)]
    u_qT = units(NSEQ * NT); u_kT = units(NSEQ * NT); u_vda = units(NSEQ * NT)
    u_gqT = units(NSEQ * NT); u_gkT = units(NSEQ * NT); u_gkn = units(NSEQ * NT)
    u_gv = units(NSEQ * NT); u_gz = units(NSEQ * NT); u_gsc = units(NSEQ * NT)
    u_mixa = units(NSEQ * NT); u_mixg = units(NSEQ * NT)
    u_wupb = Buf("u", None); u_wdnb = Buf("u", None)
    CONST = Buf("const_in", None)
    CONST_OUT = Buf("const_out", None)

    PB = [P.ps("pb%d" % i, [128, 512]) for i in range(8)]

    def pbf(i):
        return PB[i].t[:, :].bitcast(BF16)

    ARENA_BYTES = 196 * 1024
    arena = nc.alloc_sbuf_tensor("arena", [128, ARENA_BYTES // 4], F32)
    live = []
    cur = [0]

    def aalloc(name, free_shape, dt):
        n = int(np.prod(free_shape))
        nb = n * (2 if dt == BF16 else 4)
        nb = (nb + 63) // 64 * 64
        s = cur[0]
        e = s + nb
        assert e <= ARENA_BYTES, (name, e)
        cur[0] = e
        ap = arena[:, s // 4:e // 4]
        if dt == BF16:
            ap = ap.bitcast(BF16)
        ap = ap[:, 0:n]
        if len(free_shape) == 2:
            ap = ap.rearrange("p (a b) -> p a b", a=free_shape[0])
        elif len(free_shape) == 3:
            ap = ap.rearrange("p (a b c) -> p a b c", a=free_shape[0], b=free_shape[1])
        elif len(free_shape) == 4:
            ap = ap.rearrange("p (a b c d) -> p a b c d", a=free_shape[0], b=free_shape[1], c=free_shape[2])
        b = Buf(name, ap)
        for (s0, e0, ob) in live:
            if s0 < e and s < e0:
                b.readers.extend(ob.writers)
                b.readers.extend(ob.readers)
        live.append((s, e, b))
        return b

    def areset(mark=0):
        cur[0] = mark

    identb = P.sb("identb", [128, 128], BF16)
    identf = P.sb("identf", [128, 128])
    rotm = P.sb("rotm", [128, 128], BF16)
    trim = P.sb("trim", [128, 128], BF16)
    cm = P.sb("cm", [64, 5, 64])
    negh = P.sb("negh", [128, 64])
    ones_f = P.sb("ones_f", [64, 128])
    P.dma("sync", identb[:, :], identb_d[:, :], [CONST], [identb])
    P.dma("sync", identf[:, :], identf_d[:, :], [CONST], [identf])
    P.dma("sync", rotm[:, :], rot_d[:, :], [CONST], [rotm])
    P.dma("sync", trim[:, :], trim_d[:, :], [CONST], [trim])
    P.dma("sync", cm[:, :, :], cm_d[:, :, :], [CONST], [cm])
    P.memset("vector", negh[:, :], -0.5, [negh])
    P.memset("vector", ones_f[:, :], 1.0, [ones_f])

    def bc(d_buf, n):
        return d_buf.t.ap().partition_broadcast(n)

    def rsqrt_(out_ap, in_ap, scale, nh_ap, reads, writes):
        P.ts("vector", out_ap, in_ap, scale, EPS, ALU.mult, ALU.add, reads, writes)
        P.tt("gpsimd", out_ap, out_ap, nh_ap, ALU.pow, list(writes) + [negh], writes)

    areset(0)
    Win = aalloc("Win", [8, INC], BF16)
    markW = cur[0]
    stg = [aalloc("stg%d" % i, [3592], F32) for i in range(2)]
    stgb = [aalloc("stgb%d" % i, [3592], BF16) for i in range(2)]
    ci = [0]
    cast_engs = ["vector", "gpsimd"]

    def cast_to_dram(src_ap, dst_ap, ncols, unit):
        i = ci[0] % 2
        ci[0] += 1
        P.dma("sync", stg[i][:, 0:ncols], src_ap, [CONST], [stg[i]])
        P.cp(cast_engs[i], stgb[i][:, 0:ncols], stg[i][:, 0:ncols], [stg[i]], [stgb[i]])
        P.dma("sync", dst_ap, stgb[i][:, 0:ncols], [stgb[i]], [unit], nowaw=True)

    for k in range(8):
        for hf in range(2):
            cast_to_dram(wup_d[k * 128:(k + 1) * 128, hf * DFF:(hf + 1) * DFF],
                         wupb_d[k * 128:(k + 1) * 128, hf * DFF:(hf + 1) * DFF], DFF, u_wupb)
    for k in range(22):
        cast_to_dram(wdn_d[k * 128:(k + 1) * 128, :], wdnb_d[k * 128:(k + 1) * 128, :], D, u_wdnb)

    for k in range(8):
        i = ci[0] % 2
        ci[0] += 1
        P.dma("sync", stg[i][:, 0:INC], win_d[k * 128:(k + 1) * 128, :], [CONST], [stg[i]])
        P.cp(cast_engs[i], Win[:, k, :], stg[i][:, 0:INC], [stg[i]], [Win], nowaw=True)
    areset(markW)
    xbuf = [aalloc("xbuf%d" % i, [4, D], F32) for i in range(2)]
    rpbuf = [aalloc("rp%d" % i, [2, 512], F32) for i in range(2)]
    xn = aalloc("xn", [4, D], BF16)
    xnT = aalloc("xnT", [8, 512], BF16)
    nwA = aalloc("nwA", [D], F32)
    junk = aalloc("junk", [D], BF16)
    ssq = aalloc("ssq", [4], F32)
    rstd = aalloc("rstd", [4], F32)
    xb = [aalloc("xb%d" % i, [512], BF16) for i in range(2)]
    t1 = [aalloc("t1_%d" % i, [512], F32) for i in range(2)]
    t2 = [aalloc("t2_%d" % i, [512], F32) for i in range(2)]
    ro = [aalloc("ro%d" % i, [512], BF16) for i in range(3)]
    gh = aalloc("gh", [12, 515], BF16)
    gdiag = aalloc("gdiag", [12, 4, 128], BF16)
    gcw = aalloc("gcw", [12, 4], F32)
    gs = [aalloc("gs%d" % i, [512], BF16) for i in range(3)]
    knt = [aalloc("knt%d" % i, [128], BF16) for i in range(2)]
    knTt = [aalloc("knTt%d" % i, [512], BF16) for i in range(2)]
    kn_t = aalloc("kn_t", [4, 512], BF16)
    v_t = aalloc("v_t", [4, 512], BF16)
    va_t = aalloc("va_t", [4, 512], BF16)
    z_t = aalloc("z_t", [4, 512], BF16)
    ssqk = aalloc("ssqk", [4, 8], F32)
    rk1 = aalloc("rk1", [4, 8], F32)
    gsc_t = aalloc("gsc_t", [4, 16], F32)
    ba_t = aalloc("ba_t", [4, 8], F32)
    tmp4 = aalloc("tmp4", [4, 4], F32)
    dtbB = aalloc("dtbB", [4], F32)
    negA = aalloc("negA", [4], F32)
    markA = cur[0]

    P.dma("sync", nwA[:, :], bc(anw_d, 128), [CONST], [nwA])
    P.dma("sync", gcw[:, :, :], gcw_d[:, :, :], [CONST], [gcw])
    P.dma("sync", dtbB[:, :], bc(dtb_d, 128), [CONST], [dtbB])
    P.dma("sync", negA[:, :], bc(alog_d, 128), [CONST], [negA])
    P.act(negA[:, :], negA[:, :], AF.Exp, [negA], [negA])
    P.ts("vector", negA[:, :], negA[:, :], -1.0, None, ALU.mult, None, [negA], [negA])
    for c in range(12):
        for j in range(4):
            P.ts("gpsimd", gdiag[:, c, j, :], identb[:, :], gcw[:, c, j:j + 1], None, ALU.mult, None,
                 [identb, gcw], [gdiag], nowaw=True)

    tiles = [(s, tt) for s in range(NSEQ) for tt in range(NT)]

    def loadA(g):
        s, tt = tiles[g]
        t0 = tt * 512
        P.dma("sync", xbuf[g % 2][:, :, :], x_d[s, t0:t0 + 512, :].rearrange("(j p) d -> p j d", p=128),
              [CONST], [xbuf[g % 2]])
        P.dma("sync", rpbuf[g % 2][:, :, :], rope_d[:, :, t0:t0 + 512], [CONST], [rpbuf[g % 2]])

    evi = [0]

    def ev_eng():
        evi[0] += 1
        return "vector" if evi[0] % 2 else "scalar"

    loadA(0)
    for g, (s, tt) in enumerate(tiles):
        t0 = tt * 512
        ug = s * NT + tt
        if g + 1 < len(tiles):
            loadA(g + 1)
        xt = xbuf[g % 2]
        rp = rpbuf[g % 2]
        P.memset("vector", ssq[:, :], 0.0, [ssq])
        for j in range(4):
            P.act(junk[:, :], xt[:, j, :], AF.Square, [xt, ssq], [junk, ssq], accum=ssq[:, j:j + 1], nowaw=True)
        rsqrt_(rstd[:, :], ssq[:, :], 1.0 / D, negh[:, 0:4], [ssq], [rstd])
        for j in range(4):
            P.stt("vector", xn[:, j, :], xt[:, j, :], rstd[:, j:j + 1], nwA[:, :], ALU.mult, ALU.mult,
                  [xt, rstd, nwA], [xn], nowaw=True)
        for k in range(8):
            bk = k % 2
            for j in range(4):
                P.tr(pbf(bk)[:, j * 128:(j + 1) * 128], xn[:, j, k * 128:(k + 1) * 128], identb[:, :],
                     [xn, identb], [PB[bk]], nowaw=(j > 0))
            P.cp(ev_eng(), xnT[:, k, :], pbf(bk)[:, 0:512], [PB[bk]], [xnT], nowaw=True)

        def proj_fm(c0, bank):
            for k in range(8):
                P.mm(PB[bank][:, :], Win[:, k, c0:c0 + 128], xnT[:, k, :], k == 0, k == 7, [Win, xnT], [PB[bank]],
                     nowaw=(k > 0))

        def projqk(c):
            bank = 2 + (c % 2)
            proj_fm(c * 128, bank)
            i2 = c % 2
            P.cp("scalar", xb[i2][:, :], PB[bank][:, :], [PB[bank]], [xb[i2]])
            P.tt("vector", t1[i2][:, :], PB[bank][:, :], rp[:, 0, :], ALU.mult, [PB[bank], rp], [t1[i2]])

        def ropeqk(c):
            i2 = c % 2
            rb = 4 + (c % 2)
            P.mm(PB[rb][:, :], rotm[:, :], xb[i2][:, :], True, True, [rotm, xb[i2]], [PB[rb]])
            P.tt("vector", t2[i2][:, :], PB[rb][:, :], rp[:, 1, :], ALU.mult, [PB[rb], rp], [t2[i2]])
            r3 = ro[c % 3]
            P.tt("gpsimd", r3[:, :], t1[i2][:, :], t2[i2][:, :], ALU.add, [t1[i2], t2[i2]], [r3])
            if c < 4:
                P.dma("gpsimd", qT_d[s, c, :, t0:t0 + 512], r3[:, :], [r3], [u_qT[ug]], nowaw=True)
            else:
                P.dma("gpsimd", kT_d[s, c - 4, :, t0:t0 + 512], r3[:, :], [r3], [u_kT[ug]], nowaw=True)

        projqk(0)
        for c in range(8):
            if c + 1 < 8:
                projqk(c + 1)
            ropeqk(c)
        if tt == 0:
            P.memset("vector", gh[:, :, 0:3], 0.0, [gh])
        P.memset("vector", ssqk[:, :, :], 0.0, [ssqk])
        for c in range(12):
            bank = 2 + (c % 2)
            proj_fm(1536 + c * 128, bank)
            P.cp("scalar", gh[:, c, 3:515], PB[bank][:, :], [PB[bank]], [gh], nowaw=True)
        def gconv(c):
            bank = 4 + (c % 2)
            for j in range(4):
                P.mm(PB[bank][:, :], gdiag[:, c, j, :], gh[:, c, j:j + 512], j == 0, j == 3, [gdiag, gh], [PB[bank]],
                     nowaw=(j > 0))
            P.act(gs[c % 3][:, :], PB[bank][:, :], AF.Silu, [PB[bank]], [gs[c % 3]])

        gconv(0)
        for c in range(12):
            if c + 1 < 12:
                gconv(c + 1)
            g3 = gs[c % 3]
            h = c % 4
            if c < 4:
                P.dma("gpsimd", gqT_d[s, h, :, t0:t0 + 512], g3[:, :], [g3], [u_gqT[ug]], nowaw=True)
                tb = 6 + (c % 2)
                for i in range(4):
                    P.tr(pbf(tb)[:, i * 128:(i + 1) * 128], g3[:, i * 128:(i + 1) * 128], identb[:, :],
                         [g3, identb], [PB[tb]], nowaw=(i > 0))
                for i in range(4):
                    P.act(junk[:, 0:128], pbf(tb)[:, i * 128:(i + 1) * 128], AF.Square, [PB[tb], ssqk], [junk, ssqk],
                          accum=ssqk[:, i, 4 + h:5 + h], nowaw=True)
            elif c < 8:
                tb = 6 + (c % 2)
                for i in range(4):
                    P.tr(pbf(tb)[:, i * 128:(i + 1) * 128], g3[:, i * 128:(i + 1) * 128], identb[:, :],
                         [g3, identb], [PB[tb]], nowaw=(i > 0))
                for i in range(4):
                    P.act(junk[:, 0:128], pbf(tb)[:, i * 128:(i + 1) * 128], AF.Square, [PB[tb], ssqk], [junk, ssqk],
                          accum=ssqk[:, i, h:h + 1], nowaw=True)
                rsqrt_(rk1[:, :, h:h + 1], ssqk[:, :, h:h + 1], 1.0, negh[:, 0:4].unsqueeze(2), [ssqk], [rk1])
                for i in range(4):
                    P.ts("vector", kn_t[:, i, h * 128:(h + 1) * 128], pbf(tb)[:, i * 128:(i + 1) * 128],
                         rk1[:, i, h:h + 1], None, ALU.mult, None, [PB[tb], rk1], [kn_t], nowaw=True)
                kT_ = knTt[c % 2]
                tb2 = 2 + (c % 2)
                for i in range(4):
                    P.tr(pbf(tb2)[:, i * 128:(i + 1) * 128], kn_t[:, i, h * 128:(h + 1) * 128], identb[:, :],
                         [kn_t, identb], [PB[tb2]], nowaw=(i > 0))
                P.cp("vector", kT_[:, :], pbf(tb2)[:, 0:512], [PB[tb2]], [kT_])
                P.dma("gpsimd", gkT_d[s, h, :, t0:t0 + 512], kT_[:, :], [kT_], [u_gkT[ug]], nowaw=True)
            else:
                tb = 6 + (c % 2)
                for i in range(4):
                    P.tr(pbf(tb)[:, i * 128:(i + 1) * 128], g3[:, i * 128:(i + 1) * 128], identb[:, :],
                         [g3, identb], [PB[tb]], nowaw=(i > 0))
                P.cp("vector", v_t[:, :, h * 128:(h + 1) * 128],
                     pbf(tb)[:, 0:512].rearrange("p (i d) -> p i d", i=4), [PB[tb]], [v_t], nowaw=True)
        P.cp("gpsimd", gh[:, :, 0:3], gh[:, :, 512:515], [gh], [gh])
        P.dma("gpsimd", gkn_d[s, t0:t0 + 512, :].rearrange("(j p) f -> p j f", p=128), kn_t[:, :, :], [kn_t], [u_gkn[ug]])
        P.dma("gpsimd", gv_d[s, t0:t0 + 512, :].rearrange("(j p) f -> p j f", p=128), v_t[:, :, :], [v_t], [u_gv[ug]])
        for i in range(4):
            bank = 2 + (i % 2)
            for k in range(8):
                P.mm(PB[bank][:, :], xnT[:, k, i * 128:(i + 1) * 128], Win[:, k, 1024:1536], k == 0, k == 7,
                     [Win, xnT], [PB[bank]], nowaw=(k > 0))
            P.cp(ev_eng(), va_t[:, i, :], PB[bank][:, :], [PB[bank]], [va_t], nowaw=True)
            bank = 4 + (i % 2)
            for k in range(8):
                P.mm(PB[bank][:, :], xnT[:, k, i * 128:(i + 1) * 128], Win[:, k, 3072:3584], k == 0, k == 7,
                     [Win, xnT], [PB[bank]], nowaw=(k > 0))
            P.act(z_t[:, i, :], PB[bank][:, :], AF.Silu, [PB[bank]], [z_t], nowaw=True)
        for i in range(4):
            for k in range(8):
                P.mm(PB[6][:, i * 8:(i + 1) * 8], xnT[:, k, i * 128:(i + 1) * 128], Win[:, k, 3584:3592], k == 0, k == 7,
                     [Win, xnT], [PB[6]], nowaw=not (i == 0 and k == 0))
        P.cp("vector", ba_t[:, :, :], PB[6][:, 0:32].rearrange("p (i e) -> p i e", i=4), [PB[6]], [ba_t])
        P.dma("gpsimd", vda_d[s, t0:t0 + 512, :].rearrange("(j p) f -> p j f", p=128), va_t[:, :, :], [va_t], [u_vda[ug]])
        P.dma("gpsimd", gz_d[s, t0:t0 + 512, :].rearrange("(j p) f -> p j f", p=128), z_t[:, :, :], [z_t], [u_gz[ug]])
        rsqrt_(gsc_t[:, :, 4:8], ssqk[:, :, 4:8], 1.0, negh[:, 0:16].rearrange("p (a b) -> p a b", a=4), [ssqk], [gsc_t])
        P.ts("vector", gsc_t[:, :, 4:8], gsc_t[:, :, 4:8], 128.0 ** -0.5, None, ALU.mult, None, [gsc_t], [gsc_t])
        P.cp("vector", gsc_t[:, :, 0:4], rk1[:, :, 0:4], [rk1], [gsc_t])
        P.act(gsc_t[:, :, 8:12], ba_t[:, :, 0:4], AF.Sigmoid, [ba_t], [gsc_t])
        P.tt("vector", tmp4[:, :, :], ba_t[:, :, 4:8], dtbB[:, :].unsqueeze(1).to_broadcast([128, 4, 4]), ALU.add,
             [ba_t, dtbB], [tmp4])
        P.act(tmp4[:, :, :], tmp4[:, :, :], AF.Exp, [tmp4], [tmp4])
        P.act(tmp4[:, :, :], tmp4[:, :, :], AF.Ln, [tmp4], [tmp4], bias=1.0)
        P.tt("vector", gsc_t[:, :, 12:16], tmp4[:, :, :], negA[:, :].unsqueeze(1).to_broadcast([128, 4, 4]), ALU.mult,
             [tmp4, negA], [gsc_t])
        P.dma("gpsimd", gsc_d[s, t0:t0 + 512, :].rearrange("(j p) f -> p j f", p=128), gsc_t[:, :, :], [gsc_t], [u_gsc[ug]])


    if stop_after == "A":
        P.emit()
        return nc

    def phase_c():
        areset(0)
        knT_c = aalloc("c_knT", [4, 512], BF16)
        kn_c = aalloc("c_kn", [8, 512], BF16)
        v_c = aalloc("c_v", [8, 512], BF16)
        sc_c = aalloc("c_sc", [8, 16], F32)
        g_c = aalloc("c_g", [32], F32)
        gcum = aalloc("c_gcum", [32], F32)
        egc = aalloc("c_egc", [32], F32)
        kdc = aalloc("c_kdc", [32], F32)
        beta_c = aalloc("c_beta", [32], F32)
        gU = aalloc("c_gU", [8, 64], F32)
        Gt = aalloc("c_Gt", [8, 64], F32)
        tSU = aalloc("c_tSU", [8, 64], F32)
        tU = aalloc("c_tU", [8, 64], F32)
        Pm = [aalloc("c_P%d" % i, [8, 64], F32) for i in range(2)]
        PTm = [aalloc("c_PT%d" % i, [8, 64], F32) for i in range(2)]
        Am = aalloc("c_A", [8, 64], F32)
        Wb = aalloc("c_Wb", [8, 64], BF16)
        kg_b = aalloc("c_kg", [8, 128], BF16)
        qT_c2 = [aalloc("c_qT%d" % i, [4, 512], BF16) for i in range(2)]
        z_c2 = [aalloc("c_z%d" % i, [8, 512], BF16) for i in range(2)]
        wT_all2 = [aalloc("c_wT%d" % i, [32, 64], BF16) for i in range(2)]
        ub_all2 = [aalloc("c_ub%d" % i, [32, 128], F32) for i in range(2)]
        Aq_all2 = [aalloc("c_Aq%d" % i, [32, 64], BF16) for i in range(2)]
        kdec_all2 = [aalloc("c_kdec%d" % i, [32, 128], BF16) for i in range(2)]
        egl2 = [aalloc("c_egl%d" % i, [32], F32) for i in range(2)]
        rq_c2 = [aalloc("c_rq%d" % i, [32], F32) for i in range(2)]
        rqe2 = [aalloc("c_rqe%d" % i, [32], F32) for i in range(2)]
        nbeta2 = [aalloc("c_nbeta%d" % i, [32], F32) for i in range(2)]
        S = [aalloc("c_S%d" % i, [128], F32) for i in range(4)]
        Sb = [aalloc("c_Sb%d" % i, [128], BF16) for i in range(4)]
        vnew = [aalloc("c_vn%d" % i, [128], BF16) for i in range(4)]
        t1c = [aalloc("c_t1%d" % i, [128], F32) for i in range(4)]
        obuf = aalloc("c_o", [8, 512], F32)
        osq = aalloc("c_osq", [8, 512], F32)
        ssqg = aalloc("c_ssqg", [32], F32)
        rstdg = aalloc("c_rstdg", [32], F32)
        mixg = aalloc("c_mixg", [8, 512], BF16)
        gnwB = aalloc("c_gnwB", [128], F32)
        P.dma("sync", gnwB[:, :], bc(gnw_d, 128), [CONST], [gnwB])
        H = slice(0, 64)

        def b3(ap2, n):
            return ap2.unsqueeze(2).to_broadcast([64, 8, n])

        def pre(g):
            s, tt = tiles[g]
            t0 = tt * 512
            ug = s * NT + tt
            qT_c, z_c, wT_all, ub_all = qT_c2[g % 2], z_c2[g % 2], wT_all2[g % 2], ub_all2[g % 2]
            Aq_all, kdec_all, egl, rq_c, rqe, nbeta = (Aq_all2[g % 2], kdec_all2[g % 2], egl2[g % 2], rq_c2[g % 2],
                                                       rqe2[g % 2], nbeta2[g % 2])
            P.dma("sync", knT_c[:, :, :], gkT_d[s, :, :, t0:t0 + 512].rearrange("h p t -> p h t"), [u_gkT[ug]], [knT_c])
            P.dma("sync", qT_c[:, :, :], gqT_d[s, :, :, t0:t0 + 512].rearrange("h p t -> p h t"), [u_gqT[ug]], [qT_c])
            P.dma("sync", kn_c[H, :, :], gkn_d[s, t0:t0 + 512, :].rearrange("(n c) f -> c n f", c=64), [u_gkn[ug]], [kn_c])
            P.dma("sync", v_c[H, :, :], gv_d[s, t0:t0 + 512, :].rearrange("(n c) f -> c n f", c=64), [u_gv[ug]], [v_c])
            P.dma("sync", z_c[H, :, :], gz_d[s, t0:t0 + 512, :].rearrange("(n c) f -> c n f", c=64), [u_gz[ug]], [z_c])
            P.dma("sync", sc_c[H, :, :], gsc_d[s, t0:t0 + 512, :].rearrange("(n c) f -> c n f", c=64), [u_gsc[ug]], [sc_c])
            yield
            g3 = g_c[H, :].rearrange("p (n h) -> p n h", n=8)
            P.cp("vector", g3, sc_c[H, :, 12:16], [sc_c], [g_c])
            P.cp("vector", beta_c[H, :].rearrange("p (n h) -> p n h", n=8), sc_c[H, :, 8:12], [sc_c], [beta_c])
            P.cp("vector", rq_c[H, :].rearrange("p (n h) -> p n h", n=8), sc_c[H, :, 4:8], [sc_c], [rq_c])
            P.ts("vector", nbeta[H, :], beta_c[H, :], -1.0, None, ALU.mult, None, [beta_c], [nbeta])
            yield
            P.mm(PB[0][H, 0:32], cm[:, 0, :], g_c[H, :], True, True, [cm, g_c], [PB[0]])
            P.mm(PB[1][:, 0:32], ones_f[:, :], g_c[H, :], True, True, [ones_f, g_c], [PB[1]])
            P.cp("vector", gcum[H, :], PB[0][H, 0:32], [PB[0]], [gcum])
            P.act(egc[H, :], PB[0][H, 0:32], AF.Exp, [PB[0]], [egc])
            yield
            P.act(egl[:, :], PB[1][:, 0:32], AF.Exp, [PB[1]], [egl])
            P.tt("vector", kdc[H, :], PB[1][H, 0:32], gcum[H, :], ALU.subtract, [PB[1], gcum], [kdc])
            P.act(kdc[H, :], kdc[H, :], AF.Exp, [kdc], [kdc])
            P.tt("vector", rqe[H, :], rq_c[H, :], egc[H, :], ALU.mult, [rq_c, egc], [rqe])
            yield
            for bb in range(4):
                ps = slice(bb * 8, bb * 8 + 8)
                pairs = [(2 * bb + q // 4, q % 4) for q in range(8)]
                P.tt("vector", gU[H, :, :], cm[:, 0, :].unsqueeze(1).to_broadcast([64, 8, 64]), b3(g_c[H, ps], 64), ALU.mult,
                     [cm, g_c], [gU])
                for q, (n, h) in enumerate(pairs):
                    P.mm(PB[0][H, q * 64:(q + 1) * 64], cm[:, 4, :], gU[H, q, :], True, True, [cm, gU], [PB[0]], nowaw=(q > 0))
                yield
                P.act(Gt[H, :, :], PB[0][H, :].rearrange("p (q i) -> p q i", q=8), AF.Exp, [PB[0]], [Gt])
                for q, (n, h) in enumerate(pairs):
                    ks = knT_c[:, h, n * 64:(n + 1) * 64]
                    P.mm(PB[1][H, q * 64:(q + 1) * 64], ks, ks, True, True, [knT_c], [PB[1]], nowaw=(q > 0))
                yield
                for q, (n, h) in enumerate(pairs):
                    ks = knT_c[:, h, n * 64:(n + 1) * 64]
                    P.mm(PB[2][H, q * 64:(q + 1) * 64], ks, qT_c[:, h, n * 64:(n + 1) * 64], True, True, [knT_c, qT_c], [PB[2]],
                         nowaw=(q > 0))
                yield
                m8 = lambda i: cm[:, i, :].unsqueeze(1).to_broadcast([64, 8, 64])
                P.tt("gpsimd", tSU[H, :, :], Gt[H, :, :], m8(1), ALU.mult, [Gt, cm], [tSU])
                P.stt("vector", tSU[H, :, :], tSU[H, :, :], -1.0, b3(beta_c[H, ps], 64), ALU.mult, ALU.mult, [tSU, beta_c], [tSU])
                P.tt("gpsimd", tU[H, :, :], Gt[H, :, :], m8(2), ALU.mult, [Gt, cm], [tU])
                yield
                P0 = Pm[0]
                P.tt("vector", P0[H, :, :], PB[1][H, :].rearrange("p (q i) -> p q i", q=8), tSU[H, :, :], ALU.mult,
                     [PB[1], tSU], [P0])
                P.tt("vector", Aq_all[H, ps, :], PB[2][H, :].rearrange("p (q i) -> p q i", q=8), tU[H, :, :], ALU.mult,
                     [PB[2], tU], [Aq_all], nowaw=(bb > 0))
                yield
                for q in range(8):
                    P.tr(PB[3][H, q * 64:(q + 1) * 64], P0[H, q, :], identf[H, H], [P0, identf], [PB[3]], nowaw=(q > 0))
                P.cp("scalar", PTm[0][H, :, :], PB[3][H, :].rearrange("p (q i) -> p q i", q=8), [PB[3]], [PTm[0]])
                P.tt("gpsimd", Am[H, :, :], P0[H, :, :], m8(3), ALU.add, [P0, cm], [Am])
                yield
                for m in range(5):
                    Pc, PTc = Pm[m % 2], PTm[m % 2]
                    Pn, PTn = Pm[(m + 1) % 2], PTm[(m + 1) % 2]
                    if m < 4:
                        for q in range(8):
                            P.mm(PB[1][H, q * 64:(q + 1) * 64], PTc[H, q, :], Pc[H, q, :], True, True, [PTc, Pc], [PB[1]],
                                 nowaw=(q > 0))
                        yield
                    for q in range(8):
                        P.mm(PB[2][H, q * 64:(q + 1) * 64], Pc[H, q, :], PTc[H, q, :], True, True, [PTc, Pc], [PB[2]],
                             nowaw=(q > 0))
                    yield
                    if m < 4:
                        P.cp("vector", Pn[H, :, :], PB[1][H, :].rearrange("p (q i) -> p q i", q=8), [PB[1]], [Pn])
                    P.cp("scalar", PTn[H, :, :], PB[2][H, :].rearrange("p (q i) -> p q i", q=8), [PB[2]], [PTn])
                    yield
                    for q in range(8):
                        P.mm(PB[3][H, q * 64:(q + 1) * 64], PTn[H, q, :], Am[H, q, :], True, True, [PTn, Am], [PB[3]],
                             nowaw=(q > 0))
                    yield
                    P.tt("vector", Am[H, :, :], PB[3][H, :].rearrange("p (q i) -> p q i", q=8), Am[H, :, :], ALU.add,
                         [PB[3], Am], [Am])
                    yield
                P.cp("gpsimd", Wb[H, :, :], Am[H, :, :], [Am], [Wb])
                knv = kn_c[H, 2 * bb:2 * bb + 2, :].rearrange("p n (h d) -> p (n h) d", h=4)
                vv = v_c[H, 2 * bb:2 * bb + 2, :].rearrange("p n (h d) -> p (n h) d", h=4)
                P.tt("gpsimd", kg_b[H, :, :], knv, b3(egc[H, ps], 128), ALU.mult, [kn_c, egc], [kg_b])
                P.tt("gpsimd", kdec_all[H, ps, :], knv, b3(kdc[H, ps], 128), ALU.mult, [kn_c, kdc], [kdec_all], nowaw=(bb > 0))
                yield
                for q in range(8):
                    P.mm(PB[0][:, q * 64:(q + 1) * 64], kg_b[H, q, :], Wb[H, q, :], True, True, [kg_b, Wb], [PB[0]], nowaw=(q > 0))
                yield
                P.cp("scalar", wT_all[:, ps, :], PB[0][:, :].rearrange("p (q i) -> p q i", q=8), [PB[0]], [wT_all], nowaw=(bb > 0))
                for hf in range(2):
                    bank = 1 + hf
                    for q4 in range(4):
                        q = hf * 4 + q4
                        P.mm(PB[bank][H, q4 * 128:(q4 + 1) * 128], Wb[H, q, :], vv[:, q, :], True, True, [Wb, v_c], [PB[bank]],
                             nowaw=(q4 > 0))
                    pq = slice(bb * 8 + hf * 4, bb * 8 + hf * 4 + 4)
                    P.tt("vector", ub_all[H, pq, :], PB[bank][H, :].rearrange("p (q d) -> p q d", q=4),
                         beta_c[H, pq].unsqueeze(2).to_broadcast([64, 4, 128]), ALU.mult, [PB[bank], beta_c], [ub_all],
                         nowaw=not (bb == 0 and hf == 0))
                    yield

        def scan(g):
            s, tt = tiles[g]
            t0 = tt * 512
            ug = s * NT + tt
            qT_c, z_c, wT_all, ub_all = qT_c2[g % 2], z_c2[g % 2], wT_all2[g % 2], ub_all2[g % 2]
            Aq_all, kdec_all, egl, rq_c, rqe, nbeta = (Aq_all2[g % 2], kdec_all2[g % 2], egl2[g % 2], rq_c2[g % 2],
                                                       rqe2[g % 2], nbeta2[g % 2])
            if tt == 0:
                for h in range(4):
                    P.memset("vector", S[h][:, :], 0.0, [S[h]])
                    P.memset("vector", Sb[h][:, :], 0.0, [Sb[h]])
            for n in range(8):
                for h in range(4):
                    p = n * 4 + h
                    bank = 4 + h
                    B_ = PB[bank]
                    P.mm(B_[H, 0:128], wT_all[:, p, :], Sb[h][:, :], True, True, [wT_all, Sb[h]], [B_])
                    P.mm(B_[H, 128:256], qT_c[:, h, n * 64:(n + 1) * 64], Sb[h][:, :], True, True, [qT_c, Sb[h]], [B_], nowaw=True)
                    P.stt("vector", vnew[h][H, :], B_[H, 0:128], nbeta[H, p:p + 1], ub_all[H, p, :], ALU.mult, ALU.add,
                          [B_, nbeta, ub_all], [vnew[h]])
                    P.ts("vector", t1c[h][H, :], B_[H, 128:256], rqe[H, p:p + 1], None, ALU.mult, None, [B_, rqe], [t1c[h]])
                    P.mm(B_[H, 256:384], Aq_all[H, p, :], vnew[h][H, :], True, True, [Aq_all, vnew[h]], [B_])
                    P.mm(B_[:, 384:512], kdec_all[H, p, :], vnew[h][H, :], True, True, [kdec_all, vnew[h]], [B_], nowaw=True)
                    P.stt("vector", obuf[H, n, h * 128:(h + 1) * 128], B_[H, 256:384], rq_c[H, p:p + 1], t1c[h][H, :],
                          ALU.mult, ALU.add, [B_, rq_c, t1c[h]], [obuf], nowaw=not (n == 0 and h == 0))
                    P.stt("vector", S[h][:, :], S[h][:, :], egl[:, p:p + 1], B_[:, 384:512], ALU.mult, ALU.add,
                          [S[h], egl, B_], [S[h]])
                    P.cp("gpsimd", Sb[h][:, :], S[h][:, :], [S[h]], [Sb[h]])
                    yield
            o3 = obuf[H, :, :].rearrange("p n (h d) -> p (n h) d", h=4)
            P.tt("gpsimd", osq[H, :, :], obuf[H, :, :], obuf[H, :, :], ALU.mult, [obuf], [osq])
            P.op("vector", lambda e: e.reduce_sum(out=ssqg[H, :], in_=osq[H, :, :].rearrange("p n (h d) -> p (n h) d", h=4), axis=AX.X),
                 [osq], [ssqg])
            rsqrt_(rstdg[H, :], ssqg[H, :], 1.0 / 128, negh[H, 0:32], [ssqg], [rstdg])
            yield
            P.tt("vector", o3, o3, rstdg[H, :].unsqueeze(2).to_broadcast([64, 32, 128]), ALU.mult, [obuf, rstdg], [obuf])
            P.tt("gpsimd", o3, o3, gnwB[H, :].unsqueeze(1).to_broadcast([64, 32, 128]), ALU.mult, [obuf, gnwB], [obuf])
            yield
            P.tt("vector", mixg[H, :, :], obuf[H, :, :], z_c[H, :, :], ALU.mult, [obuf, z_c], [mixg])
            P.dma("gpsimd", mix_d[s, t0:t0 + 512, 512:1024].rearrange("(n c) f -> c n f", c=64), mixg[H, :, :], [mixg],
                  [u_mixg[ug]])
            yield

        for _ in pre(0):
            pass
        KADV = 5
        for g in range(len(tiles)):
            gn = pre(g + 1) if g + 1 < len(tiles) else None
            for _ in scan(g):
                if gn is not None:
                    for _k in range(KADV):
                        if next(gn, "done") == "done":
                            gn = None
                            break
            if gn is not None:
                for _ in gn:
                    pass

    areset(0)
    qTb = [aalloc("qTb%d" % i, [T], BF16) for i in range(2)]
    kTb = [aalloc("kTb%d" % i, [T], BF16) for i in range(2)]
    vAb = [aalloc("vAb%d" % i, [NKT, 130], BF16) for i in range(2)]
    pT = [aalloc("pT%d" % i, [512], BF16) for i in range(3)]
    o1 = aalloc("o1", [4, 128], F32)
    o2 = aalloc("o2", [4, 128], F32)
    rinv = aalloc("rinv", [4], F32)
    rinv2 = aalloc("rinv2", [4], F32)
    ssqo = aalloc("ssqo", [4], F32)
    rstdo = aalloc("rstdo", [4], F32)
    mixt = [aalloc("mixt%d" % i, [4, 128], BF16) for i in range(2)]
    lamb = aalloc("lamb", [4, 64], F32)
    lamp = aalloc("lamp", [2, 64], F32)
    lams = aalloc("lams", [2], F32)
    neglam = aalloc("neglam", [1], F32)
    sublnB = aalloc("sublnB", [128], F32)
    junkB = aalloc("junkB", [128], F32)

    for i, dd in enumerate((lq1_d, lk1_d, lq2_d, lk2_d)):
        P.dma("sync", lamb[:, i, :], bc(dd, 128), [CONST], [lamb], nowaw=True)
    P.dma("sync", sublnB[:, :], bc(subln_d, 128), [CONST], [sublnB])
    P.ts("vector", sublnB[:, :], sublnB[:, :], 1.0 - LAMBDA_INIT, None, ALU.mult, None, [sublnB], [sublnB])
    P.tt("vector", lamp[:, 0, :], lamb[:, 0, :], lamb[:, 1, :], ALU.mult, [lamb], [lamp], nowaw=True)
    P.tt("vector", lamp[:, 1, :], lamb[:, 2, :], lamb[:, 3, :], ALU.mult, [lamb], [lamp], nowaw=True)
    P.op("vector", lambda e: e.reduce_sum(out=lams[:, :], in_=lamp[:, :, :], axis=AX.X), [lamp], [lams])
    P.act(lams[:, :], lams[:, :], AF.Exp, [lams], [lams])
    P.tt("vector", neglam[:, :], lams[:, 1:2], lams[:, 0:1], ALU.subtract, [lams], [neglam])
    P.ts("vector", neglam[:, :], neglam[:, :], -LAMBDA_INIT, None, ALU.add, None, [neglam], [neglam])
    for i in range(2):
        P.memset("vector", vAb[i][:, :, 128:130], 1.0, [vAb[i]])

    heads = [(s, h) for s in range(NSEQ) for h in range(4)]

    def loadB(n):
        s, h = heads[n]
        i = n % 2
        rd = [u_qT[s * NT + tt] for tt in range(NT)]
        P.dma("sync", qTb[i][:, :], qT_d[s, h, :, :], rd, [qTb[i]])
        rd = [u_kT[s * NT + tt] for tt in range(NT)]
        P.dma("sync", kTb[i][:, :], kT_d[s, h, :, :], rd, [kTb[i]])
        rd = [u_vda[s * NT + tt] for tt in range(NT)]
        nq = max(1, NKT // 8)
        for a in range(0, NKT, nq):
            P.dma("sync", vAb[i][:, a:a + nq, 0:128],
                  vda_d[s, a * 128:(a + nq) * 128, h * 128:(h + 1) * 128].rearrange("(n p) d -> p n d", p=128),
                  rd, [vAb[i]], nowaw=(a > 0))

    loadB(0)
    kidx = 0
    ucnt = 0
    pending = [None]

    def flush():
        if pending[0] is not None:
            f = pending[0]
            pending[0] = None
            f()

    for n, (s, h) in enumerate(heads):
        flush()
        if n + 1 < len(heads):
            loadB(n + 1)
        qb, kb, vb = qTb[n % 2], kTb[n % 2], vAb[n % 2]
        for qt in range(NT):
            for c in range(2):
                pob = (4, 5) if ucnt % 2 == 0 else (6, 7)
                ucnt += 1
                fresh = {pob[0]: True, pob[1]: True}
                nk = 4 * qt + 4
                cs = slice(c * 64, (c + 1) * 64)
                for kt in range(nk):
                    j = kt - 4 * qt
                    sbk = PB[kidx % 3]
                    pt = pT[kidx % 3]
                    kidx += 1
                    ksl = kb[cs, kt * 128:(kt + 1) * 128]
                    q0 = qt * 512
                    if j < 0:
                        lo = 0
                        P.mm(sbk[:, 0:512], ksl, qb[cs, q0:q0 + 512], True, True, [kb, qb], [sbk])
                    else:
                        lo = j * 128
                        P.mm(sbk[:, lo:lo + 128], ksl, qb[cs, q0 + lo:q0 + lo + 128], True, False, [kb, qb], [sbk])
                        P.mm(sbk[:, lo:lo + 128], identb[:, :], trim[:, :], False, True, [identb, trim], [sbk], nowaw=True)
                        if lo + 128 < 512:
                            P.mm(sbk[:, lo + 128:512], ksl, qb[cs, q0 + lo + 128:q0 + 512], True, True, [kb, qb], [sbk],
                                 nowaw=True)
                    P.act(pt[:, lo:512], sbk[:, lo:512], AF.Exp, [sbk], [pt], scale=0.125)
                    flush()

                    def pv(kt=kt, j=j, pt=pt, vb=vb, pob=pob, fresh=fresh, qt=qt, c=c, nk=nk, s=s, h=h, n=n):
                        for i in range(max(j, 0), 4):
                            bank = pob[i // 2]
                            off = (i % 2) * 130
                            P.mm(PB[bank][:, off:off + 129], pt[:, i * 128:(i + 1) * 128], vb[:, kt, 0:129],
                                 fresh[bank], (kt == 4 * qt + i and i % 2 == 1), [pt, vb], [PB[bank]], nowaw=not fresh[bank])
                            fresh[bank] = False
                        if kt != nk - 1:
                            return
                        for i in range(4):
                            bank = pob[i // 2]
                            off = (i % 2) * 130
                            P.op("vector", (lambda bank=bank, off=off, i=i: (lambda e: e.reciprocal(out=rinv[:, i:i + 1], in_=PB[bank][:, off + 128:off + 129])))(),
                                 [PB[bank]], [rinv], nowaw=(i > 0))
                            if c == 0:
                                P.ts("vector", o1[:, i, :], PB[bank][:, off:off + 128], rinv[:, i:i + 1], None, ALU.mult, None,
                                     [PB[bank], rinv], [o1], nowaw=(i > 0))
                            else:
                                P.ts("vector", rinv2[:, i:i + 1], rinv[:, i:i + 1], neglam[:, 0:1], None, ALU.mult, None,
                                     [rinv, neglam], [rinv2], nowaw=(i > 0))
                                P.stt("vector", o2[:, i, :], PB[bank][:, off:off + 128], rinv2[:, i:i + 1], o1[:, i, :],
                                      ALU.mult, ALU.add, [PB[bank], rinv2, o1], [o2], nowaw=(i > 0))
                        if c == 0:
                            return
                        mt = mixt[(n * NT + qt) % 2]
                        P.memset("vector", ssqo[:, :], 0.0, [ssqo])
                        for i in range(4):
                            P.act(junkB[:, :], o2[:, i, :], AF.Square, [o2, ssqo], [junkB, ssqo], accum=ssqo[:, i:i + 1], nowaw=True)
                        rsqrt_(rstdo[:, :], ssqo[:, :], 1.0 / 128, negh[:, 0:4], [ssqo], [rstdo])
                        for i in range(4):
                            P.stt("vector", mt[:, i, :], o2[:, i, :], rstdo[:, i:i + 1], sublnB[:, :], ALU.mult, ALU.mult,
                                  [o2, rstdo, sublnB], [mt], nowaw=(i > 0))
                        P.dma("gpsimd", mix_d[s, qt * 512:(qt + 1) * 512, h * 128:(h + 1) * 128].rearrange("(i p) d -> p i d", p=128),
                              mt[:, :, :], [mt], [u_mixa[s * NT + qt]], nowaw=True)

                    pending[0] = pv
    flush()

    if stop_after == "B":
        P.emit()
        return nc
    if do_c:
        phase_c()
    if stop_after == "C":
        P.emit()
        return nc

    areset(0)
    Wout = aalloc("Wout", [8, D], BF16)
    xh = [aalloc("xh%d" % i, [4, D], F32) for i in range(2)]
    wst = [xh[1][:, 0, :], xh[1][:, 1, :]]
    mixb = aalloc("mixb", [4, D], BF16)
    mixT = aalloc("mixTD", [8, 512], BF16)
    wup = [aalloc("wup%d" % i, [2, 8, 256], BF16) for i in range(3)]
    wdn = [aalloc("wdn%d" % i, [11, 512], BF16) for i in range(2)]
    aT = aalloc("aT", [22, 512], BF16)
    hp = [aalloc("hp%d" % i, [4, 514], BF16) for i in range(2)]
    halo = aalloc("halo", [11, 4, 2], BF16)
    fdiag = aalloc("fdiag", [44, 3, 128], BF16)
    fcw = aalloc("fcw", [44, 3], F32)
    gt = [aalloc("gt%d" % i, [512], BF16) for i in range(4)]
    nwF = aalloc("nwF", [D], F32)
    nwO = aalloc("nwO", [D], F32)
    junkD = aalloc("junkD", [D], BF16)
    ssq2 = aalloc("ssq2", [4], F32)
    rstd2 = aalloc("rstd2", [4], F32)

    for k in range(8):
        P.dma("sync", wst[k % 2], wout_d[k * 128:(k + 1) * 128, :], [CONST], [xh[1]])
        P.cp(cast_engs[k % 2], Wout[:, k, :], wst[k % 2], [xh[1]], [Wout], nowaw=True)
    P.dma("sync", fcw[:, :, :], fcw_d[:, :, :], [CONST], [fcw])
    di = 0
    for cc in range(44):
        for j in range(3):
            eng_ = "vector" if di % 2 == 0 else "gpsimd"
            di += 1
            P.ts(eng_, fdiag[:, cc, j, :], identb[:, :], fcw[:, cc, j:j + 1], 1.0, ALU.mult, ALU.mult,
                 [identb, fcw], [fdiag], nowaw=True)
    P.dma("sync", nwF[:, :], bc(fnw_d, 128), [CONST], [nwF])
    P.dma("sync", nwO[:, :], bc(finw_d, 128), [CONST], [nwO])

    def loadD(g):
        s, tt = tiles[g]
        t0 = tt * 512
        P.dma("sync", xh[g % 2][:, :, :], x_d[s, t0:t0 + 512, :].rearrange("(j p) d -> p j d", p=128), [CONST], [xh[g % 2]])

    def load_wup(grp):
        sl = wup[grp % 3]
        P.dma("sync", sl[:, 0, :, :], wupb_d[:, grp * 256:(grp + 1) * 256].rearrange("(k p) c -> p k c", p=128),
              [u_wupb], [sl])
        P.dma("sync", sl[:, 1, :, :], wupb_d[:, DFF + grp * 256:DFF + (grp + 1) * 256].rearrange("(k p) c -> p k c", p=128),
              [u_wupb], [sl], nowaw=True)

    def load_wdn(idx):
        half, piece = idx // 2, idx % 2
        sl = wdn[idx % 2]
        P.dma("sync", sl[:, :, :],
              wdnb_d[piece * 11 * 128:(piece + 1) * 11 * 128, half * 512:(half + 1) * 512].rearrange("(f p) c -> p f c", p=128),
              [u_wdnb], [sl])

    def rms_to(src, dst, nw):
        P.memset("vector", ssq2[:, :], 0.0, [ssq2])
        for j in range(4):
            P.act(junkD[:, :], src[:, j, :], AF.Square, [src, ssq2], [junkD, ssq2], accum=ssq2[:, j:j + 1], nowaw=True)
        rsqrt_(rstd2[:, :], ssq2[:, :], 1.0 / D, negh[:, 0:4], [ssq2], [rstd2])
        for j in range(4):
            P.stt("vector", dst[:, j, :], src[:, j, :], rstd2[:, j:j + 1], nw[:, :], ALU.mult, ALU.mult,
                  [src, rstd2, nw], [dst], nowaw=(j > 0))

    def transpose_to(src, dst):
        for k in range(8):
            bk = k % 2
            for j in range(4):
                P.tr(pbf(bk)[:, j * 128:(j + 1) * 128], src[:, j, k * 128:(k + 1) * 128], identb[:, :],
                     [src, identb], [PB[bk]], nowaw=(j > 0))
            P.cp(ev_eng(), dst[:, k, :], pbf(bk)[:, 0:512], [PB[bk]], [dst], nowaw=(k > 0))

    loadD(0)
    for g, (s, tt) in enumerate(tiles):
        t0 = tt * 512
        ug = s * NT + tt
        if g + 1 < len(tiles):
            loadD(g + 1)
        xt = xh[g % 2]
        rdm = [u_mixa[ug]] + ([u_mixg[ug]] if do_c else [])
        P.dma("sync", mixb[:, :, :], mix_d[s, t0:t0 + 512, :].rearrange("(j p) d -> p j d", p=128), rdm, [mixb])
        load_wup(0)
        load_wup(1)
        transpose_to(mixb, mixT)
        bi = 0
        for j in range(4):
            for hf in range(2):
                bank = 2 + (bi % 2)
                bi += 1
                for k in range(8):
                    P.mm(PB[bank][:, :], mixT[:, k, j * 128:(j + 1) * 128], Wout[:, k, hf * 512:(hf + 1) * 512],
                         k == 0, k == 7, [mixT, Wout], [PB[bank]], nowaw=(k > 0))
                P.tt("vector", xt[:, j, hf * 512:(hf + 1) * 512], PB[bank][:, :], xt[:, j, hf * 512:(hf + 1) * 512], ALU.add,
                     [PB[bank], xt], [xt])
        rms_to(xt, mixb, nwF)
        transpose_to(mixb, mixT)
        if tt == 0:
            P.memset("vector", halo[:, :, :, :], 0.0, [halo])
        def up(grp):
            if grp + 2 < 11:
                load_wup(grp + 2)
            if grp == 9:
                load_wdn(0)
            if grp == 10:
                load_wdn(1)
            sl = wup[grp % 3]
            hpb = hp[grp % 2]
            P.cp("gpsimd", hpb[:, :, 0:2], halo[:, grp, :, :], [halo], [hpb])
            for q in range(4):
                bank = 2 + (q % 2)
                for k in range(8):
                    P.mm(PB[bank][:, :], sl[:, q // 2, k, (q % 2) * 128:(q % 2) * 128 + 128], mixT[:, k, :], k == 0, k == 7,
                         [sl, mixT], [PB[bank]], nowaw=(k > 0))
                P.cp(ev_eng(), hpb[:, q, 2:514], PB[bank][:, :], [PB[bank]], [hpb], nowaw=True)

        def conv(grp):
            hpb = hp[grp % 2]
            chunks = [2 * grp, 2 * grp + 1, 22 + 2 * grp, 23 + 2 * grp]
            for q in range(4):
                bank = 4 + (q % 2)
                for j in range(3):
                    P.mm(PB[bank][:, :], fdiag[:, chunks[q], j, :], hpb[:, q, j:j + 512], j == 0, j == 2, [fdiag, hpb], [PB[bank]],
                         nowaw=(j > 0))
                if q < 2:
                    P.act(gt[(grp % 2) * 2 + q][:, :], PB[bank][:, :], AF.Silu, [PB[bank]], [gt[(grp % 2) * 2 + q]])
                else:
                    P.tt("vector", aT[:, 2 * grp + q - 2, :], PB[bank][:, :], gt[(grp % 2) * 2 + q - 2][:, :], ALU.mult,
                         [PB[bank], gt[(grp % 2) * 2 + q - 2]], [aT], nowaw=True)
            P.cp("gpsimd", halo[:, grp, :, :], hpb[:, :, 512:514], [hpb], [halo], nowaw=True)

        up(0)
        for grp in range(11):
            if grp + 1 < 11:
                up(grp + 1)
            conv(grp)
        for idx in range(4):
            half, piece = idx // 2, idx % 2
            sl = wdn[idx % 2]
            for f in range(11):
                fc = piece * 11 + f
                for j in range(4):
                    P.mm(PB[4 + j][:, :], aT[:, fc, j * 128:(j + 1) * 128], sl[:, f, :], fc == 0, fc == 21,
                         [aT, sl], [PB[4 + j]], nowaw=(fc > 0))
            if idx + 2 < 4:
                load_wdn(idx + 2)
            if piece == 1:
                for j in range(4):
                    P.tt("vector", xt[:, j, half * 512:(half + 1) * 512], PB[4 + j][:, :],
                         xt[:, j, half * 512:(half + 1) * 512], ALU.add, [PB[4 + j], xt], [xt])
        rms_to(xt, xt, nwO)
        P.dma("gpsimd", out_d[s, t0:t0 + 512, :].rearrange("(j p) d -> p j d", p=128), xt[:, :, :], [xt], [CONST_OUT])

    P.emit()
    return nc


def host_consts(T):
    identf = np.eye(128, dtype=np.float32)
    identb = identf.astype(ml_dtypes.bfloat16)
    rot = np.zeros((128, 128), np.float32)
    for g in range(2):
        for d in range(32):
            rot[g * 64 + d + 32, g * 64 + d] = -1.0
            rot[g * 64 + d, g * 64 + d + 32] = 1.0
    inv_freq = (10000.0 ** (-np.arange(0, 64, 2, dtype=np.float32) / 64)).astype(np.float32)
    ang = np.arange(T, dtype=np.float32)[None, :] * inv_freq[:, None]
    cos = np.cos(ang).astype(np.float32); sin = np.sin(ang).astype(np.float32)
    rope = np.zeros((128, 2, T), np.float32)
    for r in range(128):
        rope[r, 0] = cos[r % 32]
        rope[r, 1] = sin[r % 32]
    kk = np.arange(128)[:, None]; qq = np.arange(128)[None, :]
    trim = np.where(kk <= qq, 0.0, -30000.0).astype(np.float32).astype(ml_dtypes.bfloat16)
    s = np.arange(64)[:, None]; i = np.arange(64)[None, :]
    cm = np.zeros((64, 5, 64), np.float32)
    cm[:, 0] = (s <= i); cm[:, 1] = (s < i); cm[:, 2] = (s <= i); cm[:, 3] = (s == i); cm[:, 4] = (s > i)
    return {"c_identb": identb, "c_identf": identf, "c_rot": rot.astype(ml_dtypes.bfloat16), "c_rope": rope,
            "c_trimask": trim, "c_masks": cm}


_W1 = ["attn_norm_w", "w_in", "da_lambda_q1", "da_lambda_k1", "da_lambda_q2", "da_lambda_k2", "da_subln_w",
       "gdn_conv_w", "gdn_a_log", "gdn_dt_bias", "gdn_norm_w", "w_out", "ffn_norm_w", "ffn_w_up", "ffn_conv_w",
       "ffn_w_down"]


def make_in_maps(inputs, n_cores, nseq, T):
    consts = host_consts(T)
    base = {k: np.ascontiguousarray(np.asarray(inputs[k], np.float32)[0]) for k in _W1}
    base["final_norm_w"] = np.ascontiguousarray(np.asarray(inputs["final_norm_w"], np.float32))
    base["gdn_conv_w"] = np.ascontiguousarray(base["gdn_conv_w"].reshape(4, 12, 128).transpose(2, 1, 0))
    base["ffn_conv_w"] = np.ascontiguousarray(base["ffn_conv_w"].reshape(3, 44, 128).transpose(2, 1, 0))
    base.update(consts)
    x = np.asarray(inputs["x"], np.float32)
    maps = []
    for c in range(n_cores):
        m = dict(base)
        m["x"] = np.ascontiguousarray(x[c * nseq:(c + 1) * nseq])
        maps.append(m)
    return maps


def kernel(**inputs):
    x = inputs["x"]
    B, T, _ = x.shape
    n = 8
    nseq = B // n
    nc = build(T, nseq)
    maps = make_in_maps(inputs, n, nseq, T)
    res = run_bass_kernel_spmd(nc, maps, core_ids=list(range(n)))
    return np.concatenate([r["out"] for r in res.results], axis=0)
```

```python
import contextlib
import math
import numpy as np
import ml_dtypes
import concourse.bass as bass
import concourse.mybir as mybir
from concourse.bass_utils import run_bass_kernel_spmd

F32 = mybir.dt.float32
BF16 = mybir.dt.bfloat16
AF = mybir.ActivationFunctionType
ALU = mybir.AluOpType
AX = mybir.AxisListType

ENGS = ("sync", "scalar", "gpsimd", "vector", "tensor")
NDMA = 8
EPS = 1e-6
D = 1024
DFF = 2816
INC = 3592
LAMBDA_INIT = 0.8 - 0.6 * math.exp(-0.3 * 0)


class Buf:
    def __init__(self, name, t):
        self.name = name
        self.t = t
        self.writers = []
        self.readers = []
        self.gen_deps = set()
        self.psum = False

    def __getitem__(self, idx):
        return self.t[idx]


class Op:
    __slots__ = ("eng", "fn", "deps", "is_dma", "signal", "tok", "idx")

    def __init__(self, eng, fn, is_dma):
        self.eng = eng
        self.fn = fn
        self.deps = set()
        self.is_dma = is_dma
        self.signal = False
        self.tok = None


class Prog:
    def __init__(self, nc):
        self.nc = nc
        self.ops = []

    def sb(self, name, shape, dt=F32):
        return Buf(name, self.nc.alloc_sbuf_tensor(name, list(shape), dt))

    def ps(self, name, shape, dt=F32):
        b = Buf(name, self.nc.alloc_psum_tensor(name, list(shape), dt))
        b.psum = True
        return b

    def dram(self, name, shape, dt=F32, kind="Internal"):
        return Buf(name, self.nc.dram_tensor(name, list(shape), dt, kind=kind))

    def op(self, eng, fn, reads=(), writes=(), dma=False, nowaw=False):
        o = Op(eng, fn, dma)
        o.idx = len(self.ops)
        deps = o.deps
        for r in reads:
            deps.update(r.writers)
            if r.psum:
                for ri in r.readers:
                    if self.ops[ri].eng != eng:
                        deps.add(ri)
        for w in writes:
            if nowaw and w.writers:
                deps.update(w.gen_deps)
                deps.update(w.readers)
                w.gen_deps.update(w.readers)
                w.writers.append(o.idx)
                w.readers = []
            else:
                g = set(w.writers) | set(w.readers)
                deps.update(g)
                w.gen_deps = g
                w.writers = [o.idx]
                w.readers = []
        for r in reads:
            r.readers.append(o.idx)
        deps.discard(o.idx)
        self.ops.append(o)
        return o

    def dma(self, eng, out_ap, in_ap, reads, writes, nowaw=False):
        return self.op(eng, lambda e: e.dma_start(out=out_ap, in_=in_ap), reads, writes, dma=True, nowaw=nowaw)

    def mm(self, out_ap, lhsT, rhs, start, stop, reads, writes, nowaw=False):
        return self.op("tensor", lambda e: e.matmul(out_ap, lhsT=lhsT, rhs=rhs, start=start, stop=stop),
                       reads, writes, nowaw=nowaw)

    def tr(self, out_ap, in_ap, ident, reads, writes, nowaw=False):
        return self.op("tensor", lambda e: e.transpose(out_ap, in_ap, ident), reads, writes, nowaw=nowaw)

    def act(self, out_ap, in_ap, func, reads, writes, scale=1.0, bias=None, accum=None, eng="scalar", nowaw=False):
        def f(e):
            kw = dict(out=out_ap, in_=in_ap, func=func, scale=scale)
            if bias is not None:
                kw["bias"] = bias
            if accum is not None:
                kw["accum_out"] = accum
            return e.activation(**kw)
        return self.op(eng, f, reads, writes, nowaw=nowaw)

    def cp(self, eng, out_ap, in_ap, reads, writes, nowaw=False):
        if eng == "scalar":
            return self.op(eng, lambda e: e.copy(out=out_ap, in_=in_ap), reads, writes, nowaw=nowaw)
        return self.op(eng, lambda e: e.tensor_copy(out=out_ap, in_=in_ap), reads, writes, nowaw=nowaw)

    def tt(self, eng, out_ap, in0, in1, op, reads, writes, nowaw=False):
        return self.op(eng, lambda e: e.tensor_tensor(out=out_ap, in0=in0, in1=in1, op=op), reads, writes, nowaw=nowaw)

    def ts(self, eng, out_ap, in0, s1, s2, op0, op1, reads, writes, nowaw=False):
        if s2 is None:
            return self.op(eng, lambda e: e.tensor_scalar(out=out_ap, in0=in0, scalar1=s1, scalar2=None, op0=op0),
                           reads, writes, nowaw=nowaw)
        return self.op(eng, lambda e: e.tensor_scalar(out=out_ap, in0=in0, scalar1=s1, scalar2=s2, op0=op0, op1=op1),
                       reads, writes, nowaw=nowaw)

    def stt(self, eng, out_ap, in0, scalar, in1, op0, op1, reads, writes, nowaw=False):
        return self.op(eng, lambda e: e.scalar_tensor_tensor(out=out_ap, in0=in0, scalar=scalar, in1=in1, op0=op0, op1=op1),
                       reads, writes, nowaw=nowaw)

    def memset(self, eng, ap, val, writes, nowaw=False):
        return self.op(eng, lambda e: e.memset(ap, val), [], writes, nowaw=nowaw)

    def emit(self, final_wait_eng="sync"):
        nc = self.nc
        import os
        kcut = int(os.environ.get("KCUT", "0"))
        if kcut:
            self.ops = self.ops[:kcut]
        ops = self.ops
        print("emit: n_ops =", len(ops), flush=True)
        for o in ops:
            last = {}
            keep = set()
            for d in o.deps:
                od = ops[d]
                if od.is_dma:
                    keep.add(d)
                elif od.eng not in last or last[od.eng] < d:
                    last[od.eng] = d
            keep.update(last.values())
            o.deps = keep
        for o in ops:
            for d in o.deps:
                od = ops[d]
                if od.eng == "tensor" and o.eng == "tensor" and not od.is_dma and not o.is_dma:
                    continue
                od.signal = True
            if o.is_dma:
                o.signal = True
        with contextlib.ExitStack() as st:
            esem = {e: st.enter_context(nc.semaphore("s_" + e)) for e in ENGS}
            dsem = {e: [st.enter_context(nc.semaphore("d_%s%d" % (e, i))) for i in range(NDMA)]
                    for e in ("sync", "scalar", "gpsimd")}
            ecount = {e: 0 for e in ENGS}
            dcount = {e: [0] * NDMA for e in dsem}
            drot = {e: 0 for e in dsem}
            prewait = {}
            for o in ops:
                if not o.signal:
                    continue
                if o.is_dma:
                    i = drot[o.eng]
                    drot[o.eng] = (i + 1) % NDMA
                    prewait[o.idx] = (dsem[o.eng][i], dcount[o.eng][i])
                    dcount[o.eng][i] += 16
                    o.tok = (dsem[o.eng][i], dcount[o.eng][i], 16)
                else:
                    ecount[o.eng] += 1
                    o.tok = (esem[o.eng], ecount[o.eng], 1)
            by_eng = {e: [o for o in ops if o.eng == e] for e in ENGS}
            block = st.enter_context(nc.Block())

            def run(e, eng):
                known = {}
                for o in by_eng[e]:
                    waits = {}
                    for d in o.deps:
                        od = ops[d]
                        if od.tok is None:
                            continue
                        if od.eng == "tensor" and e == "tensor" and not od.is_dma and not o.is_dma:
                            continue
                        s, v, _ = od.tok
                        k = id(s)
                        if known.get(k, 0) >= v:
                            continue
                        if k not in waits or waits[k][1] < v:
                            waits[k] = (s, v)
                    if o.idx in prewait:
                        s, v = prewait[o.idx]
                        k = id(s)
                        if v > 0 and known.get(k, 0) < v and (k not in waits or waits[k][1] < v):
                            waits[k] = (s, v)
                    for k, (s, v) in waits.items():
                        eng.wait_ge(s, v)
                        known[k] = v
                    ins = o.fn(eng)
                    if o.tok is not None:
                        ins.then_inc(o.tok[0], o.tok[2])
                if e == final_wait_eng:
                    for q in dsem:
                        for i in range(NDMA):
                            if dcount[q][i] > 0:
                                eng.wait_ge(dsem[q][i], dcount[q][i])

            @block.sync
            def _(eng):
                run("sync", eng)

            @block.scalar
            def _(eng):
                run("scalar", eng)

            @block.gpsimd
            def _(eng):
                run("gpsimd", eng)

            @block.vector
            def _(eng):
                run("vector", eng)

            @block.tensor
            def _(eng):
                run("tensor", eng)


def build(T, NSEQ, debug=False, do_c=True, stop_after=None):
    nc = bass.Bass("TRN2", target_bir_lowering=False)
    P = Prog(nc)
    NT = T // 512
    NKT = T // 128
    dk = "ExternalOutput"

    def din(name, shape, dt=F32):
        return P.dram(name, shape, dt, kind="ExternalInput")

    x_d = din("x", [NSEQ, T, D])
    anw_d = din("attn_norm_w", [D])
    win_d = din("w_in", [D, INC])
    lq1_d = din("da_lambda_q1", [64]); lk1_d = din("da_lambda_k1", [64])
    lq2_d = din("da_lambda_q2", [64]); lk2_d = din("da_lambda_k2", [64])
    subln_d = din("da_subln_w", [128])
    gcw_d = din("gdn_conv_w", [128, 12, 4])
    alog_d = din("gdn_a_log", [4]); dtb_d = din("gdn_dt_bias", [4])
    gnw_d = din("gdn_norm_w", [128])
    wout_d = din("w_out", [D, D])
    fnw_d = din("ffn_norm_w", [D])
    wup_d = din("ffn_w_up", [D, 2 * DFF])
    fcw_d = din("ffn_conv_w", [128, 44, 3])
    wdn_d = din("ffn_w_down", [DFF, D])
    finw_d = din("final_norm_w", [D])
    identb_d = din("c_identb", [128, 128], BF16)
    identf_d = din("c_identf", [128, 128])
    rot_d = din("c_rot", [128, 128], BF16)
    rope_d = din("c_rope", [128, 2, T])
    trim_d = din("c_trimask", [128, 128], BF16)
    cm_d = din("c_masks", [64, 5, 64])
    out_d = P.dram("out", [NSEQ, T, D], F32, kind="ExternalOutput")

    qT_d = P.dram("s_qT", [NSEQ, 4, 128, T], BF16, kind=dk)
    kT_d = P.dram("s_kT", [NSEQ, 4, 128, T], BF16, kind=dk)
    vda_d = P.dram("s_vda", [NSEQ, T, 512], BF16, kind=dk)
    gqT_d = P.dram("s_gqT", [NSEQ, 4, 128, T], BF16, kind=dk)
    gkT_d = P.dram("s_gkT", [NSEQ, 4, 128, T], BF16, kind=dk)
    gkn_d = P.dram("s_gkn", [NSEQ, T, 512], BF16, kind=dk)
    gv_d = P.dram("s_gv", [NSEQ, T, 512], BF16, kind=dk)
    gz_d = P.dram("s_gz", [NSEQ, T, 512], BF16, kind=dk)
    gsc_d = P.dram("s_gsc", [NSEQ, T, 16], F32, kind=dk)
    mix_d = P.dram("s_mix", [NSEQ, T, D], BF16, kind=dk)
    wupb_d = P.dram("s_wupb", [D, 2 * DFF], BF16)
    wdnb_d = P.dram("s_wdnb", [DFF, D], BF16)
    def units(n):
        return [Buf("u", None) for _ in range(n)]
    u_qT = units(NSEQ * NT); u_kT = units(NSEQ * NT); u_vda = units(NSEQ * NT)
    u_gqT = units(NSEQ * NT); u_gkT = units(NSEQ * NT); u_gkn = units(NSEQ * NT)
    u_gv = units(NSEQ * NT); u_gz = units(NSEQ * NT); u_gsc = units(NSEQ * NT)
    u_mixa = units(NSEQ * NT); u_mixg = units(NSEQ * NT)
    u_wupb = Buf("u", None); u_wdnb = Buf("u", None)
    CONST = Buf("const_in", None)
    CONST_OUT = Buf("const_out", None)

    PB = [P.ps("pb%d" % i, [128, 512]) for i in range(8)]

    def pbf(i):
        return PB[i].t[:, :].bitcast(BF16)

    ARENA_BYTES = 196 * 1024
    arena = nc.alloc_sbuf_tensor("arena", [128, ARENA_BYTES // 4], F32)
    live = []
    cur = [0]

    def aalloc(name, free_shape, dt):
        n = int(np.prod(free_shape))
        nb = n * (2 if dt == BF16 else 4)
        nb = (nb + 63) // 64 * 64
        s = cur[0]
        e = s + nb
        assert e <= ARENA_BYTES, (name, e)
        cur[0] = e
        ap = arena[:, s // 4:e // 4]
        if dt == BF16:
            ap = ap.bitcast(BF16)
        ap = ap[:, 0:n]
        if len(free_shape) == 2:
            ap = ap.rearrange("p (a b) -> p a b", a=free_shape[0])
        elif len(free_shape) == 3:
            ap = ap.rearrange("p (a b c) -> p a b c", a=free_shape[0], b=free_shape[1])
        elif len(free_shape) == 4:
            ap = ap.rearrange("p (a b c d) -> p a b c d", a=free_shape[0], b=free_shape[1], c=free_shape[2])
        b = Buf(name, ap)
        for (s0, e0, ob) in live:
            if s0 < e and s < e0:
                b.readers.extend(ob.writers)
                b.readers.extend(ob.readers)
        live.append((s, e, b))
        return b

    def areset(mark=0):
        cur[0] = mark

    identb = P.sb("identb", [128, 128], BF16)
    identf = P.sb("identf", [128, 128])
    rotm = P.sb("rotm", [128, 128], BF16)
    trim = P.sb("trim", [128, 128], BF16)
    cm = P.sb("cm", [64, 5, 64])
    negh = P.sb("negh", [128, 64])
    ones_f = P.sb("ones_f", [64, 128])
    P.dma("sync", identb[:, :], identb_d[:, :], [CONST], [identb])
    P.dma("sync", identf[:, :], identf_d[:, :], [CONST], [identf])
    P.dma("sync", rotm[:, :], rot_d[:, :], [CONST], [rotm])
    P.dma("sync", trim[:, :], trim_d[:, :], [CONST], [trim])
    P.dma("sync", cm[:, :, :], cm_d[:, :, :], [CONST], [cm])
    P.memset("vector", negh[:, :], -0.5, [negh])
    P.memset("vector", ones_f[:, :], 1.0, [ones_f])

    def bc(d_buf, n):
        return d_buf.t.ap().partition_broadcast(n)

    def rsqrt_(out_ap, in_ap, scale, nh_ap, reads, writes):
        P.ts("vector", out_ap, in_ap, scale, EPS, ALU.mult, ALU.add, reads, writes)
        P.tt("gpsimd", out_ap, out_ap, nh_ap, ALU.pow, list(writes) + [negh], writes)

    areset(0)
    Win = aalloc("Win", [8, INC], BF16)
    markW = cur[0]
    stg = [aalloc("stg%d" % i, [3592], F32) for i in range(2)]
    stgb = [aalloc("stgb%d" % i, [3592], BF16) for i in range(2)]
    ci = [0]
    cast_engs = ["vector", "gpsimd"]

    def cast_to_dram(src_ap, dst_ap, ncols, unit):
        i = ci[0] % 2
        ci[0] += 1
        P.dma("sync", stg[i][:, 0:ncols], src_ap, [CONST], [stg[i]])
        P.cp(cast_engs[i], stgb[i][:, 0:ncols], stg[i][:, 0:ncols], [stg[i]], [stgb[i]])
        P.dma("sync", dst_ap, stgb[i][:, 0:ncols], [stgb[i]], [unit], nowaw=True)

    for k in range(8):
        for hf in range(2):
            cast_to_dram(wup_d[k * 128:(k + 1) * 128, hf * DFF:(hf + 1) * DFF],
                         wupb_d[k * 128:(k + 1) * 128, hf * DFF:(hf + 1) * DFF], DFF, u_wupb)
    for k in range(22):
        cast_to_dram(wdn_d[k * 128:(k + 1) * 128, :], wdnb_d[k * 128:(k + 1) * 128, :], D, u_wdnb)

    for k in range(8):
        i = ci[0] % 2
        ci[0] += 1
        P.dma("sync", stg[i][:, 0:INC], win_d[k * 128:(k + 1) * 128, :], [CONST], [stg[i]])
        P.cp(cast_engs[i], Win[:, k, :], stg[i][:, 0:INC], [stg[i]], [Win], nowaw=True)
    areset(markW)
    xbuf = [aalloc("xbuf%d" % i, [4, D], F32) for i in range(2)]
    rpbuf = [aalloc("rp%d" % i, [2, 512], F32) for i in range(2)]
    xn = aalloc("xn", [4, D], BF16)
    xnT = aalloc("xnT", [8, 512], BF16)
    nwA = aalloc("nwA", [D], F32)
    junk = aalloc("junk", [D], BF16)
    ssq = aalloc("ssq", [4], F32)
    rstd = aalloc("rstd", [4], F32)
    xb = [aalloc("xb%d" % i, [512], BF16) for i in range(2)]
    t1 = [aalloc("t1_%d" % i, [512], F32) for i in range(2)]
    t2 = [aalloc("t2_%d" % i, [512], F32) for i in range(2)]
    ro = [aalloc("ro%d" % i, [512], BF16) for i in range(3)]
    gh = aalloc("gh", [12, 515], BF16)
    gdiag = aalloc("gdiag", [12, 4, 128], BF16)
    gcw = aalloc("gcw", [12, 4], F32)
    gs = [aalloc("gs%d" % i, [512], BF16) for i in range(3)]
    knt = [aalloc("knt%d" % i, [128], BF16) for i in range(2)]
    knTt = [aalloc("knTt%d" % i, [512], BF16) for i in range(2)]
    kn_t = aalloc("kn_t", [4, 512], BF16)
    v_t = aalloc("v_t", [4, 512], BF16)
    va_t = aalloc("va_t", [4, 512], BF16)
    z_t = aalloc("z_t", [4, 512], BF16)
    ssqk = aalloc("ssqk", [4, 8], F32)
    rk1 = aalloc("rk1", [4, 8], F32)
    gsc_t = aalloc("gsc_t", [4, 16], F32)
    ba_t = aalloc("ba_t", [4, 8], F32)
    tmp4 = aalloc("tmp4", [4, 4], F32)
    dtbB = aalloc("dtbB", [4], F32)
    negA = aalloc("negA", [4], F32)
    markA = cur[0]

    P.dma("sync", nwA[:, :], bc(anw_d, 128), [CONST], [nwA])
    P.dma("sync", gcw[:, :, :], gcw_d[:, :, :], [CONST], [gcw])
    P.dma("sync", dtbB[:, :], bc(dtb_d, 128), [CONST], [dtbB])
    P.dma("sync", negA[:, :], bc(alog_d, 128), [CONST], [negA])
    P.act(negA[:, :], negA[:, :], AF.Exp, [negA], [negA])
    P.ts("vector", negA[:, :], negA[:, :], -1.0, None, ALU.mult, None, [negA], [negA])
    for c in range(12):
        for j in range(4):
            P.ts("gpsimd", gdiag[:, c, j, :], identb[:, :], gcw[:, c, j:j + 1], None, ALU.mult, None,
                 [identb, gcw], [gdiag], nowaw=True)

    tiles = [(s, tt) for s in range(NSEQ) for tt in range(NT)]

    def loadA(g):
        s, tt = tiles[g]
        t0 = tt * 512
        P.dma("sync", xbuf[g % 2][:, :, :], x_d[s, t0:t0 + 512, :].rearrange("(j p) d -> p j d", p=128),
              [CONST], [xbuf[g % 2]])
        P.dma("sync", rpbuf[g % 2][:, :, :], rope_d[:, :, t0:t0 + 512], [CONST], [rpbuf[g % 2]])

    evi = [0]

    def ev_eng():
        evi[0] += 1
        return "vector" if evi[0] % 2 else "scalar"

    loadA(0)
    for g, (s, tt) in enumerate(tiles):
        t0 = tt * 512
        ug = s * NT + tt
        if g + 1 < len(tiles):
            loadA(g + 1)
        xt = xbuf[g % 2]
        rp = rpbuf[g % 2]
        P.memset("vector", ssq[:, :], 0.0, [ssq])
        for j in range(4):
            P.act(junk[:, :], xt[:, j, :], AF.Square, [xt, ssq], [junk, ssq], accum=ssq[:, j:j + 1], nowaw=True)
        rsqrt_(rstd[:, :], ssq[:, :], 1.0 / D, negh[:, 0:4], [ssq], [rstd])
        for j in range(4):
            P.stt("vector", xn[:, j, :], xt[:, j, :], rstd[:, j:j + 1], nwA[:, :], ALU.mult, ALU.mult,
                  [xt, rstd, nwA], [xn], nowaw=True)
        for k in range(8):
            bk = k % 2
            for j in range(4):
                P.tr(pbf(bk)[:, j * 128:(j + 1) * 128], xn[:, j, k * 128:(k + 1) * 128], identb[:, :],
                     [xn, identb], [PB[bk]], nowaw=(j > 0))
            P.cp(ev_eng(), xnT[:, k, :], pbf(bk)[:, 0:512], [PB[bk]], [xnT], nowaw=True)

        def proj_fm(c0, bank):
            for k in range(8):
                P.mm(PB[bank][:, :], Win[:, k, c0:c0 + 128], xnT[:, k, :], k == 0, k == 7, [Win, xnT], [PB[bank]],
                     nowaw=(k > 0))

        def projqk(c):
            bank = 2 + (c % 2)
            proj_fm(c * 128, bank)
            i2 = c % 2
            P.cp("scalar", xb[i2][:, :], PB[bank][:, :], [PB[bank]], [xb[i2]])
            P.tt("vector", t1[i2][:, :], PB[bank][:, :], rp[:, 0, :], ALU.mult, [PB[bank], rp], [t1[i2]])

        def ropeqk(c):
            i2 = c % 2
            rb = 4 + (c % 2)
            P.mm(PB[rb][:, :], rotm[:, :], xb[i2][:, :], True, True, [rotm, xb[i2]], [PB[rb]])
            P.tt("vector", t2[i2][:, :], PB[rb][:, :], rp[:, 1, :], ALU.mult, [PB[rb], rp], [t2[i2]])
            r3 = ro[c % 3]
            P.tt("gpsimd", r3[:, :], t1[i2][:, :], t2[i2][:, :], ALU.add, [t1[i2], t2[i2]], [r3])
            if c < 4:
                P.dma("gpsimd", qT_d[s, c, :, t0:t0 + 512], r3[:, :], [r3], [u_qT[ug]], nowaw=True)
            else:
                P.dma("gpsimd", kT_d[s, c - 4, :, t0:t0 + 512], r3[:, :], [r3], [u_kT[ug]], nowaw=True)

        projqk(0)
        for c in range(8):
            if c + 1 < 8:
                projqk(c + 1)
            ropeqk(c)
        if tt == 0:
            P.memset("vector", gh[:, :, 0:3], 0.0, [gh])
        P.memset("vector", ssqk[:, :, :], 0.0, [ssqk])
        for c in range(12):
            bank = 2 + (c % 2)
            proj_fm(1536 + c * 128, bank)
            P.cp("scalar", gh[:, c, 3:515], PB[bank][:, :], [PB[bank]], [gh], nowaw=True)
        def gconv(c):
            bank = 4 + (c % 2)
            for j in range(4):
                P.mm(PB[bank][:, :], gdiag[:, c, j, :], gh[:, c, j:j + 512], j == 0, j == 3, [gdiag, gh], [PB[bank]],
                     nowaw=(j > 0))
            P.act(gs[c % 3][:, :], PB[bank][:, :], AF.Silu, [PB[bank]], [gs[c % 3]])

        gconv(0)
        for c in range(12):
            if c + 1 < 12:
                gconv(c + 1)
            g3 = gs[c % 3]
            h = c % 4
            if c < 4:
                P.dma("gpsimd", gqT_d[s, h, :, t0:t0 + 512], g3[:, :], [g3], [u_gqT[ug]], nowaw=True)
                tb = 6 + (c % 2)
                for i in range(4):
                    P.tr(pbf(tb)[:, i * 128:(i + 1) * 128], g3[:, i * 128:(i + 1) * 128], identb[:, :],
                         [g3, identb], [PB[tb]], nowaw=(i > 0))
                for i in range(4):
                    P.act(junk[:, 0:128], pbf(tb)[:, i * 128:(i + 1) * 128], AF.Square, [PB[tb], ssqk], [junk, ssqk],
                          accum=ssqk[:, i, 4 + h:5 + h], nowaw=True)
            elif c < 8:
                tb = 6 + (c % 2)
                for i in range(4):
                    P.tr(pbf(tb)[:, i * 128:(i + 1) * 128], g3[:, i * 128:(i + 1) * 128], identb[:, :],
                         [g3, identb], [PB[tb]], nowaw=(i > 0))
                for i in range(4):
                    P.act(junk[:, 0:128], pbf(tb)[:, i * 128:(i + 1) * 128], AF.Square, [PB[tb], ssqk], [junk, ssqk],
                          accum=ssqk[:, i, h:h + 1], nowaw=True)
                rsqrt_(rk1[:, :, h:h + 1], ssqk[:, :, h:h + 1], 1.0, negh[:, 0:4].unsqueeze(2), [ssqk], [rk1])
                for i in range(4):
                    P.ts("vector", kn_t[:, i, h * 128:(h + 1) * 128], pbf(tb)[:, i * 128:(i + 1) * 128],
                         rk1[:, i, h:h + 1], None, ALU.mult, None, [PB[tb], rk1], [kn_t], nowaw=True)
                kT_ = knTt[c % 2]
                tb2 = 2 + (c % 2)
                for i in range(4):
                    P.tr(pbf(tb2)[:, i * 128:(i + 1) * 128], kn_t[:, i, h * 128:(h + 1) * 128], identb[:, :],
                         [kn_t, identb], [PB[tb2]], nowaw=(i > 0))
                P.cp("vector", kT_[:, :], pbf(tb2)[:, 0:512], [PB[tb2]], [kT_])
                P.dma("gpsimd", gkT_d[s, h, :, t0:t0 + 512], kT_[:, :], [kT_], [u_gkT[ug]], nowaw=True)
            else:
                tb = 6 + (c % 2)
                for i in range(4):
                    P.tr(pbf(tb)[:, i * 128:(i + 1) * 128], g3[:, i * 128:(i + 1) * 128], identb[:, :],
                         [g3, identb], [PB[tb]], nowaw=(i > 0))
                P.cp("vector", v_t[:, :, h * 128:(h + 1) * 128],
                     pbf(tb)[:, 0:512].rearrange("p (i d) -> p i d", i=4), [PB[tb]], [v_t], nowaw=True)
        P.cp("gpsimd", gh[:, :, 0:3], gh[:, :, 512:515], [gh], [gh])
        P.dma("gpsimd", gkn_d[s, t0:t0 + 512, :].rearrange("(j p) f -> p j f", p=128), kn_t[:, :, :], [kn_t], [u_gkn[ug]])
        P.dma("gpsimd", gv_d[s, t0:t0 + 512, :].rearrange("(j p) f -> p j f", p=128), v_t[:, :, :], [v_t], [u_gv[ug]])
        for i in range(4):
            bank = 2 + (i % 2)
            for k in range(8):
                P.mm(PB[bank][:, :], xnT[:, k, i * 128:(i + 1) * 128], Win[:, k, 1024:1536], k == 0, k == 7,
                     [Win, xnT], [PB[bank]], nowaw=(k > 0))
            P.cp(ev_eng(), va_t[:, i, :], PB[bank][:, :], [PB[bank]], [va_t], nowaw=True)
            bank = 4 + (i % 2)
            for k in range(8):
                P.mm(PB[bank][:, :], xnT[:, k, i * 128:(i + 1) * 128], Win[:, k, 3072:3584], k == 0, k == 7,
                     [Win, xnT], [PB[bank]], nowaw=(k > 0))
            P.act(z_t[:, i, :], PB[bank][:, :], AF.Silu, [PB[bank]], [z_t], nowaw=True)
        for i in range(4):
            for k in range(8):
                P.mm(PB[6][:, i * 8:(i + 1) * 8], xnT[:, k, i * 128:(i + 1) * 128], Win[:, k, 3584:3592], k == 0, k == 7,
                     [Win, xnT], [PB[6]], nowaw=not (i == 0 and k == 0))
        P.cp("vector", ba_t[:, :, :], PB[6][:, 0:32].rearrange("p (i e) -> p i e", i=4), [PB[6]], [ba_t])
        P.dma("gpsimd", vda_d[s, t0:t0 + 512, :].rearrange("(j p) f -> p j f", p=128), va_t[:, :, :], [va_t], [u_vda[ug]])
        P.dma("gpsimd", gz_d[s, t0:t0 + 512, :].rearrange("(j p) f -> p j f", p=128), z_t[:, :, :], [z_t], [u_gz[ug]])
        rsqrt_(gsc_t[:, :, 4:8], ssqk[:, :, 4:8], 1.0, negh[:, 0:16].rearrange("p (a b) -> p a b", a=4), [ssqk], [gsc_t])
        P.ts("vector", gsc_t[:, :, 4:8], gsc_t[:, :, 4:8], 128.0 ** -0.5, None, ALU.mult, None, [gsc_t], [gsc_t])
        P.cp("vector", gsc_t[:, :, 0:4], rk1[:, :, 0:4], [rk1], [gsc_t])
        P.act(gsc_t[:, :, 8:12], ba_t[:, :, 0:4], AF.Sigmoid, [ba_t], [gsc_t])
        P.tt("vector", tmp4[:, :, :], ba_t[:, :, 4:8], dtbB[:, :].unsqueeze(1).to_broadcast([128, 4, 4]), ALU.add,
             [ba_t, dtbB], [tmp4])
        P.act(tmp4[:, :, :], tmp4[:, :, :], AF.Exp, [tmp4], [tmp4])
        P.act(tmp4[:, :, :], tmp4[:, :, :], AF.Ln, [tmp4], [tmp4], bias=1.0)
        P.tt("vector", gsc_t[:, :, 12:16], tmp4[:, :, :], negA[:, :].unsqueeze(1).to_broadcast([128, 4, 4]), ALU.mult,
             [tmp4, negA], [gsc_t])
        P.dma("gpsimd", gsc_d[s, t0:t0 + 512, :].rearrange("(j p) f -> p j f", p=128), gsc_t[:, :, :], [gsc_t], [u_gsc[ug]])


    if stop_after == "A":
        P.emit()
        return nc

    def phase_c():
        areset(0)
        knT_c = aalloc("c_knT", [4, 512], BF16)
        kn_c = aalloc("c_kn", [8, 512], BF16)
        v_c = aalloc("c_v", [8, 512], BF16)
        sc_c = aalloc("c_sc", [8, 16], F32)
        g_c = aalloc("c_g", [32], F32)
        gcum = aalloc("c_gcum", [32], F32)
        egc = aalloc("c_egc", [32], F32)
        kdc = aalloc("c_kdc", [32], F32)
        beta_c = aalloc("c_beta", [32], F32)
        SETS = []
        for k_ in range(2):
            SETS.append(dict(
                gU=aalloc("c_gU%d" % k_, [8, 64], F32), Gt=aalloc("c_Gt%d" % k_, [8, 64], F32),
                tSU=aalloc("c_tSU%d" % k_, [8, 64], F32), tU=aalloc("c_tU%d" % k_, [8, 64], F32),
                Pm=[aalloc("c_P%d_%d" % (i, k_), [8, 64], F32) for i in range(2)],
                PTm=[aalloc("c_PT%d_%d" % (i, k_), [8, 64], F32) for i in range(2)],
                Am=aalloc("c_A%d" % k_, [8, 64], F32), Wb=aalloc("c_Wb%d" % k_, [8, 64], BF16),
                kg_b=aalloc("c_kg%d" % k_, [8, 128], BF16), b0=PB[2 * k_], b1=PB[2 * k_ + 1]))
        qT_c2 = [aalloc("c_qT%d" % i, [4, 512], BF16) for i in range(2)]
        z_c2 = [aalloc("c_z%d" % i, [8, 512], BF16) for i in range(2)]
        wT_all2 = [aalloc("c_wT%d" % i, [32, 64], BF16) for i in range(2)]
        ub_all2 = [aalloc("c_ub%d" % i, [32, 128], F32) for i in range(2)]
        Aq_all2 = [aalloc("c_Aq%d" % i, [32, 64], BF16) for i in range(2)]
        kdec_all2 = [aalloc("c_kdec%d" % i, [32, 128], BF16) for i in range(2)]
        egl2 = [aalloc("c_egl%d" % i, [32], F32) for i in range(2)]
        rq_c2 = [aalloc("c_rq%d" % i, [32], F32) for i in range(2)]
        rqe2 = [aalloc("c_rqe%d" % i, [32], F32) for i in range(2)]
        nbeta2 = [aalloc("c_nbeta%d" % i, [32], F32) for i in range(2)]
        S = [aalloc("c_S%d" % i, [128], F32) for i in range(4)]
        Sb = [aalloc("c_Sb%d" % i, [128], BF16) for i in range(4)]
        vnew = [aalloc("c_vn%d" % i, [128], BF16) for i in range(4)]
        t1c = [aalloc("c_t1%d" % i, [128], F32) for i in range(4)]
        obuf = aalloc("c_o", [8, 512], F32)
        osq = aalloc("c_osq", [8, 512], BF16)
        ssqg = aalloc("c_ssqg", [32], F32)
        rstdg = aalloc("c_rstdg", [32], F32)
        mixg = aalloc("c_mixg", [8, 512], BF16)
        gnwB = aalloc("c_gnwB", [128], F32)
        P.dma("sync", gnwB[:, :], bc(gnw_d, 128), [CONST], [gnwB])
        H = slice(0, 64)

        def b3(ap2, n):
            return ap2.unsqueeze(2).to_broadcast([64, 8, n])

        def pre(g):
            s, tt = tiles[g]
            t0 = tt * 512
            ug = s * NT + tt
            qT_c, z_c, wT_all, ub_all = qT_c2[g % 2], z_c2[g % 2], wT_all2[g % 2], ub_all2[g % 2]
            Aq_all, kdec_all, egl, rq_c, rqe, nbeta = (Aq_all2[g % 2], kdec_all2[g % 2], egl2[g % 2], rq_c2[g % 2],
                                                       rqe2[g % 2], nbeta2[g % 2])
            P.dma("sync", knT_c[:, :, :], gkT_d[s, :, :, t0:t0 + 512].rearrange("h p t -> p h t"), [u_gkT[ug]], [knT_c])
            P.dma("sync", qT_c[:, :, :], gqT_d[s, :, :, t0:t0 + 512].rearrange("h p t -> p h t"), [u_gqT[ug]], [qT_c])
            P.dma("sync", kn_c[H, :, :], gkn_d[s, t0:t0 + 512, :].rearrange("(n c) f -> c n f", c=64), [u_gkn[ug]], [kn_c])
            P.dma("sync", v_c[H, :, :], gv_d[s, t0:t0 + 512, :].rearrange("(n c) f -> c n f", c=64), [u_gv[ug]], [v_c])
            P.dma("sync", z_c[H, :, :], gz_d[s, t0:t0 + 512, :].rearrange("(n c) f -> c n f", c=64), [u_gz[ug]], [z_c])
            P.dma("sync", sc_c[H, :, :], gsc_d[s, t0:t0 + 512, :].rearrange("(n c) f -> c n f", c=64), [u_gsc[ug]], [sc_c])
            yield
            g3 = g_c[H, :].rearrange("p (n h) -> p n h", n=8)
            P.cp("vector", g3, sc_c[H, :, 12:16], [sc_c], [g_c])
            P.cp("vector", beta_c[H, :].rearrange("p (n h) -> p n h", n=8), sc_c[H, :, 8:12], [sc_c], [beta_c])
            P.cp("vector", rq_c[H, :].rearrange("p (n h) -> p n h", n=8), sc_c[H, :, 4:8], [sc_c], [rq_c])
            P.ts("vector", nbeta[H, :], beta_c[H, :], -1.0, None, ALU.mult, None, [beta_c], [nbeta])
            yield
            P.mm(PB[0][H, 0:32], cm[:, 0, :], g_c[H, :], True, True, [cm, g_c], [PB[0]])
            P.mm(PB[1][:, 0:32], ones_f[:, :], g_c[H, :], True, True, [ones_f, g_c], [PB[1]])
            P.cp("vector", gcum[H, :], PB[0][H, 0:32], [PB[0]], [gcum])
            P.act(egc[H, :], PB[0][H, 0:32], AF.Exp, [PB[0]], [egc])
            yield
            P.act(egl[:, :], PB[1][:, 0:32], AF.Exp, [PB[1]], [egl])
            P.tt("vector", kdc[H, :], PB[1][H, 0:32], gcum[H, :], ALU.subtract, [PB[1], gcum], [kdc])
            P.act(kdc[H, :], kdc[H, :], AF.Exp, [kdc], [kdc])
            P.tt("vector", rqe[H, :], rq_c[H, :], egc[H, :], ALU.mult, [rq_c, egc], [rqe])
            yield
            def batch(bb, BS):
                gU, Gt, tSU, tU, Pm, PTm, Am, Wb, kg_b = (BS['gU'], BS['Gt'], BS['tSU'], BS['tU'], BS['Pm'], BS['PTm'],
                                                          BS['Am'], BS['Wb'], BS['kg_b'])
                b0, b1 = BS['b0'], BS['b1']
                ps = slice(bb * 8, bb * 8 + 8)
                pairs = [(2 * bb + q // 4, q % 4) for q in range(8)]
                P.tt("vector", gU[H, :, :], cm[:, 0, :].unsqueeze(1).to_broadcast([64, 8, 64]), b3(g_c[H, ps], 64), ALU.mult,
                     [cm, g_c], [gU])
                for q, (n, h) in enumerate(pairs):
                    P.mm(b0[H, q * 64:(q + 1) * 64], cm[:, 4, :], gU[H, q, :], True, True, [cm, gU], [b0], nowaw=(q > 0))
                yield
                P.act(Gt[H, :, :], b0[H, :].rearrange("p (q i) -> p q i", q=8), AF.Exp, [b0], [Gt])
                for q, (n, h) in enumerate(pairs):
                    ks = knT_c[:, h, n * 64:(n + 1) * 64]
                    P.mm(b1[H, q * 64:(q + 1) * 64], ks, ks, True, True, [knT_c], [b1], nowaw=(q > 0))
                yield
                for q, (n, h) in enumerate(pairs):
                    ks = knT_c[:, h, n * 64:(n + 1) * 64]
                    P.mm(b0[H, q * 64:(q + 1) * 64], ks, qT_c[:, h, n * 64:(n + 1) * 64], True, True, [knT_c, qT_c], [b0],
                         nowaw=(q > 0))
                yield
                m8 = lambda i: cm[:, i, :].unsqueeze(1).to_broadcast([64, 8, 64])
                P.tt("gpsimd", tSU[H, :, :], Gt[H, :, :], m8(1), ALU.mult, [Gt, cm], [tSU])
                P.stt("vector", tSU[H, :, :], tSU[H, :, :], -1.0, b3(beta_c[H, ps], 64), ALU.mult, ALU.mult, [tSU, beta_c], [tSU])
                P.tt("gpsimd", tU[H, :, :], Gt[H, :, :], m8(2), ALU.mult, [Gt, cm], [tU])
                yield
                P0 = Pm[0]
                P.tt("vector", P0[H, :, :], b1[H, :].rearrange("p (q i) -> p q i", q=8), tSU[H, :, :], ALU.mult,
                     [b1, tSU], [P0])
                P.tt("vector", Aq_all[H, ps, :], b0[H, :].rearrange("p (q i) -> p q i", q=8), tU[H, :, :], ALU.mult,
                     [b0, tU], [Aq_all], nowaw=(bb > 0))
                yield
                for q in range(8):
                    P.tr(b1[H, q * 64:(q + 1) * 64], P0[H, q, :], identf[H, H], [P0, identf], [b1], nowaw=(q > 0))
                P.cp("scalar", PTm[0][H, :, :], b1[H, :].rearrange("p (q i) -> p q i", q=8), [b1], [PTm[0]])
                P.tt("gpsimd", Am[H, :, :], P0[H, :, :], m8(3), ALU.add, [P0, cm], [Am])
                yield
                for m in range(5):
                    Pc, PTc = Pm[m % 2], PTm[m % 2]
                    Pn, PTn = Pm[(m + 1) % 2], PTm[(m + 1) % 2]
                    if m < 4:
                        for q in range(8):
                            P.mm(b1[H, q * 64:(q + 1) * 64], PTc[H, q, :], Pc[H, q, :], True, True, [PTc, Pc], [b1],
                                 nowaw=(q > 0))
                        yield
                    for q in range(8):
                        P.mm(b0[H, q * 64:(q + 1) * 64], Pc[H, q, :], PTc[H, q, :], True, True, [PTc, Pc], [b0],
                             nowaw=(q > 0))
                    yield
                    if m < 4:
                        P.cp("vector", Pn[H, :, :], b1[H, :].rearrange("p (q i) -> p q i", q=8), [b1], [Pn])
                    P.cp("scalar", PTn[H, :, :], b0[H, :].rearrange("p (q i) -> p q i", q=8), [b0], [PTn])
                    yield
                    for q in range(8):
                        P.mm(b1[H, q * 64:(q + 1) * 64], PTn[H, q, :], Am[H, q, :], True, True, [PTn, Am], [b1],
                             nowaw=(q > 0))
                    yield
                    P.tt("vector", Am[H, :, :], b1[H, :].rearrange("p (q i) -> p q i", q=8), Am[H, :, :], ALU.add,
                         [b1, Am], [Am])
                    yield
                P.cp("gpsimd", Wb[H, :, :], Am[H, :, :], [Am], [Wb])
                knv = kn_c[H, 2 * bb:2 * bb + 2, :].rearrange("p n (h d) -> p (n h) d", h=4)
                vv = v_c[H, 2 * bb:2 * bb + 2, :].rearrange("p n (h d) -> p (n h) d", h=4)
                P.tt("gpsimd", kg_b[H, :, :], knv, b3(egc[H, ps], 128), ALU.mult, [kn_c, egc], [kg_b])
                P.tt("gpsimd", kdec_all[H, ps, :], knv, b3(kdc[H, ps], 128), ALU.mult, [kn_c, kdc], [kdec_all], nowaw=(bb > 0))
                yield
                for q in range(8):
                    P.mm(b0[:, q * 64:(q + 1) * 64], kg_b[H, q, :], Wb[H, q, :], True, True, [kg_b, Wb], [b0], nowaw=(q > 0))
                yield
                P.cp("scalar", wT_all[:, ps, :], b0[:, :].rearrange("p (q i) -> p q i", q=8), [b0], [wT_all], nowaw=(bb > 0))
                for hf in range(2):
                    bk_ = b1 if hf == 0 else b0
                    for q4 in range(4):
                        q = hf * 4 + q4
                        P.mm(bk_[H, q4 * 128:(q4 + 1) * 128], Wb[H, q, :], vv[:, q, :], True, True, [Wb, v_c], [bk_],
                             nowaw=(q4 > 0))
                    pq = slice(bb * 8 + hf * 4, bb * 8 + hf * 4 + 4)
                    P.tt("vector", ub_all[H, pq, :], bk_[H, :].rearrange("p (q d) -> p q d", q=4),
                         beta_c[H, pq].unsqueeze(2).to_broadcast([64, 4, 128]), ALU.mult, [bk_, beta_c], [ub_all],
                         nowaw=not (bb == 0 and hf == 0))
                    yield

            for pr_ in ((0, 1), (2, 3)):
                alive = [batch(pr_[0], SETS[0]), batch(pr_[1], SETS[1])]
                while alive:
                    for gi in list(alive):
                        if next(gi, "done") == "done":
                            alive.remove(gi)
                        else:
                            yield

        def scan(g):
            s, tt = tiles[g]
            t0 = tt * 512
            ug = s * NT + tt
            qT_c, z_c, wT_all, ub_all = qT_c2[g % 2], z_c2[g % 2], wT_all2[g % 2], ub_all2[g % 2]
            Aq_all, kdec_all, egl, rq_c, rqe, nbeta = (Aq_all2[g % 2], kdec_all2[g % 2], egl2[g % 2], rq_c2[g % 2],
                                                       rqe2[g % 2], nbeta2[g % 2])
            if tt == 0:
                for h in range(4):
                    P.memset("vector", S[h][:, :], 0.0, [S[h]])
                    P.memset("vector", Sb[h][:, :], 0.0, [Sb[h]])
            for n in range(8):
                for h in range(4):
                    p = n * 4 + h
                    bank = 4 + h
                    B_ = PB[bank]
                    P.mm(B_[H, 0:128], wT_all[:, p, :], Sb[h][:, :], True, True, [wT_all, Sb[h]], [B_])
                    P.mm(B_[H, 128:256], qT_c[:, h, n * 64:(n + 1) * 64], Sb[h][:, :], True, True, [qT_c, Sb[h]], [B_], nowaw=True)
                    P.stt("vector", vnew[h][H, :], B_[H, 0:128], nbeta[H, p:p + 1], ub_all[H, p, :], ALU.mult, ALU.add,
                          [B_, nbeta, ub_all], [vnew[h]])
                    P.ts("vector", t1c[h][H, :], B_[H, 128:256], rqe[H, p:p + 1], None, ALU.mult, None, [B_, rqe], [t1c[h]])
                    P.mm(B_[H, 256:384], Aq_all[H, p, :], vnew[h][H, :], True, True, [Aq_all, vnew[h]], [B_])
                    P.mm(B_[:, 384:512], kdec_all[H, p, :], vnew[h][H, :], True, True, [kdec_all, vnew[h]], [B_], nowaw=True)
                    P.stt("vector", obuf[H, n, h * 128:(h + 1) * 128], B_[H, 256:384], rq_c[H, p:p + 1], t1c[h][H, :],
                          ALU.mult, ALU.add, [B_, rq_c, t1c[h]], [obuf], nowaw=not (n == 0 and h == 0))
                    P.stt("vector", S[h][:, :], S[h][:, :], egl[:, p:p + 1], B_[:, 384:512], ALU.mult, ALU.add,
                          [S[h], egl, B_], [S[h]])
                    P.cp("gpsimd", Sb[h][:, :], S[h][:, :], [S[h]], [Sb[h]])
                    yield
            o3 = obuf[H, :, :].rearrange("p n (h d) -> p (n h) d", h=4)
            P.tt("gpsimd", osq[H, :, :], obuf[H, :, :], obuf[H, :, :], ALU.mult, [obuf], [osq])
            P.op("vector", lambda e: e.reduce_sum(out=ssqg[H, :], in_=osq[H, :, :].rearrange("p n (h d) -> p (n h) d", h=4), axis=AX.X),
                 [osq], [ssqg])
            rsqrt_(rstdg[H, :], ssqg[H, :], 1.0 / 128, negh[H, 0:32], [ssqg], [rstdg])
            yield
            P.tt("vector", o3, o3, rstdg[H, :].unsqueeze(2).to_broadcast([64, 32, 128]), ALU.mult, [obuf, rstdg], [obuf])
            P.tt("gpsimd", o3, o3, gnwB[H, :].unsqueeze(1).to_broadcast([64, 32, 128]), ALU.mult, [obuf, gnwB], [obuf])
            yield
            P.tt("vector", mixg[H, :, :], obuf[H, :, :], z_c[H, :, :], ALU.mult, [obuf, z_c], [mixg])
            P.dma("gpsimd", mix_d[s, t0:t0 + 512, 512:1024].rearrange("(n c) f -> c n f", c=64), mixg[H, :, :], [mixg],
                  [u_mixg[ug]])
            yield

        for _ in pre(0):
            pass
        KADV = 5
        for g in range(len(tiles)):
            gn = pre(g + 1) if g + 1 < len(tiles) else None
            for _ in scan(g):
                if gn is not None:
                    for _k in range(KADV):
                        if next(gn, "done") == "done":
                            gn = None
                            break
            if gn is not None:
                for _ in gn:
                    pass

    areset(0)
    qTb = [aalloc("qTb%d" % i, [T], BF16) for i in range(2)]
    kTb = [aalloc("kTb%d" % i, [T], BF16) for i in range(2)]
    vAb = [aalloc("vAb%d" % i, [NKT, 130], BF16) for i in range(2)]
    pT = [aalloc("pT%d" % i, [512], BF16) for i in range(3)]
    o1 = aalloc("o1", [4, 128], F32)
    o2 = aalloc("o2", [4, 128], F32)
    rinv = aalloc("rinv", [4], F32)
    rinv2 = aalloc("rinv2", [4], F32)
    ssqo = aalloc("ssqo", [4], F32)
    rstdo = aalloc("rstdo", [4], F32)
    mixt = [aalloc("mixt%d" % i, [4, 128], BF16) for i in range(2)]
    lamb = aalloc("lamb", [4, 64], F32)
    lamp = aalloc("lamp", [2, 64], F32)
    lams = aalloc("lams", [2], F32)
    neglam = aalloc("neglam", [1], F32)
    sublnB = aalloc("sublnB", [128], F32)
    junkB = aalloc("junkB", [128], F32)

    for i, dd in enumerate((lq1_d, lk1_d, lq2_d, lk2_d)):
        P.dma("sync", lamb[:, i, :], bc(dd, 128), [CONST], [lamb], nowaw=True)
    P.dma("sync", sublnB[:, :], bc(subln_d, 128), [CONST], [sublnB])
    P.ts("vector", sublnB[:, :], sublnB[:, :], 1.0 - LAMBDA_INIT, None, ALU.mult, None, [sublnB], [sublnB])
    P.tt("vector", lamp[:, 0, :], lamb[:, 0, :], lamb[:, 1, :], ALU.mult, [lamb], [lamp], nowaw=True)
    P.tt("vector", lamp[:, 1, :], lamb[:, 2, :], lamb[:, 3, :], ALU.mult, [lamb], [lamp], nowaw=True)
    P.op("vector", lambda e: e.reduce_sum(out=lams[:, :], in_=lamp[:, :, :], axis=AX.X), [lamp], [lams])
    P.act(lams[:, :], lams[:, :], AF.Exp, [lams], [lams])
    P.tt("vector", neglam[:, :], lams[:, 1:2], lams[:, 0:1], ALU.subtract, [lams], [neglam])
    P.ts("vector", neglam[:, :], neglam[:, :], -LAMBDA_INIT, None, ALU.add, None, [neglam], [neglam])
    for i in range(2):
        P.memset("vector", vAb[i][:, :, 128:130], 1.0, [vAb[i]])

    heads = [(s, h) for s in range(NSEQ) for h in range(4)]

    def loadB(n):
        s, h = heads[n]
        i = n % 2
        rd = [u_qT[s * NT + tt] for tt in range(NT)]
        P.dma("sync", qTb[i][:, :], qT_d[s, h, :, :], rd, [qTb[i]])
        rd = [u_kT[s * NT + tt] for tt in range(NT)]
        P.dma("sync", kTb[i][:, :], kT_d[s, h, :, :], rd, [kTb[i]])
        rd = [u_vda[s * NT + tt] for tt in range(NT)]
        nq = max(1, NKT // 8)
        for a in range(0, NKT, nq):
            P.dma("sync", vAb[i][:, a:a + nq, 0:128],
                  vda_d[s, a * 128:(a + nq) * 128, h * 128:(h + 1) * 128].rearrange("(n p) d -> p n d", p=128),
                  rd, [vAb[i]], nowaw=(a > 0))

    loadB(0)
    kidx = 0
    ucnt = 0
    pending = [None]

    def flush():
        if pending[0] is not None:
            f = pending[0]
            pending[0] = None
            f()

    for n, (s, h) in enumerate(heads):
        flush()
        if n + 1 < len(heads):
            loadB(n + 1)
        qb, kb, vb = qTb[n % 2], kTb[n % 2], vAb[n % 2]
        for qt in range(NT):
            for c in range(2):
                pob = (4, 5) if ucnt % 2 == 0 else (6, 7)
                ucnt += 1
                fresh = {pob[0]: True, pob[1]: True}
                nk = 4 * qt + 4
                cs = slice(c * 64, (c + 1) * 64)
                for kt in range(nk):
                    j = kt - 4 * qt
                    sbk = PB[kidx % 3]
                    pt = pT[kidx % 3]
                    kidx += 1
                    ksl = kb[cs, kt * 128:(kt + 1) * 128]
                    q0 = qt * 512
                    if j < 0:
                        lo = 0
                        P.mm(sbk[:, 0:512], ksl, qb[cs, q0:q0 + 512], True, True, [kb, qb], [sbk])
                    else:
                        lo = j * 128
                        P.mm(sbk[:, lo:lo + 128], ksl, qb[cs, q0 + lo:q0 + lo + 128], True, False, [kb, qb], [sbk])
                        P.mm(sbk[:, lo:lo + 128], identb[:, :], trim[:, :], False, True, [identb, trim], [sbk], nowaw=True)
                        if lo + 128 < 512:
                            P.mm(sbk[:, lo + 128:512], ksl, qb[cs, q0 + lo + 128:q0 + 512], True, True, [kb, qb], [sbk],
                                 nowaw=True)
                    P.act(pt[:, lo:512], sbk[:, lo:512], AF.Exp, [sbk], [pt], scale=0.125)
                    flush()

                    def pv(kt=kt, j=j, pt=pt, vb=vb, pob=pob, fresh=fresh, qt=qt, c=c, nk=nk, s=s, h=h, n=n):
                        for i in range(max(j, 0), 4):
                            bank = pob[i // 2]
                            off = (i % 2) * 130
                            P.mm(PB[bank][:, off:off + 129], pt[:, i * 128:(i + 1) * 128], vb[:, kt, 0:129],
                                 fresh[bank], (kt == 4 * qt + i and i % 2 == 1), [pt, vb], [PB[bank]], nowaw=not fresh[bank])
                            fresh[bank] = False
                        if kt != nk - 1:
                            return
                        for i in range(4):
                            bank = pob[i // 2]
                            off = (i % 2) * 130
                            P.op("vector", (lambda bank=bank, off=off, i=i: (lambda e: e.reciprocal(out=rinv[:, i:i + 1], in_=PB[bank][:, off + 128:off + 129])))(),
                                 [PB[bank]], [rinv], nowaw=(i > 0))
                            if c == 0:
                                P.ts("vector", o1[:, i, :], PB[bank][:, off:off + 128], rinv[:, i:i + 1], None, ALU.mult, None,
                                     [PB[bank], rinv], [o1], nowaw=(i > 0))
                            else:
                                P.ts("vector", rinv2[:, i:i + 1], rinv[:, i:i + 1], neglam[:, 0:1], None, ALU.mult, None,
                                     [rinv, neglam], [rinv2], nowaw=(i > 0))
                                P.stt("vector", o2[:, i, :], PB[bank][:, off:off + 128], rinv2[:, i:i + 1], o1[:, i, :],
                                      ALU.mult, ALU.add, [PB[bank], rinv2, o1], [o2], nowaw=(i > 0))
                        if c == 0:
                            return
                        mt = mixt[(n * NT + qt) % 2]
                        P.memset("vector", ssqo[:, :], 0.0, [ssqo])
                        for i in range(4):
                            P.act(junkB[:, :], o2[:, i, :], AF.Square, [o2, ssqo], [junkB, ssqo], accum=ssqo[:, i:i + 1], nowaw=True)
                        rsqrt_(rstdo[:, :], ssqo[:, :], 1.0 / 128, negh[:, 0:4], [ssqo], [rstdo])
                        for i in range(4):
                            P.stt("vector", mt[:, i, :], o2[:, i, :], rstdo[:, i:i + 1], sublnB[:, :], ALU.mult, ALU.mult,
                                  [o2, rstdo, sublnB], [mt], nowaw=(i > 0))
                        P.dma("gpsimd", mix_d[s, qt * 512:(qt + 1) * 512, h * 128:(h + 1) * 128].rearrange("(i p) d -> p i d", p=128),
                              mt[:, :, :], [mt], [u_mixa[s * NT + qt]], nowaw=True)

                    pending[0] = pv
    flush()

    if stop_after == "B":
        P.emit()
        return nc
    if do_c:
        phase_c()
    if stop_after == "C":
        P.emit()
        return nc

    areset(0)
    Wout = aalloc("Wout", [8, D], BF16)
    xh = [aalloc("xh%d" % i, [4, D], F32) for i in range(2)]
    wst = [xh[1][:, 0, :], xh[1][:, 1, :]]
    mixb = aalloc("mixb", [4, D], BF16)
    mixT = aalloc("mixTD", [8, 512], BF16)
    wup = [aalloc("wup%d" % i, [2, 8, 256], BF16) for i in range(3)]
    wdn = [aalloc("wdn%d" % i, [11, 512], BF16) for i in range(2)]
    aT = aalloc("aT", [22, 512], BF16)
    hp = [aalloc("hp%d" % i, [4, 514], BF16) for i in range(2)]
    halo = aalloc("halo", [11, 4, 2], BF16)
    fdiag = aalloc("fdiag", [44, 3, 128], BF16)
    fcw = aalloc("fcw", [44, 3], F32)
    gt = [aalloc("gt%d" % i, [512], BF16) for i in range(4)]
    nwF = aalloc("nwF", [D], F32)
    nwO = aalloc("nwO", [D], F32)
    junkD = aalloc("junkD", [D], BF16)
    ssq2 = aalloc("ssq2", [4], F32)
    rstd2 = aalloc("rstd2", [4], F32)

    for k in range(8):
        P.dma("sync", wst[k % 2], wout_d[k * 128:(k + 1) * 128, :], [CONST], [xh[1]])
        P.cp(cast_engs[k % 2], Wout[:, k, :], wst[k % 2], [xh[1]], [Wout], nowaw=True)
    P.dma("sync", fcw[:, :, :], fcw_d[:, :, :], [CONST], [fcw])
    di = 0
    for cc in range(44):
        for j in range(3):
            eng_ = "vector" if di % 2 == 0 else "gpsimd"
            di += 1
            P.ts(eng_, fdiag[:, cc, j, :], identb[:, :], fcw[:, cc, j:j + 1], 1.0, ALU.mult, ALU.mult,
                 [identb, fcw], [fdiag], nowaw=True)
    P.dma("sync", nwF[:, :], bc(fnw_d, 128), [CONST], [nwF])
    P.dma("sync", nwO[:, :], bc(finw_d, 128), [CONST], [nwO])

    def loadD(g):
        s, tt = tiles[g]
        t0 = tt * 512
        P.dma("sync", xh[g % 2][:, :, :], x_d[s, t0:t0 + 512, :].rearrange("(j p) d -> p j d", p=128), [CONST], [xh[g % 2]])

    def load_wup(grp):
        sl = wup[grp % 3]
        P.dma("sync", sl[:, 0, :, :], wupb_d[:, grp * 256:(grp + 1) * 256].rearrange("(k p) c -> p k c", p=128),
              [u_wupb], [sl])
        P.dma("sync", sl[:, 1, :, :], wupb_d[:, DFF + grp * 256:DFF + (grp + 1) * 256].rearrange("(k p) c -> p k c", p=128),
              [u_wupb], [sl], nowaw=True)

    def load_wdn(idx):
        half, piece = idx // 2, idx % 2
        sl = wdn[idx % 2]
        P.dma("sync", sl[:, :, :],
              wdnb_d[piece * 11 * 128:(piece + 1) * 11 * 128, half * 512:(half + 1) * 512].rearrange("(f p) c -> p f c", p=128),
              [u_wdnb], [sl])

    def rms_to(src, dst, nw):
        P.memset("vector", ssq2[:, :], 0.0, [ssq2])
        for j in range(4):
            P.act(junkD[:, :], src[:, j, :], AF.Square, [src, ssq2], [junkD, ssq2], accum=ssq2[:, j:j + 1], nowaw=True)
        rsqrt_(rstd2[:, :], ssq2[:, :], 1.0 / D, negh[:, 0:4], [ssq2], [rstd2])
        for j in range(4):
            P.stt("vector", dst[:, j, :], src[:, j, :], rstd2[:, j:j + 1], nw[:, :], ALU.mult, ALU.mult,
                  [src, rstd2, nw], [dst], nowaw=(j > 0))

    def transpose_to(src, dst):
        for k in range(8):
            bk = k % 2
            for j in range(4):
                P.tr(pbf(bk)[:, j * 128:(j + 1) * 128], src[:, j, k * 128:(k + 1) * 128], identb[:, :],
                     [src, identb], [PB[bk]], nowaw=(j > 0))
            P.cp(ev_eng(), dst[:, k, :], pbf(bk)[:, 0:512], [PB[bk]], [dst], nowaw=(k > 0))

    loadD(0)
    for g, (s, tt) in enumerate(tiles):
        t0 = tt * 512
        ug = s * NT + tt
        if g + 1 < len(tiles):
            loadD(g + 1)
        xt = xh[g % 2]
        rdm = [u_mixa[ug]] + ([u_mixg[ug]] if do_c else [])
        P.dma("sync", mixb[:, :, :], mix_d[s, t0:t0 + 512, :].rearrange("(j p) d -> p j d", p=128), rdm, [mixb])
        load_wup(0)
        load_wup(1)
        transpose_to(mixb, mixT)
        bi = 0
        for j in range(4):
            for hf in range(2):
                bank = 2 + (bi % 2)
                bi += 1
                for k in range(8):
                    P.mm(PB[bank][:, :], mixT[:, k, j * 128:(j + 1) * 128], Wout[:, k, hf * 512:(hf + 1) * 512],
                         k == 0, k == 7, [mixT, Wout], [PB[bank]], nowaw=(k > 0))
                P.tt("vector", xt[:, j, hf * 512:(hf + 1) * 512], PB[bank][:, :], xt[:, j, hf * 512:(hf + 1) * 512], ALU.add,
                     [PB[bank], xt], [xt])
        rms_to(xt, mixb, nwF)
        transpose_to(mixb, mixT)
        if tt == 0:
            P.memset("vector", halo[:, :, :, :], 0.0, [halo])
        def up(grp):
            if grp + 2 < 11:
                load_wup(grp + 2)
            if grp == 9:
                load_wdn(0)
            if grp == 10:
                load_wdn(1)
            sl = wup[grp % 3]
            hpb = hp[grp % 2]
            P.cp("gpsimd", hpb[:, :, 0:2], halo[:, grp, :, :], [halo], [hpb])
            for q in range(4):
                bank = 2 + (q % 2)
                for k in range(8):
                    P.mm(PB[bank][:, :], sl[:, q // 2, k, (q % 2) * 128:(q % 2) * 128 + 128], mixT[:, k, :], k == 0, k == 7,
                         [sl, mixT], [PB[bank]], nowaw=(k > 0))
                P.cp(ev_eng(), hpb[:, q, 2:514], PB[bank][:, :], [PB[bank]], [hpb], nowaw=True)

        def conv(grp):
            hpb = hp[grp % 2]
            chunks = [2 * grp, 2 * grp + 1, 22 + 2 * grp, 23 + 2 * grp]
            for q in range(4):
                bank = 4 + (q % 2)
                for j in range(3):
                    P.mm(PB[bank][:, :], fdiag[:, chunks[q], j, :], hpb[:, q, j:j + 512], j == 0, j == 2, [fdiag, hpb], [PB[bank]],
                         nowaw=(j > 0))
                if q < 2:
                    P.act(gt[(grp % 2) * 2 + q][:, :], PB[bank][:, :], AF.Silu, [PB[bank]], [gt[(grp % 2) * 2 + q]])
                else:
                    P.tt("vector", aT[:, 2 * grp + q - 2, :], PB[bank][:, :], gt[(grp % 2) * 2 + q - 2][:, :], ALU.mult,
                         [PB[bank], gt[(grp % 2) * 2 + q - 2]], [aT], nowaw=True)
            P.cp("gpsimd", halo[:, grp, :, :], hpb[:, :, 512:514], [hpb], [halo], nowaw=True)

        up(0)
        for grp in range(11):
            if grp + 1 < 11:
                up(grp + 1)
            conv(grp)
        for idx in range(4):
            half, piece = idx // 2, idx % 2
            sl = wdn[idx % 2]
            for f in range(11):
                fc = piece * 11 + f
                for j in range(4):
                    P.mm(PB[4 + j][:, :], aT[:, fc, j * 128:(j + 1) * 128], sl[:, f, :], fc == 0, fc == 21,
                         [aT, sl], [PB[4 + j]], nowaw=(fc > 0))
            if idx + 2 < 4:
                load_wdn(idx + 2)
            if piece == 1:
                for j in range(4):
                    P.tt("vector", xt[:, j, half * 512:(half + 1) * 512], PB[4 + j][:, :],
                         xt[:, j, half * 512:(half + 1) * 512], ALU.add, [PB[4 + j], xt], [xt])
        rms_to(xt, xt, nwO)
        P.dma("gpsimd", out_d[s, t0:t0 + 512, :].rearrange("(j p) d -> p j d", p=128), xt[:, :, :], [xt], [CONST_OUT])

    P.emit()
    return nc


def host_consts(T):
    identf = np.eye(128, dtype=np.float32)
    identb = identf.astype(ml_dtypes.bfloat16)
    rot = np.zeros((128, 128), np.float32)
    for g in range(2):
        for d in range(32):
            rot[g * 64 + d + 32, g * 64 + d] = -1.0
            rot[g * 64 + d, g * 64 + d + 32] = 1.0
    inv_freq = (10000.0 ** (-np.arange(0, 64, 2, dtype=np.float32) / 64)).astype(np.float32)
    ang = np.arange(T, dtype=np.float32)[None, :] * inv_freq[:, None]
    cos = np.cos(ang).astype(np.float32); sin = np.sin(ang).astype(np.float32)
    rope = np.zeros((128, 2, T), np.float32)
    for r in range(128):
        rope[r, 0] = cos[r % 32]
        rope[r, 1] = sin[r % 32]
    kk = np.arange(128)[:, None]; qq = np.arange(128)[None, :]
    trim = np.where(kk <= qq, 0.0, -30000.0).astype(np.float32).astype(ml_dtypes.bfloat16)
    s = np.arange(64)[:, None]; i = np.arange(64)[None, :]
    cm = np.zeros((64, 5, 64), np.float32)
    cm[:, 0] = (s <= i); cm[:, 1] = (s < i); cm[:, 2] = (s <= i); cm[:, 3] = (s == i); cm[:, 4] = (s > i)
    return {"c_identb": identb, "c_identf": identf, "c_rot": rot.astype(ml_dtypes.bfloat16), "c_rope": rope,
            "c_trimask": trim, "c_masks": cm}


_W1 = ["attn_norm_w", "w_in", "da_lambda_q1", "da_lambda_k1", "da_lambda_q2", "da_lambda_k2", "da_subln_w",
       "gdn_conv_w", "gdn_a_log", "gdn_dt_bias", "gdn_norm_w", "w_out", "ffn_norm_w", "ffn_w_up", "ffn_conv_w",
       "ffn_w_down"]


def make_in_maps(inputs, n_cores, nseq, T):
    consts = host_consts(T)
    base = {k: np.ascontiguousarray(np.asarray(inputs[k], np.float32)[0]) for k in _W1}
    base["final_norm_w"] = np.ascontiguousarray(np.asarray(inputs["final_norm_w"], np.float32))
    base["gdn_conv_w"] = np.ascontiguousarray(base["gdn_conv_w"].reshape(4, 12, 128).transpose(2, 1, 0))
    base["ffn_conv_w"] = np.ascontiguousarray(base["ffn_conv_w"].reshape(3, 44, 128).transpose(2, 1, 0))
    base.update(consts)
    x = np.asarray(inputs["x"], np.float32)
    maps = []
    for c in range(n_cores):
        m = dict(base)
        m["x"] = np.ascontiguousarray(x[c * nseq:(c + 1) * nseq])
        maps.append(m)
    return maps


def kernel(**inputs):
    x = inputs["x"]
    B, T, _ = x.shape
    n = 8
    nseq = B // n
    nc = build(T, nseq)
    maps = make_in_maps(inputs, n, nseq, T)
    res = run_bass_kernel_spmd(nc, maps, core_ids=list(range(n)))
    return np.concatenate([r["out"] for r in res.results], axis=0)
```

```python
import contextlib
import math
import numpy as np
import ml_dtypes
import concourse.bass as bass
import concourse.mybir as mybir
from concourse.bass_utils import run_bass_kernel_spmd

F32 = mybir.dt.float32
BF16 = mybir.dt.bfloat16
AF = mybir.ActivationFunctionType
ALU = mybir.AluOpType
AX = mybir.AxisListType

ENGS = ("sync", "scalar", "gpsimd", "vector", "tensor")
NDMA = 8
EPS = 1e-6
D = 1024
DFF = 2816
INC = 3592
LAMBDA_INIT = 0.8 - 0.6 * math.exp(-0.3 * 0)


class Buf:
    def __init__(self, name, t):
        self.name = name
        self.t = t
        self.writers = []
        self.readers = []
        self.gen_deps = set()
        self.psum = False

    def __getitem__(self, idx):
        return self.t[idx]


class Op:
    __slots__ = ("eng", "fn", "deps", "is_dma", "signal", "tok", "idx")

    def __init__(self, eng, fn, is_dma):
        self.eng = eng
        self.fn = fn
        self.deps = set()
        self.is_dma = is_dma
        self.signal = False
        self.tok = None


class Prog:
    def __init__(self, nc):
        self.nc = nc
        self.ops = []

    def sb(self, name, shape, dt=F32):
        return Buf(name, self.nc.alloc_sbuf_tensor(name, list(shape), dt))

    def ps(self, name, shape, dt=F32):
        b = Buf(name, self.nc.alloc_psum_tensor(name, list(shape), dt))
        b.psum = True
        return b

    def dram(self, name, shape, dt=F32, kind="Internal"):
        return Buf(name, self.nc.dram_tensor(name, list(shape), dt, kind=kind))

    def op(self, eng, fn, reads=(), writes=(), dma=False, nowaw=False):
        o = Op(eng, fn, dma)
        o.idx = len(self.ops)
        deps = o.deps
        for r in reads:
            deps.update(r.writers)
            if r.psum:
                for ri in r.readers:
                    if self.ops[ri].eng != eng:
                        deps.add(ri)
        for w in writes:
            if nowaw and w.writers:
                deps.update(w.gen_deps)
                deps.update(w.readers)
                w.gen_deps.update(w.readers)
                w.writers.append(o.idx)
                w.readers = []
            else:
                g = set(w.writers) | set(w.readers)
                deps.update(g)
                w.gen_deps = g
                w.writers = [o.idx]
                w.readers = []
        for r in reads:
            r.readers.append(o.idx)
        deps.discard(o.idx)
        self.ops.append(o)
        return o

    def dma(self, eng, out_ap, in_ap, reads, writes, nowaw=False):
        return self.op(eng, lambda e: e.dma_start(out=out_ap, in_=in_ap), reads, writes, dma=True, nowaw=nowaw)

    def mm(self, out_ap, lhsT, rhs, start, stop, reads, writes, nowaw=False):
        return self.op("tensor", lambda e: e.matmul(out_ap, lhsT=lhsT, rhs=rhs, start=start, stop=stop),
                       reads, writes, nowaw=nowaw)

    def tr(self, out_ap, in_ap, ident, reads, writes, nowaw=False):
        return self.op("tensor", lambda e: e.transpose(out_ap, in_ap, ident), reads, writes, nowaw=nowaw)

    def act(self, out_ap, in_ap, func, reads, writes, scale=1.0, bias=None, accum=None, eng="scalar", nowaw=False):
        def f(e):
            kw = dict(out=out_ap, in_=in_ap, func=func, scale=scale)
            if bias is not None:
                kw["bias"] = bias
            if accum is not None:
                kw["accum_out"] = accum
            return e.activation(**kw)
        return self.op(eng, f, reads, writes, nowaw=nowaw)

    def cp(self, eng, out_ap, in_ap, reads, writes, nowaw=False):
        if eng == "scalar":
            return self.op(eng, lambda e: e.copy(out=out_ap, in_=in_ap), reads, writes, nowaw=nowaw)
        return self.op(eng, lambda e: e.tensor_copy(out=out_ap, in_=in_ap), reads, writes, nowaw=nowaw)

    def tt(self, eng, out_ap, in0, in1, op, reads, writes, nowaw=False):
        return self.op(eng, lambda e: e.tensor_tensor(out=out_ap, in0=in0, in1=in1, op=op), reads, writes, nowaw=nowaw)

    def ts(self, eng, out_ap, in0, s1, s2, op0, op1, reads, writes, nowaw=False):
        if s2 is None:
            return self.op(eng, lambda e: e.tensor_scalar(out=out_ap, in0=in0, scalar1=s1, scalar2=None, op0=op0),
                           reads, writes, nowaw=nowaw)
        return self.op(eng, lambda e: e.tensor_scalar(out=out_ap, in0=in0, scalar1=s1, scalar2=s2, op0=op0, op1=op1),
                       reads, writes, nowaw=nowaw)

    def stt(self, eng, out_ap, in0, scalar, in1, op0, op1, reads, writes, nowaw=False):
        return self.op(eng, lambda e: e.scalar_tensor_tensor(out=out_ap, in0=in0, scalar=scalar, in1=in1, op0=op0, op1=op1),
                       reads, writes, nowaw=nowaw)

    def memset(self, eng, ap, val, writes, nowaw=False):
        return self.op(eng, lambda e: e.memset(ap, val), [], writes, nowaw=nowaw)

    def emit(self, final_wait_eng="sync"):
        nc = self.nc
        import os
        kcut = int(os.environ.get("KCUT", "0"))
        if kcut:
            self.ops = self.ops[:kcut]
        ops = self.ops
        print("emit: n_ops =", len(ops), flush=True)
        for o in ops:
            last = {}
            keep = set()
            for d in o.deps:
                od = ops[d]
                if od.is_dma:
                    keep.add(d)
                elif od.eng not in last or last[od.eng] < d:
                    last[od.eng] = d
            keep.update(last.values())
            o.deps = keep
        for o in ops:
            for d in o.deps:
                od = ops[d]
                if od.eng == "tensor" and o.eng == "tensor" and not od.is_dma and not o.is_dma:
                    continue
                od.signal = True
            if o.is_dma:
                o.signal = True
        with contextlib.ExitStack() as st:
            esem = {e: st.enter_context(nc.semaphore("s_" + e)) for e in ENGS}
            dsem = {e: [st.enter_context(nc.semaphore("d_%s%d" % (e, i))) for i in range(NDMA)]
                    for e in ("sync", "scalar", "gpsimd")}
            ecount = {e: 0 for e in ENGS}
            dcount = {e: [0] * NDMA for e in dsem}
            drot = {e: 0 for e in dsem}
            prewait = {}
            for o in ops:
                if not o.signal:
                    continue
                if o.is_dma:
                    i = drot[o.eng]
                    drot[o.eng] = (i + 1) % NDMA
                    prewait[o.idx] = (dsem[o.eng][i], dcount[o.eng][i])
                    dcount[o.eng][i] += 16
                    o.tok = (dsem[o.eng][i], dcount[o.eng][i], 16)
                else:
                    ecount[o.eng] += 1
                    o.tok = (esem[o.eng], ecount[o.eng], 1)
            by_eng = {e: [o for o in ops if o.eng == e] for e in ENGS}
            block = st.enter_context(nc.Block())

            def run(e, eng):
                known = {}
                for o in by_eng[e]:
                    waits = {}
                    for d in o.deps:
                        od = ops[d]
                        if od.tok is None:
                            continue
                        if od.eng == "tensor" and e == "tensor" and not od.is_dma and not o.is_dma:
                            continue
                        s, v, _ = od.tok
                        k = id(s)
                        if known.get(k, 0) >= v:
                            continue
                        if k not in waits or waits[k][1] < v:
                            waits[k] = (s, v)
                    if o.idx in prewait:
                        s, v = prewait[o.idx]
                        k = id(s)
                        if v > 0 and known.get(k, 0) < v and (k not in waits or waits[k][1] < v):
                            waits[k] = (s, v)
                    for k, (s, v) in waits.items():
                        eng.wait_ge(s, v)
                        known[k] = v
                    ins = o.fn(eng)
                    if o.tok is not None:
                        ins.then_inc(o.tok[0], o.tok[2])
                if e == final_wait_eng:
                    for q in dsem:
                        for i in range(NDMA):
                            if dcount[q][i] > 0:
                                eng.wait_ge(dsem[q][i], dcount[q][i])

            @block.sync
            def _(eng):
                run("sync", eng)

            @block.scalar
            def _(eng):
                run("scalar", eng)

            @block.gpsimd
            def _(eng):
                run("gpsimd", eng)

            @block.vector
            def _(eng):
                run("vector", eng)

            @block.tensor
            def _(eng):
                run("tensor", eng)


def build(T, NSEQ, debug=False, do_c=True, stop_after=None):
    nc = bass.Bass("TRN2", target_bir_lowering=False)
    P = Prog(nc)
    NT = T // 512
    NKT = T // 128
    dk = "ExternalOutput"

    def din(name, shape, dt=F32):
        return P.dram(name, shape, dt, kind="ExternalInput")

    x_d = din("x", [NSEQ, T, D])
    anw_d = din("attn_norm_w", [D])
    win_d = din("w_in", [D, INC])
    lq1_d = din("da_lambda_q1", [64]); lk1_d = din("da_lambda_k1", [64])
    lq2_d = din("da_lambda_q2", [64]); lk2_d = din("da_lambda_k2", [64])
    subln_d = din("da_subln_w", [128])
    gcw_d = din("gdn_conv_w", [128, 12, 4])
    alog_d = din("gdn_a_log", [4]); dtb_d = din("gdn_dt_bias", [4])
    gnw_d = din("gdn_norm_w", [128])
    wout_d = din("w_out", [D, D])
    fnw_d = din("ffn_norm_w", [D])
    wup_d = din("ffn_w_up", [D, 2 * DFF])
    fcw_d = din("ffn_conv_w", [128, 44, 3])
    wdn_d = din("ffn_w_down", [DFF, D])
    finw_d = din("final_norm_w", [D])
    identb_d = din("c_identb", [128, 128], BF16)
    identf_d = din("c_identf", [128, 128])
    rot_d = din("c_rot", [128, 128], BF16)
    rope_d = din("c_rope", [128, 2, T])
    trim_d = din("c_trimask", [128, 128], BF16)
    cm_d = din("c_masks", [64, 5, 64])
    out_d = P.dram("out", [NSEQ, T, D], F32, kind="ExternalOutput")

    qT_d = P.dram("s_qT", [NSEQ, 4, 128, T], BF16, kind=dk)
    kT_d = P.dram("s_kT", [NSEQ, 4, 128, T], BF16, kind=dk)
    vda_d = P.dram("s_vda", [NSEQ, T, 512], BF16, kind=dk)
    gqT_d = P.dram("s_gqT", [NSEQ, 4, 128, T], BF16, kind=dk)
    gkT_d = P.dram("s_gkT", [NSEQ, 4, 128, T], BF16, kind=dk)
    gkn_d = P.dram("s_gkn", [NSEQ, T, 512], BF16, kind=dk)
    gv_d = P.dram("s_gv", [NSEQ, T, 512], BF16, kind=dk)
    gz_d = P.dram("s_gz", [NSEQ, T, 512], BF16, kind=dk)
    gsc_d = P.dram("s_gsc", [NSEQ, T, 16], F32, kind=dk)
    mix_d = P.dram("s_mix", [NSEQ, T, D], BF16, kind=dk)
    wupb_d = P.dram("s_wupb", [D, 2 * DFF], BF16)
    wdnb_d = P.dram("s_wdnb", [DFF, D], BF16)
    def units(n):
        return [Buf("u", None) for _ in range(n)]
    u_qT = units(NSEQ * NT); u_kT = units(NSEQ * NT); u_vda = units(NSEQ * NT)
    u_gqT = units(NSEQ * NT); u_gkT = units(NSEQ * NT); u_gkn = units(NSEQ * NT)
    u_gv = units(NSEQ * NT); u_gz = units(NSEQ * NT); u_gsc = units(NSEQ * NT)
    u_mixa = units(NSEQ * NT); u_mixg = units(NSEQ * NT)
    u_wupb = Buf("u", None); u_wdnb = Buf("u", None)
    CONST = Buf("const_in", None)
    CONST_OUT = Buf("const_out", None)

    PB = [P.ps("pb%d" % i, [128, 512]) for i in range(8)]

    def pbf(i):
        return PB[i].t[:, :].bitcast(BF16)

    ARENA_BYTES = 196 * 1024
    arena = nc.alloc_sbuf_tensor("arena", [128, ARENA_BYTES // 4], F32)
    live = []
    cur = [0]

    def aalloc(name, free_shape, dt):
        n = int(np.prod(free_shape))
        nb = n * (2 if dt == BF16 else 4)
        nb = (nb + 63) // 64 * 64
        s = cur[0]
        e = s + nb
        assert e <= ARENA_BYTES, (name, e)
        cur[0] = e
        ap = arena[:, s // 4:e // 4]
        if dt == BF16:
            ap = ap.bitcast(BF16)
        ap = ap[:, 0:n]
        if len(free_shape) == 2:
            ap = ap.rearrange("p (a b) -> p a b", a=free_shape[0])
        elif len(free_shape) == 3:
            ap = ap.rearrange("p (a b c) -> p a b c", a=free_shape[0], b=free_shape[1])
        elif len(free_shape) == 4:
            ap = ap.rearrange("p (a b c d) -> p a b c d", a=free_shape[0], b=free_shape[1], c=free_shape[2])
        b = Buf(name, ap)
        for (s0, e0, ob) in live:
            if s0 < e and s < e0:
                b.readers.extend(ob.writers)
                b.readers.extend(ob.readers)
        live.append((s, e, b))
        return b

    def areset(mark=0):
        cur[0] = mark

    identb = P.sb("identb", [128, 128], BF16)
    identf = P.sb("identf", [128, 128])
    rotm = P.sb("rotm", [128, 128], BF16)
    trim = P.sb("trim", [128, 128], BF16)
    cm = P.sb("cm", [64, 5, 64])
    negh = P.sb("negh", [128, 64])
    ones_f = P.sb("ones_f", [64, 128])
    P.dma("sync", identb[:, :], identb_d[:, :], [CONST], [identb])
    P.dma("sync", identf[:, :], identf_d[:, :], [CONST], [identf])
    P.dma("sync", rotm[:, :], rot_d[:, :], [CONST], [rotm])
    P.dma("sync", trim[:, :], trim_d[:, :], [CONST], [trim])
    P.dma("sync", cm[:, :, :], cm_d[:, :, :], [CONST], [cm])
    P.memset("vector", negh[:, :], -0.5, [negh])
    P.memset("vector", ones_f[:, :], 1.0, [ones_f])

    def bc(d_buf, n):
        return d_buf.t.ap().partition_broadcast(n)

    def rsqrt_(out_ap, in_ap, scale, nh_ap, reads, writes):
        P.ts("vector", out_ap, in_ap, scale, EPS, ALU.mult, ALU.add, reads, writes)
        P.tt("gpsimd", out_ap, out_ap, nh_ap, ALU.pow, list(writes) + [negh], writes)

    areset(0)
    Win = aalloc("Win", [8, INC], BF16)
    markW = cur[0]
    stg = [aalloc("stg%d" % i, [3592], F32) for i in range(2)]
    stgb = [aalloc("stgb%d" % i, [3592], BF16) for i in range(2)]
    ci = [0]
    cast_engs = ["vector", "gpsimd"]

    def cast_to_dram(src_ap, dst_ap, ncols, unit):
        i = ci[0] % 2
        ci[0] += 1
        P.dma("sync", stg[i][:, 0:ncols], src_ap, [CONST], [stg[i]])
        P.cp(cast_engs[i], stgb[i][:, 0:ncols], stg[i][:, 0:ncols], [stg[i]], [stgb[i]])
        P.dma("sync", dst_ap, stgb[i][:, 0:ncols], [stgb[i]], [unit], nowaw=True)

    for k in range(8):
        for hf in range(2):
            cast_to_dram(wup_d[k * 128:(k + 1) * 128, hf * DFF:(hf + 1) * DFF],
                         wupb_d[k * 128:(k + 1) * 128, hf * DFF:(hf + 1) * DFF], DFF, u_wupb)
    for k in range(22):
        cast_to_dram(wdn_d[k * 128:(k + 1) * 128, :], wdnb_d[k * 128:(k + 1) * 128, :], D, u_wdnb)

    for k in range(8):
        i = ci[0] % 2
        ci[0] += 1
        P.dma("sync", stg[i][:, 0:INC], win_d[k * 128:(k + 1) * 128, :], [CONST], [stg[i]])
        P.cp(cast_engs[i], Win[:, k, :], stg[i][:, 0:INC], [stg[i]], [Win], nowaw=True)
    areset(markW)
    xbuf = [aalloc("xbuf%d" % i, [4, D], F32) for i in range(2)]
    rpbuf = [aalloc("rp%d" % i, [2, 512], F32) for i in range(2)]
    xn = aalloc("xn", [4, D], BF16)
    xnT = aalloc("xnT", [8, 512], BF16)
    nwA = aalloc("nwA", [D], F32)
    junk = aalloc("junk", [D], BF16)
    ssq = aalloc("ssq", [4], F32)
    rstd = aalloc("rstd", [4], F32)
    xb = [aalloc("xb%d" % i, [512], BF16) for i in range(2)]
    t1 = [aalloc("t1_%d" % i, [512], F32) for i in range(2)]
    t2 = [aalloc("t2_%d" % i, [512], F32) for i in range(2)]
    ro = [aalloc("ro%d" % i, [512], BF16) for i in range(3)]
    gh = aalloc("gh", [12, 515], BF16)
    gdiag = aalloc("gdiag", [12, 4, 128], BF16)
    gcw = aalloc("gcw", [12, 4], F32)
    gs = [aalloc("gs%d" % i, [512], BF16) for i in range(3)]
    knt = [aalloc("knt%d" % i, [128], BF16) for i in range(2)]
    knTt = [aalloc("knTt%d" % i, [512], BF16) for i in range(2)]
    kn_t = aalloc("kn_t", [4, 512], BF16)
    v_t = aalloc("v_t", [4, 512], BF16)
    va_t = aalloc("va_t", [4, 512], BF16)
    z_t = aalloc("z_t", [4, 512], BF16)
    ssqk = aalloc("ssqk", [4, 8], F32)
    rk1 = aalloc("rk1", [4, 8], F32)
    gsc_t = aalloc("gsc_t", [4, 16], F32)
    ba_t = aalloc("ba_t", [4, 8], F32)
    tmp4 = aalloc("tmp4", [4, 4], F32)
    dtbB = aalloc("dtbB", [4], F32)
    negA = aalloc("negA", [4], F32)
    markA = cur[0]

    P.dma("sync", nwA[:, :], bc(anw_d, 128), [CONST], [nwA])
    P.dma("sync", gcw[:, :, :], gcw_d[:, :, :], [CONST], [gcw])
    P.dma("sync", dtbB[:, :], bc(dtb_d, 128), [CONST], [dtbB])
    P.dma("sync", negA[:, :], bc(alog_d, 128), [CONST], [negA])
    P.act(negA[:, :], negA[:, :], AF.Exp, [negA], [negA])
    P.ts("vector", negA[:, :], negA[:, :], -1.0, None, ALU.mult, None, [negA], [negA])
    for c in range(12):
        for j in range(4):
            P.ts("gpsimd", gdiag[:, c, j, :], identb[:, :], gcw[:, c, j:j + 1], None, ALU.mult, None,
                 [identb, gcw], [gdiag], nowaw=True)

    tiles = [(s, tt) for s in range(NSEQ) for tt in range(NT)]

    def loadA(g):
        s, tt = tiles[g]
        t0 = tt * 512
        P.dma("sync", xbuf[g % 2][:, :, :], x_d[s, t0:t0 + 512, :].rearrange("(j p) d -> p j d", p=128),
              [CONST], [xbuf[g % 2]])
        P.dma("sync", rpbuf[g % 2][:, :, :], rope_d[:, :, t0:t0 + 512], [CONST], [rpbuf[g % 2]])

    evi = [0]

    def ev_eng():
        evi[0] += 1
        return "vector" if evi[0] % 2 else "scalar"

    loadA(0)
    for g, (s, tt) in enumerate(tiles):
        t0 = tt * 512
        ug = s * NT + tt
        if g + 1 < len(tiles):
            loadA(g + 1)
        xt = xbuf[g % 2]
        rp = rpbuf[g % 2]
        P.memset("vector", ssq[:, :], 0.0, [ssq])
        for j in range(4):
            P.act(junk[:, :], xt[:, j, :], AF.Square, [xt, ssq], [junk, ssq], accum=ssq[:, j:j + 1], nowaw=True)
        rsqrt_(rstd[:, :], ssq[:, :], 1.0 / D, negh[:, 0:4], [ssq], [rstd])
        for j in range(4):
            P.stt("vector", xn[:, j, :], xt[:, j, :], rstd[:, j:j + 1], nwA[:, :], ALU.mult, ALU.mult,
                  [xt, rstd, nwA], [xn], nowaw=True)
        for k in range(8):
            bk = k % 2
            for j in range(4):
                P.tr(pbf(bk)[:, j * 128:(j + 1) * 128], xn[:, j, k * 128:(k + 1) * 128], identb[:, :],
                     [xn, identb], [PB[bk]], nowaw=(j > 0))
            P.cp(ev_eng(), xnT[:, k, :], pbf(bk)[:, 0:512], [PB[bk]], [xnT], nowaw=True)

        def proj_fm(c0, bank):
            for k in range(8):
                P.mm(PB[bank][:, :], Win[:, k, c0:c0 + 128], xnT[:, k, :], k == 0, k == 7, [Win, xnT], [PB[bank]],
                     nowaw=(k > 0))

        def projqk(c):
            bank = 2 + (c % 2)
            proj_fm(c * 128, bank)
            i2 = c % 2
            P.cp("scalar", xb[i2][:, :], PB[bank][:, :], [PB[bank]], [xb[i2]])
            P.tt("vector", t1[i2][:, :], PB[bank][:, :], rp[:, 0, :], ALU.mult, [PB[bank], rp], [t1[i2]])

        def ropeqk(c):
            i2 = c % 2
            rb = 4 + (c % 2)
            P.mm(PB[rb][:, :], rotm[:, :], xb[i2][:, :], True, True, [rotm, xb[i2]], [PB[rb]])
            P.tt("vector", t2[i2][:, :], PB[rb][:, :], rp[:, 1, :], ALU.mult, [PB[rb], rp], [t2[i2]])
            r3 = ro[c % 3]
            P.tt("gpsimd", r3[:, :], t1[i2][:, :], t2[i2][:, :], ALU.add, [t1[i2], t2[i2]], [r3])
            if c < 4:
                P.dma("gpsimd", qT_d[s, c, :, t0:t0 + 512], r3[:, :], [r3], [u_qT[ug]], nowaw=True)
            else:
                P.dma("gpsimd", kT_d[s, c - 4, :, t0:t0 + 512], r3[:, :], [r3], [u_kT[ug]], nowaw=True)

        projqk(0)
        for c in range(8):
            if c + 1 < 8:
                projqk(c + 1)
            ropeqk(c)
        if tt == 0:
            P.memset("vector", gh[:, :, 0:3], 0.0, [gh])
        P.memset("vector", ssqk[:, :, :], 0.0, [ssqk])
        for c in range(12):
            bank = 2 + (c % 2)
            proj_fm(1536 + c * 128, bank)
            P.cp("scalar", gh[:, c, 3:515], PB[bank][:, :], [PB[bank]], [gh], nowaw=True)
        def gconv(c):
            bank = 4 + (c % 2)
            for j in range(4):
                P.mm(PB[bank][:, :], gdiag[:, c, j, :], gh[:, c, j:j + 512], j == 0, j == 3, [gdiag, gh], [PB[bank]],
                     nowaw=(j > 0))
            P.act(gs[c % 3][:, :], PB[bank][:, :], AF.Silu, [PB[bank]], [gs[c % 3]])

        gconv(0)
        for c in range(12):
            if c + 1 < 12:
                gconv(c + 1)
            g3 = gs[c % 3]
            h = c % 4
            if c < 4:
                P.dma("gpsimd", gqT_d[s, h, :, t0:t0 + 512], g3[:, :], [g3], [u_gqT[ug]], nowaw=True)
                tb = 6 + (c % 2)
                for i in range(4):
                    P.tr(pbf(tb)[:, i * 128:(i + 1) * 128], g3[:, i * 128:(i + 1) * 128], identb[:, :],
                         [g3, identb], [PB[tb]], nowaw=(i > 0))
                for i in range(4):
                    P.act(junk[:, 0:128], pbf(tb)[:, i * 128:(i + 1) * 128], AF.Square, [PB[tb], ssqk], [junk, ssqk],
                          accum=ssqk[:, i, 4 + h:5 + h], nowaw=True)
            elif c < 8:
                tb = 6 + (c % 2)
                for i in range(4):
                    P.tr(pbf(tb)[:, i * 128:(i + 1) * 128], g3[:, i * 128:(i + 1) * 128], identb[:, :],
                         [g3, identb], [PB[tb]], nowaw=(i > 0))
                for i in range(4):
                    P.act(junk[:, 0:128], pbf(tb)[:, i * 128:(i + 1) * 128], AF.Square, [PB[tb], ssqk], [junk, ssqk],
                          accum=ssqk[:, i, h:h + 1], nowaw=True)
                rsqrt_(rk1[:, :, h:h + 1], ssqk[:, :, h:h + 1], 1.0, negh[:, 0:4].unsqueeze(2), [ssqk], [rk1])
                for i in range(4):
                    P.ts("vector", kn_t[:, i, h * 128:(h + 1) * 128], pbf(tb)[:, i * 128:(i + 1) * 128],
                         rk1[:, i, h:h + 1], None, ALU.mult, None, [PB[tb], rk1], [kn_t], nowaw=True)
                kT_ = knTt[c % 2]
                tb2 = 2 + (c % 2)
                for i in range(4):
                    P.tr(pbf(tb2)[:, i * 128:(i + 1) * 128], kn_t[:, i, h * 128:(h + 1) * 128], identb[:, :],
                         [kn_t, identb], [PB[tb2]], nowaw=(i > 0))
                P.cp("vector", kT_[:, :], pbf(tb2)[:, 0:512], [PB[tb2]], [kT_])
                P.dma("gpsimd", gkT_d[s, h, :, t0:t0 + 512], kT_[:, :], [kT_], [u_gkT[ug]], nowaw=True)
            else:
                tb = 6 + (c % 2)
                for i in range(4):
                    P.tr(pbf(tb)[:, i * 128:(i + 1) * 128], g3[:, i * 128:(i + 1) * 128], identb[:, :],
                         [g3, identb], [PB[tb]], nowaw=(i > 0))
                P.cp("vector", v_t[:, :, h * 128:(h + 1) * 128],
                     pbf(tb)[:, 0:512].rearrange("p (i d) -> p i d", i=4), [PB[tb]], [v_t], nowaw=True)
        P.cp("gpsimd", gh[:, :, 0:3], gh[:, :, 512:515], [gh], [gh])
        P.dma("gpsimd", gkn_d[s, t0:t0 + 512, :].rearrange("(j p) f -> p j f", p=128), kn_t[:, :, :], [kn_t], [u_gkn[ug]])
        P.dma("gpsimd", gv_d[s, t0:t0 + 512, :].rearrange("(j p) f -> p j f", p=128), v_t[:, :, :], [v_t], [u_gv[ug]])
        for i in range(4):
            bank = 2 + (i % 2)
            for k in range(8):
                P.mm(PB[bank][:, :], xnT[:, k, i * 128:(i + 1) * 128], Win[:, k, 1024:1536], k == 0, k == 7,
                     [Win, xnT], [PB[bank]], nowaw=(k > 0))
            P.cp(ev_eng(), va_t[:, i, :], PB[bank][:, :], [PB[bank]], [va_t], nowaw=True)
            bank = 4 + (i % 2)
            for k in range(8):
                P.mm(PB[bank][:, :], xnT[:, k, i * 128:(i + 1) * 128], Win[:, k, 3072:3584], k == 0, k == 7,
                     [Win, xnT], [PB[bank]], nowaw=(k > 0))
            P.act(z_t[:, i, :], PB[bank][:, :], AF.Silu, [PB[bank]], [z_t], nowaw=True)
        for i in range(4):
            for k in range(8):
                P.mm(PB[6][:, i * 8:(i + 1) * 8], xnT[:, k, i * 128:(i + 1) * 128], Win[:, k, 3584:3592], k == 0, k == 7,
                     [Win, xnT], [PB[6]], nowaw=not (i == 0 and k == 0))
        P.cp("vector", ba_t[:, :, :], PB[6][:, 0:32].rearrange("p (i e) -> p i e", i=4), [PB[6]], [ba_t])
        P.dma("gpsimd", vda_d[s, t0:t0 + 512, :].rearrange("(j p) f -> p j f", p=128), va_t[:, :, :], [va_t], [u_vda[ug]])
        P.dma("gpsimd", gz_d[s, t0:t0 + 512, :].rearrange("(j p) f -> p j f", p=128), z_t[:, :, :], [z_t], [u_gz[ug]])
        rsqrt_(gsc_t[:, :, 4:8], ssqk[:, :, 4:8], 1.0, negh[:, 0:16].rearrange("p (a b) -> p a b", a=4), [ssqk], [gsc_t])
        P.ts("vector", gsc_t[:, :, 4:8], gsc_t[:, :, 4:8], 128.0 ** -0.5, None, ALU.mult, None, [gsc_t], [gsc_t])
        P.cp("vector", gsc_t[:, :, 0:4], rk1[:, :, 0:4], [rk1], [gsc_t])
        P.act(gsc_t[:, :, 8:12], ba_t[:, :, 0:4], AF.Sigmoid, [ba_t], [gsc_t])
        P.tt("vector", tmp4[:, :, :], ba_t[:, :, 4:8], dtbB[:, :].unsqueeze(1).to_broadcast([128, 4, 4]), ALU.add,
             [ba_t, dtbB], [tmp4])
        P.act(tmp4[:, :, :], tmp4[:, :, :], AF.Exp, [tmp4], [tmp4])
        P.act(tmp4[:, :, :], tmp4[:, :, :], AF.Ln, [tmp4], [tmp4], bias=1.0)
        P.tt("vector", gsc_t[:, :, 12:16], tmp4[:, :, :], negA[:, :].unsqueeze(1).to_broadcast([128, 4, 4]), ALU.mult,
             [tmp4, negA], [gsc_t])
        P.dma("gpsimd", gsc_d[s, t0:t0 + 512, :].rearrange("(j p) f -> p j f", p=128), gsc_t[:, :, :], [gsc_t], [u_gsc[ug]])


    if stop_after == "A":
        P.emit()
        return nc

    def phase_c():
        areset(0)
        knT_c = aalloc("c_knT", [4, 512], BF16)
        kn_c = aalloc("c_kn", [8, 512], BF16)
        v_c = aalloc("c_v", [8, 512], BF16)
        sc_c = aalloc("c_sc", [8, 16], F32)
        g_c = aalloc("c_g", [32], F32)
        gcum = aalloc("c_gcum", [32], F32)
        egc = aalloc("c_egc", [32], F32)
        kdc = aalloc("c_kdc", [32], F32)
        beta_c = aalloc("c_beta", [32], F32)
        SETS = []
        for k_ in range(2):
            SETS.append(dict(
                gU=aalloc("c_gU%d" % k_, [8, 64], F32), Gt=aalloc("c_Gt%d" % k_, [8, 64], F32),
                tSU=aalloc("c_tSU%d" % k_, [8, 64], F32), tU=aalloc("c_tU%d" % k_, [8, 64], F32),
                Pm=[aalloc("c_P%d_%d" % (i, k_), [8, 64], BF16) for i in range(2)],
                PTm=[aalloc("c_PT%d_%d" % (i, k_), [8, 64], BF16) for i in range(2)],
                P0f=aalloc("c_P0f%d" % k_, [8, 64], F32), Amb=aalloc("c_Amb%d" % k_, [8, 64], BF16),
                Am=aalloc("c_A%d" % k_, [8, 64], F32), Wb=aalloc("c_Wb%d" % k_, [8, 64], BF16),
                kg_b=aalloc("c_kg%d" % k_, [8, 128], BF16), b0=PB[2 * k_], b1=PB[2 * k_ + 1]))
        qT_c2 = [aalloc("c_qT%d" % i, [4, 512], BF16) for i in range(2)]
        z_c2 = [aalloc("c_z%d" % i, [8, 512], BF16) for i in range(2)]
        wT_all2 = [aalloc("c_wT%d" % i, [32, 64], BF16) for i in range(2)]
        ub_all2 = [aalloc("c_ub%d" % i, [32, 128], F32) for i in range(2)]
        Aq_all2 = [aalloc("c_Aq%d" % i, [32, 64], BF16) for i in range(2)]
        kdec_all2 = [aalloc("c_kdec%d" % i, [32, 128], BF16) for i in range(2)]
        egl2 = [aalloc("c_egl%d" % i, [32], F32) for i in range(2)]
        rq_c2 = [aalloc("c_rq%d" % i, [32], F32) for i in range(2)]
        rqe2 = [aalloc("c_rqe%d" % i, [32], F32) for i in range(2)]
        nbeta2 = [aalloc("c_nbeta%d" % i, [32], F32) for i in range(2)]
        S = [aalloc("c_S%d" % i, [128], F32) for i in range(4)]
        Sb = [aalloc("c_Sb%d" % i, [128], BF16) for i in range(4)]
        vnew = [aalloc("c_vn%d" % i, [128], BF16) for i in range(4)]
        t1c = [aalloc("c_t1%d" % i, [128], F32) for i in range(4)]
        obuf = aalloc("c_o", [8, 512], F32)
        osq = aalloc("c_osq", [8, 512], BF16)
        ssqg = aalloc("c_ssqg", [32], F32)
        rstdg = aalloc("c_rstdg", [32], F32)
        mixg = aalloc("c_mixg", [8, 512], BF16)
        gnwB = aalloc("c_gnwB", [128], F32)
        P.dma("sync", gnwB[:, :], bc(gnw_d, 128), [CONST], [gnwB])
        H = slice(0, 64)

        def b3(ap2, n):
            return ap2.unsqueeze(2).to_broadcast([64, 8, n])

        def pre(g):
            s, tt = tiles[g]
            t0 = tt * 512
            ug = s * NT + tt
            qT_c, z_c, wT_all, ub_all = qT_c2[g % 2], z_c2[g % 2], wT_all2[g % 2], ub_all2[g % 2]
            Aq_all, kdec_all, egl, rq_c, rqe, nbeta = (Aq_all2[g % 2], kdec_all2[g % 2], egl2[g % 2], rq_c2[g % 2],
                                                       rqe2[g % 2], nbeta2[g % 2])
            P.dma("sync", knT_c[:, :, :], gkT_d[s, :, :, t0:t0 + 512].rearrange("h p t -> p h t"), [u_gkT[ug]], [knT_c])
            P.dma("sync", qT_c[:, :, :], gqT_d[s, :, :, t0:t0 + 512].rearrange("h p t -> p h t"), [u_gqT[ug]], [qT_c])
            P.dma("sync", kn_c[H, :, :], gkn_d[s, t0:t0 + 512, :].rearrange("(n c) f -> c n f", c=64), [u_gkn[ug]], [kn_c])
            P.dma("sync", v_c[H, :, :], gv_d[s, t0:t0 + 512, :].rearrange("(n c) f -> c n f", c=64), [u_gv[ug]], [v_c])
            P.dma("sync", z_c[H, :, :], gz_d[s, t0:t0 + 512, :].rearrange("(n c) f -> c n f", c=64), [u_gz[ug]], [z_c])
            P.dma("sync", sc_c[H, :, :], gsc_d[s, t0:t0 + 512, :].rearrange("(n c) f -> c n f", c=64), [u_gsc[ug]], [sc_c])
            yield
            g3 = g_c[H, :].rearrange("p (n h) -> p n h", n=8)
            P.cp("vector", g3, sc_c[H, :, 12:16], [sc_c], [g_c])
            P.cp("vector", beta_c[H, :].rearrange("p (n h) -> p n h", n=8), sc_c[H, :, 8:12], [sc_c], [beta_c])
            P.cp("vector", rq_c[H, :].rearrange("p (n h) -> p n h", n=8), sc_c[H, :, 4:8], [sc_c], [rq_c])
            P.ts("vector", nbeta[H, :], beta_c[H, :], -1.0, None, ALU.mult, None, [beta_c], [nbeta])
            yield
            P.mm(PB[0][H, 0:32], cm[:, 0, :], g_c[H, :], True, True, [cm, g_c], [PB[0]])
            P.mm(PB[1][:, 0:32], ones_f[:, :], g_c[H, :], True, True, [ones_f, g_c], [PB[1]])
            P.cp("vector", gcum[H, :], PB[0][H, 0:32], [PB[0]], [gcum])
            P.act(egc[H, :], PB[0][H, 0:32], AF.Exp, [PB[0]], [egc])
            yield
            P.act(egl[:, :], PB[1][:, 0:32], AF.Exp, [PB[1]], [egl])
            P.tt("vector", kdc[H, :], PB[1][H, 0:32], gcum[H, :], ALU.subtract, [PB[1], gcum], [kdc])
            P.act(kdc[H, :], kdc[H, :], AF.Exp, [kdc], [kdc])
            P.tt("vector", rqe[H, :], rq_c[H, :], egc[H, :], ALU.mult, [rq_c, egc], [rqe])
            yield
            def batch(bb, BS):
                gU, Gt, tSU, tU, Pm, PTm, Am, Wb, kg_b = (BS['gU'], BS['Gt'], BS['tSU'], BS['tU'], BS['Pm'], BS['PTm'],
                                                          BS['Am'], BS['Wb'], BS['kg_b'])
                b0, b1 = BS['b0'], BS['b1']
                P0f, Amb = BS['P0f'], BS['Amb']
                b1h = b1.t[:, :].bitcast(BF16)
                ps = slice(bb * 8, bb * 8 + 8)
                pairs = [(2 * bb + q // 4, q % 4) for q in range(8)]
                P.tt("vector", gU[H, :, :], cm[:, 0, :].unsqueeze(1).to_broadcast([64, 8, 64]), b3(g_c[H, ps], 64), ALU.mult,
                     [cm, g_c], [gU])
                for q, (n, h) in enumerate(pairs):
                    P.mm(b0[H, q * 64:(q + 1) * 64], cm[:, 4, :], gU[H, q, :], True, True, [cm, gU], [b0], nowaw=(q > 0))
                yield
                P.act(Gt[H, :, :], b0[H, :].rearrange("p (q i) -> p q i", q=8), AF.Exp, [b0], [Gt])
                for q, (n, h) in enumerate(pairs):
                    ks = knT_c[:, h, n * 64:(n + 1) * 64]
                    P.mm(b1[H, q * 64:(q + 1) * 64], ks, ks, True, True, [knT_c], [b1], nowaw=(q > 0))
                yield
                for q, (n, h) in enumerate(pairs):
                    ks = knT_c[:, h, n * 64:(n + 1) * 64]
                    P.mm(b0[H, q * 64:(q + 1) * 64], ks, qT_c[:, h, n * 64:(n + 1) * 64], True, True, [knT_c, qT_c], [b0],
                         nowaw=(q > 0))
                yield
                m8 = lambda i: cm[:, i, :].unsqueeze(1).to_broadcast([64, 8, 64])
                P.tt("gpsimd", tSU[H, :, :], Gt[H, :, :], m8(1), ALU.mult, [Gt, cm], [tSU])
                P.stt("vector", tSU[H, :, :], tSU[H, :, :], -1.0, b3(beta_c[H, ps], 64), ALU.mult, ALU.mult, [tSU, beta_c], [tSU])
                P.tt("gpsimd", tU[H, :, :], Gt[H, :, :], m8(2), ALU.mult, [Gt, cm], [tU])
                yield
                P0 = Pm[0]
                P.tt("vector", P0f[H, :, :], b1[H, :].rearrange("p (q i) -> p q i", q=8), tSU[H, :, :], ALU.mult,
                     [b1, tSU], [P0f])
                P.cp("gpsimd", P0[H, :, :], P0f[H, :, :], [P0f], [P0])
                P.tt("vector", Aq_all[H, ps, :], b0[H, :].rearrange("p (q i) -> p q i", q=8), tU[H, :, :], ALU.mult,
                     [b0, tU], [Aq_all], nowaw=(bb > 0))
                yield
                for q in range(8):
                    P.tr(b1h[H, q * 64:(q + 1) * 64], P0[H, q, :], identb[H, H], [P0, identb], [b1], nowaw=(q > 0))
                P.cp("scalar", PTm[0][H, :, :], b1h[H, 0:512].rearrange("p (q i) -> p q i", q=8), [b1], [PTm[0]])
                P.tt("gpsimd", Am[H, :, :], P0f[H, :, :], m8(3), ALU.add, [P0f, cm], [Am])
                P.cp("gpsimd", Amb[H, :, :], Am[H, :, :], [Am], [Amb])
                yield
                for m in range(5):
                    Pc, PTc = Pm[m % 2], PTm[m % 2]
                    Pn, PTn = Pm[(m + 1) % 2], PTm[(m + 1) % 2]
                    if m < 4:
                        for q in range(8):
                            P.mm(b1[H, q * 64:(q + 1) * 64], PTc[H, q, :], Pc[H, q, :], True, True, [PTc, Pc], [b1],
                                 nowaw=(q > 0))
                        yield
                    for q in range(8):
                        P.mm(b0[H, q * 64:(q + 1) * 64], Pc[H, q, :], PTc[H, q, :], True, True, [PTc, Pc], [b0],
                             nowaw=(q > 0))
                    yield
                    if m < 4:
                        P.cp("vector", Pn[H, :, :], b1[H, :].rearrange("p (q i) -> p q i", q=8), [b1], [Pn])
                    P.cp("scalar", PTn[H, :, :], b0[H, :].rearrange("p (q i) -> p q i", q=8), [b0], [PTn])
                    yield
                    for q in range(8):
                        P.mm(b1[H, q * 64:(q + 1) * 64], PTn[H, q, :], Amb[H, q, :], True, True, [PTn, Amb], [b1],
                             nowaw=(q > 0))
                    yield
                    P.tt("vector", Am[H, :, :], b1[H, :].rearrange("p (q i) -> p q i", q=8), Am[H, :, :], ALU.add,
                         [b1, Am], [Am])
                    if m < 4:
                        P.cp("gpsimd", Amb[H, :, :], Am[H, :, :], [Am], [Amb])
                    yield
                P.cp("gpsimd", Wb[H, :, :], Am[H, :, :], [Am], [Wb])
                knv = kn_c[H, 2 * bb:2 * bb + 2, :].rearrange("p n (h d) -> p (n h) d", h=4)
                vv = v_c[H, 2 * bb:2 * bb + 2, :].rearrange("p n (h d) -> p (n h) d", h=4)
                P.tt("gpsimd", kg_b[H, :, :], knv, b3(egc[H, ps], 128), ALU.mult, [kn_c, egc], [kg_b])
                P.tt("gpsimd", kdec_all[H, ps, :], knv, b3(kdc[H, ps], 128), ALU.mult, [kn_c, kdc], [kdec_all], nowaw=(bb > 0))
                yield
                for q in range(8):
                    P.mm(b0[:, q * 64:(q + 1) * 64], kg_b[H, q, :], Wb[H, q, :], True, True, [kg_b, Wb], [b0], nowaw=(q > 0))
                yield
                P.cp("scalar", wT_all[:, ps, :], b0[:, :].rearrange("p (q i) -> p q i", q=8), [b0], [wT_all], nowaw=(bb > 0))
                for hf in range(2):
                    bk_ = b1 if hf == 0 else b0
                    for q4 in range(4):
                        q = hf * 4 + q4
                        P.mm(bk_[H, q4 * 128:(q4 + 1) * 128], Wb[H, q, :], vv[:, q, :], True, True, [Wb, v_c], [bk_],
                             nowaw=(q4 > 0))
                    pq = slice(bb * 8 + hf * 4, bb * 8 + hf * 4 + 4)
                    P.tt("vector", ub_all[H, pq, :], bk_[H, :].rearrange("p (q d) -> p q d", q=4),
                         beta_c[H, pq].unsqueeze(2).to_broadcast([64, 4, 128]), ALU.mult, [bk_, beta_c], [ub_all],
                         nowaw=not (bb == 0 and hf == 0))
                    yield

            for pr_ in ((0, 1), (2, 3)):
                alive = [batch(pr_[0], SETS[0]), batch(pr_[1], SETS[1])]
                while alive:
                    for gi in list(alive):
                        if next(gi, "done") == "done":
                            alive.remove(gi)
                        else:
                            yield

        def scan(g):
            s, tt = tiles[g]
            t0 = tt * 512
            ug = s * NT + tt
            qT_c, z_c, wT_all, ub_all = qT_c2[g % 2], z_c2[g % 2], wT_all2[g % 2], ub_all2[g % 2]
            Aq_all, kdec_all, egl, rq_c, rqe, nbeta = (Aq_all2[g % 2], kdec_all2[g % 2], egl2[g % 2], rq_c2[g % 2],
                                                       rqe2[g % 2], nbeta2[g % 2])
            if tt == 0:
                for h in range(4):
                    P.memset("vector", S[h][:, :], 0.0, [S[h]])
                    P.memset("vector", Sb[h][:, :], 0.0, [Sb[h]])
            for n in range(8):
                for h in range(4):
                    p = n * 4 + h
                    bank = 4 + h
                    B_ = PB[bank]
                    P.mm(B_[H, 0:128], wT_all[:, p, :], Sb[h][:, :], True, True, [wT_all, Sb[h]], [B_])
                    P.mm(B_[H, 128:256], qT_c[:, h, n * 64:(n + 1) * 64], Sb[h][:, :], True, True, [qT_c, Sb[h]], [B_], nowaw=True)
                    P.stt("vector", vnew[h][H, :], B_[H, 0:128], nbeta[H, p:p + 1], ub_all[H, p, :], ALU.mult, ALU.add,
                          [B_, nbeta, ub_all], [vnew[h]])
                    P.ts("vector", t1c[h][H, :], B_[H, 128:256], rqe[H, p:p + 1], None, ALU.mult, None, [B_, rqe], [t1c[h]])
                    P.mm(B_[H, 256:384], Aq_all[H, p, :], vnew[h][H, :], True, True, [Aq_all, vnew[h]], [B_])
                    P.mm(B_[:, 384:512], kdec_all[H, p, :], vnew[h][H, :], True, True, [kdec_all, vnew[h]], [B_], nowaw=True)
                    P.stt("vector", obuf[H, n, h * 128:(h + 1) * 128], B_[H, 256:384], rq_c[H, p:p + 1], t1c[h][H, :],
                          ALU.mult, ALU.add, [B_, rq_c, t1c[h]], [obuf], nowaw=not (n == 0 and h == 0))
                    P.stt("vector", S[h][:, :], S[h][:, :], egl[:, p:p + 1], B_[:, 384:512], ALU.mult, ALU.add,
                          [S[h], egl, B_], [S[h]])
                    P.cp("gpsimd", Sb[h][:, :], S[h][:, :], [S[h]], [Sb[h]])
                    yield
            o3 = obuf[H, :, :].rearrange("p n (h d) -> p (n h) d", h=4)
            P.tt("gpsimd", osq[H, :, :], obuf[H, :, :], obuf[H, :, :], ALU.mult, [obuf], [osq])
            P.op("vector", lambda e: e.reduce_sum(out=ssqg[H, :], in_=osq[H, :, :].rearrange("p n (h d) -> p (n h) d", h=4), axis=AX.X),
                 [osq], [ssqg])
            rsqrt_(rstdg[H, :], ssqg[H, :], 1.0 / 128, negh[H, 0:32], [ssqg], [rstdg])
            yield
            P.tt("vector", o3, o3, rstdg[H, :].unsqueeze(2).to_broadcast([64, 32, 128]), ALU.mult, [obuf, rstdg], [obuf])
            P.tt("gpsimd", o3, o3, gnwB[H, :].unsqueeze(1).to_broadcast([64, 32, 128]), ALU.mult, [obuf, gnwB], [obuf])
            yield
            P.tt("vector", mixg[H, :, :], obuf[H, :, :], z_c[H, :, :], ALU.mult, [obuf, z_c], [mixg])
            P.dma("gpsimd", mix_d[s, t0:t0 + 512, 512:1024].rearrange("(n c) f -> c n f", c=64), mixg[H, :, :], [mixg],
                  [u_mixg[ug]])
            yield

        for _ in pre(0):
            pass
        KADV = 5
        for g in range(len(tiles)):
            gn = pre(g + 1) if g + 1 < len(tiles) else None
            for _ in scan(g):
                if gn is not None:
                    for _k in range(KADV):
                        if next(gn, "done") == "done":
                            gn = None
                            break
            if gn is not None:
                for _ in gn:
                    pass

    areset(0)
    qTb = [aalloc("qTb%d" % i, [T], BF16) for i in range(2)]
    kTb = [aalloc("kTb%d" % i, [T], BF16) for i in range(2)]
    vAb = [aalloc("vAb%d" % i, [NKT, 130], BF16) for i in range(2)]
    pT = [aalloc("pT%d" % i, [512], BF16) for i in range(3)]
    o1 = aalloc("o1", [4, 128], F32)
    o2 = aalloc("o2", [4, 128], F32)
    rinv = aalloc("rinv", [4], F32)
    rinv2 = aalloc("rinv2", [4], F32)
    ssqo = aalloc("ssqo", [4], F32)
    rstdo = aalloc("rstdo", [4], F32)
    mixt = [aalloc("mixt%d" % i, [4, 128], BF16) for i in range(2)]
    lamb = aalloc("lamb", [4, 64], F32)
    lamp = aalloc("lamp", [2, 64], F32)
    lams = aalloc("lams", [2], F32)
    neglam = aalloc("neglam", [1], F32)
    sublnB = aalloc("sublnB", [128], F32)
    junkB = aalloc("junkB", [128], F32)

    for i, dd in enumerate((lq1_d, lk1_d, lq2_d, lk2_d)):
        P.dma("sync", lamb[:, i, :], bc(dd, 128), [CONST], [lamb], nowaw=True)
    P.dma("sync", sublnB[:, :], bc(subln_d, 128), [CONST], [sublnB])
    P.ts("vector", sublnB[:, :], sublnB[:, :], 1.0 - LAMBDA_INIT, None, ALU.mult, None, [sublnB], [sublnB])
    P.tt("vector", lamp[:, 0, :], lamb[:, 0, :], lamb[:, 1, :], ALU.mult, [lamb], [lamp], nowaw=True)
    P.tt("vector", lamp[:, 1, :], lamb[:, 2, :], lamb[:, 3, :], ALU.mult, [lamb], [lamp], nowaw=True)
    P.op("vector", lambda e: e.reduce_sum(out=lams[:, :], in_=lamp[:, :, :], axis=AX.X), [lamp], [lams])
    P.act(lams[:, :], lams[:, :], AF.Exp, [lams], [lams])
    P.tt("vector", neglam[:, :], lams[:, 1:2], lams[:, 0:1], ALU.subtract, [lams], [neglam])
    P.ts("vector", neglam[:, :], neglam[:, :], -LAMBDA_INIT, None, ALU.add, None, [neglam], [neglam])
    for i in range(2):
        P.memset("vector", vAb[i][:, :, 128:130], 1.0, [vAb[i]])

    heads = [(s, h) for s in range(NSEQ) for h in range(4)]

    def loadB(n):
        s, h = heads[n]
        i = n % 2
        rd = [u_qT[s * NT + tt] for tt in range(NT)]
        P.dma("sync", qTb[i][:, :], qT_d[s, h, :, :], rd, [qTb[i]])
        rd = [u_kT[s * NT + tt] for tt in range(NT)]
        P.dma("sync", kTb[i][:, :], kT_d[s, h, :, :], rd, [kTb[i]])
        rd = [u_vda[s * NT + tt] for tt in range(NT)]
        nq = max(1, NKT // 8)
        for a in range(0, NKT, nq):
            P.dma("sync", vAb[i][:, a:a + nq, 0:128],
                  vda_d[s, a * 128:(a + nq) * 128, h * 128:(h + 1) * 128].rearrange("(n p) d -> p n d", p=128),
                  rd, [vAb[i]], nowaw=(a > 0))

    loadB(0)
    kidx = 0
    ucnt = 0
    pending = [None]

    def flush():
        if pending[0] is not None:
            f = pending[0]
            pending[0] = None
            f()

    for n, (s, h) in enumerate(heads):
        flush()
        if n + 1 < len(heads):
            loadB(n + 1)
        qb, kb, vb = qTb[n % 2], kTb[n % 2], vAb[n % 2]
        for qt in range(NT):
            for c in range(2):
                pob = (4, 5) if ucnt % 2 == 0 else (6, 7)
                ucnt += 1
                fresh = {pob[0]: True, pob[1]: True}
                nk = 4 * qt + 4
                cs = slice(c * 64, (c + 1) * 64)
                for kt in range(nk):
                    j = kt - 4 * qt
                    sbk = PB[kidx % 3]
                    pt = pT[kidx % 3]
                    kidx += 1
                    ksl = kb[cs, kt * 128:(kt + 1) * 128]
                    q0 = qt * 512
                    if j < 0:
                        lo = 0
                        P.mm(sbk[:, 0:512], ksl, qb[cs, q0:q0 + 512], True, True, [kb, qb], [sbk])
                    else:
                        lo = j * 128
                        P.mm(sbk[:, lo:lo + 128], ksl, qb[cs, q0 + lo:q0 + lo + 128], True, False, [kb, qb], [sbk])
                        P.mm(sbk[:, lo:lo + 128], identb[:, :], trim[:, :], False, True, [identb, trim], [sbk], nowaw=True)
                        if lo + 128 < 512:
                            P.mm(sbk[:, lo + 128:512], ksl, qb[cs, q0 + lo + 128:q0 + 512], True, True, [kb, qb], [sbk],
                                 nowaw=True)
                    P.act(pt[:, lo:512], sbk[:, lo:512], AF.Exp, [sbk], [pt], scale=0.125)
                    flush()

                    def pv(kt=kt, j=j, pt=pt, vb=vb, pob=pob, fresh=fresh, qt=qt, c=c, nk=nk, s=s, h=h, n=n):
                        for i in range(max(j, 0), 4):
                            bank = pob[i // 2]
                            off = (i % 2) * 130
                            P.mm(PB[bank][:, off:off + 129], pt[:, i * 128:(i + 1) * 128], vb[:, kt, 0:129],
                                 fresh[bank], (kt == 4 * qt + i and i % 2 == 1), [pt, vb], [PB[bank]], nowaw=not fresh[bank])
                            fresh[bank] = False
                        if kt != nk - 1:
                            return
                        for i in range(4):
                            bank = pob[i // 2]
                            off = (i % 2) * 130
                            P.op("vector", (lambda bank=bank, off=off, i=i: (lambda e: e.reciprocal(out=rinv[:, i:i + 1], in_=PB[bank][:, off + 128:off + 129])))(),
                                 [PB[bank]], [rinv], nowaw=(i > 0))
                            if c == 0:
                                P.ts("vector", o1[:, i, :], PB[bank][:, off:off + 128], rinv[:, i:i + 1], None, ALU.mult, None,
                                     [PB[bank], rinv], [o1], nowaw=(i > 0))
                            else:
                                P.ts("vector", rinv2[:, i:i + 1], rinv[:, i:i + 1], neglam[:, 0:1], None, ALU.mult, None,
                                     [rinv, neglam], [rinv2], nowaw=(i > 0))
                                P.stt("vector", o2[:, i, :], PB[bank][:, off:off + 128], rinv2[:, i:i + 1], o1[:, i, :],
                                      ALU.mult, ALU.add, [PB[bank], rinv2, o1], [o2], nowaw=(i > 0))
                        if c == 0:
                            return
                        mt = mixt[(n * NT + qt) % 2]
                        P.memset("vector", ssqo[:, :], 0.0, [ssqo])
                        for i in range(4):
                            P.act(junkB[:, :], o2[:, i, :], AF.Square, [o2, ssqo], [junkB, ssqo], accum=ssqo[:, i:i + 1], nowaw=True)
                        rsqrt_(rstdo[:, :], ssqo[:, :], 1.0 / 128, negh[:, 0:4], [ssqo], [rstdo])
                        for i in range(4):
                            P.stt("vector", mt[:, i, :], o2[:, i, :], rstdo[:, i:i + 1], sublnB[:, :], ALU.mult, ALU.mult,
                                  [o2, rstdo, sublnB], [mt], nowaw=(i > 0))
                        P.dma("gpsimd", mix_d[s, qt * 512:(qt + 1) * 512, h * 128:(h + 1) * 128].rearrange("(i p) d -> p i d", p=128),
                              mt[:, :, :], [mt], [u_mixa[s * NT + qt]], nowaw=True)

                    pending[0] = pv
    flush()

    if stop_after == "B":
        P.emit()
        return nc
    if do_c:
        phase_c()
    if stop_after == "C":
        P.emit()
        return nc

    areset(0)
    Wout = aalloc("Wout", [8, D], BF16)
    xh = [aalloc("xh%d" % i, [4, D], F32) for i in range(2)]
    wst = [xh[1][:, 0, :], xh[1][:, 1, :]]
    mixb = aalloc("mixb", [4, D], BF16)
    mixT = aalloc("mixTD", [8, 512], BF16)
    wup = [aalloc("wup%d" % i, [2, 8, 256], BF16) for i in range(3)]
    wdn = [aalloc("wdn%d" % i, [11, 512], BF16) for i in range(2)]
    aT = aalloc("aT", [22, 512], BF16)
    hp = [aalloc("hp%d" % i, [4, 514], BF16) for i in range(2)]
    halo = aalloc("halo", [11, 4, 2], BF16)
    fdiag = aalloc("fdiag", [44, 3, 128], BF16)
    fcw = aalloc("fcw", [44, 3], F32)
    gt = [aalloc("gt%d" % i, [512], BF16) for i in range(4)]
    nwF = aalloc("nwF", [D], F32)
    nwO = aalloc("nwO", [D], F32)
    junkD = aalloc("junkD", [D], BF16)
    ssq2 = aalloc("ssq2", [4], F32)
    rstd2 = aalloc("rstd2", [4], F32)

    for k in range(8):
        P.dma("sync", wst[k % 2], wout_d[k * 128:(k + 1) * 128, :], [CONST], [xh[1]])
        P.cp(cast_engs[k % 2], Wout[:, k, :], wst[k % 2], [xh[1]], [Wout], nowaw=True)
    P.dma("sync", fcw[:, :, :], fcw_d[:, :, :], [CONST], [fcw])
    di = 0
    for cc in range(44):
        for j in range(3):
            eng_ = "vector" if di % 2 == 0 else "gpsimd"
            di += 1
            P.ts(eng_, fdiag[:, cc, j, :], identb[:, :], fcw[:, cc, j:j + 1], 1.0, ALU.mult, ALU.mult,
                 [identb, fcw], [fdiag], nowaw=True)
    P.dma("sync", nwF[:, :], bc(fnw_d, 128), [CONST], [nwF])
    P.dma("sync", nwO[:, :], bc(finw_d, 128), [CONST], [nwO])

    def loadD(g):
        s, tt = tiles[g]
        t0 = tt * 512
        P.dma("sync", xh[g % 2][:, :, :], x_d[s, t0:t0 + 512, :].rearrange("(j p) d -> p j d", p=128), [CONST], [xh[g % 2]])

    def load_wup(grp):
        sl = wup[grp % 3]
        P.dma("sync", sl[:, 0, :, :], wupb_d[:, grp * 256:(grp + 1) * 256].rearrange("(k p) c -> p k c", p=128),
              [u_wupb], [sl])
        P.dma("sync", sl[:, 1, :, :], wupb_d[:, DFF + grp * 256:DFF + (grp + 1) * 256].rearrange("(k p) c -> p k c", p=128),
              [u_wupb], [sl], nowaw=True)

    def load_wdn(idx):
        half, piece = idx // 2, idx % 2
        sl = wdn[idx % 2]
        P.dma("sync", sl[:, :, :],
              wdnb_d[piece * 11 * 128:(piece + 1) * 11 * 128, half * 512:(half + 1) * 512].rearrange("(f p) c -> p f c", p=128),
              [u_wdnb], [sl])

    def rms_to(src, dst, nw):
        P.memset("vector", ssq2[:, :], 0.0, [ssq2])
        for j in range(4):
            P.act(junkD[:, :], src[:, j, :], AF.Square, [src, ssq2], [junkD, ssq2], accum=ssq2[:, j:j + 1], nowaw=True)
        rsqrt_(rstd2[:, :], ssq2[:, :], 1.0 / D, negh[:, 0:4], [ssq2], [rstd2])
        for j in range(4):
            P.stt("vector", dst[:, j, :], src[:, j, :], rstd2[:, j:j + 1], nw[:, :], ALU.mult, ALU.mult,
                  [src, rstd2, nw], [dst], nowaw=(j > 0))

    def transpose_to(src, dst):
        for k in range(8):
            bk = k % 2
            for j in range(4):
                P.tr(pbf(bk)[:, j * 128:(j + 1) * 128], src[:, j, k * 128:(k + 1) * 128], identb[:, :],
                     [src, identb], [PB[bk]], nowaw=(j > 0))
            P.cp(ev_eng(), dst[:, k, :], pbf(bk)[:, 0:512], [PB[bk]], [dst], nowaw=(k > 0))

    loadD(0)
    for g, (s, tt) in enumerate(tiles):
        t0 = tt * 512
        ug = s * NT + tt
        if g + 1 < len(tiles):
            loadD(g + 1)
        xt = xh[g % 2]
        rdm = [u_mixa[ug]] + ([u_mixg[ug]] if do_c else [])
        P.dma("sync", mixb[:, :, :], mix_d[s, t0:t0 + 512, :].rearrange("(j p) d -> p j d", p=128), rdm, [mixb])
        load_wup(0)
        load_wup(1)
        transpose_to(mixb, mixT)
        bi = 0
        for j in range(4):
            for hf in range(2):
                bank = 2 + (bi % 2)
                bi += 1
                for k in range(8):
                    P.mm(PB[bank][:, :], mixT[:, k, j * 128:(j + 1) * 128], Wout[:, k, hf * 512:(hf + 1) * 512],
                         k == 0, k == 7, [mixT, Wout], [PB[bank]], nowaw=(k > 0))
                P.tt("vector", xt[:, j, hf * 512:(hf + 1) * 512], PB[bank][:, :], xt[:, j, hf * 512:(hf + 1) * 512], ALU.add,
                     [PB[bank], xt], [xt])
        rms_to(xt, mixb, nwF)
        transpose_to(mixb, mixT)
        if tt == 0:
            P.memset("vector", halo[:, :, :, :], 0.0, [halo])
        def up(grp):
            if grp + 2 < 11:
                load_wup(grp + 2)
            if grp == 9:
                load_wdn(0)
            if grp == 10:
                load_wdn(1)
            sl = wup[grp % 3]
            hpb = hp[grp % 2]
            P.cp("gpsimd", hpb[:, :, 0:2], halo[:, grp, :, :], [halo], [hpb])
            for q in range(4):
                bank = 2 + (q % 2)
                for k in range(8):
                    P.mm(PB[bank][:, :], sl[:, q // 2, k, (q % 2) * 128:(q % 2) * 128 + 128], mixT[:, k, :], k == 0, k == 7,
                         [sl, mixT], [PB[bank]], nowaw=(k > 0))
                P.cp(ev_eng(), hpb[:, q, 2:514], PB[bank][:, :], [PB[bank]], [hpb], nowaw=True)

        def conv(grp):
            hpb = hp[grp % 2]
            chunks = [2 * grp, 2 * grp + 1, 22 + 2 * grp, 23 + 2 * grp]
            for q in range(4):
                bank = 4 + (q % 2)
                for j in range(3):
                    P.mm(PB[bank][:, :], fdiag[:, chunks[q], j, :], hpb[:, q, j:j + 512], j == 0, j == 2, [fdiag, hpb], [PB[bank]],
                         nowaw=(j > 0))
                if q < 2:
                    P.act(gt[(grp % 2) * 2 + q][:, :], PB[bank][:, :], AF.Silu, [PB[bank]], [gt[(grp % 2) * 2 + q]])
                else:
                    P.tt("vector", aT[:, 2 * grp + q - 2, :], PB[bank][:, :], gt[(grp % 2) * 2 + q - 2][:, :], ALU.mult,
                         [PB[bank], gt[(grp % 2) * 2 + q - 2]], [aT], nowaw=True)
            P.cp("gpsimd", halo[:, grp, :, :], hpb[:, :, 512:514], [hpb], [halo], nowaw=True)

        up(0)
        for grp in range(11):
            if grp + 1 < 11:
                up(grp + 1)
            conv(grp)
        for idx in range(4):
            half, piece = idx // 2, idx % 2
            sl = wdn[idx % 2]
            for f in range(11):
                fc = piece * 11 + f
                for j in range(4):
                    P.mm(PB[4 + j][:, :], aT[:, fc, j * 128:(j + 1) * 128], sl[:, f, :], fc == 0, fc == 21,
                         [aT, sl], [PB[4 + j]], nowaw=(fc > 0))
            if idx + 2 < 4:
                load_wdn(idx + 2)
            if piece == 1:
                for j in range(4):
                    P.tt("vector", xt[:, j, half * 512:(half + 1) * 512], PB[4 + j][:, :],
                         xt[:, j, half * 512:(half + 1) * 512], ALU.add, [PB[4 + j], xt], [xt])
        rms_to(xt, xt, nwO)
        P.dma("gpsimd", out_d[s, t0:t0 + 512, :].rearrange("(j p) d -> p j d", p=128), xt[:, :, :], [xt], [CONST_OUT])

    P.emit()
    return nc


def host_consts(T):
    identf = np.eye(128, dtype=np.float32)
    identb = identf.astype(ml_dtypes.bfloat16)
    rot = np.zeros((128, 128), np.float32)
    for g in range(2):
        for d in range(32):
            rot[g * 64 + d + 32, g * 64 + d] = -1.0
            rot[g * 64 + d, g * 64 + d + 32] = 1.0
    inv_freq = (10000.0 ** (-np.arange(0, 64, 2, dtype=np.float32) / 64)).astype(np.float32)
    ang = np.arange(T, dtype=np.float32)[None, :] * inv_freq[:, None]
    cos = np.cos(ang).astype(np.float32); sin = np.sin(ang).astype(np.float32)
    rope = np.zeros((128, 2, T), np.float32)
    for r in range(128):
        rope[r, 0] = cos[r % 32]
        rope[r, 1] = sin[r % 32]
    kk = np.arange(128)[:, None]; qq = np.arange(128)[None, :]
    trim = np.where(kk <= qq, 0.0, -30000.0).astype(np.float32).astype(ml_dtypes.bfloat16)
    s = np.arange(64)[:, None]; i = np.arange(64)[None, :]
    cm = np.zeros((64, 5, 64), np.float32)
    cm[:, 0] = (s <= i); cm[:, 1] = (s < i); cm[:, 2] = (s <= i); cm[:, 3] = (s == i); cm[:, 4] = (s > i)
    return {"c_identb": identb, "c_identf": identf, "c_rot": rot.astype(ml_dtypes.bfloat16), "c_rope": rope,
            "c_trimask": trim, "c_masks": cm}


_W1 = ["attn_norm_w", "w_in", "da_lambda_q1", "da_lambda_k1", "da_lambda_q2", "da_lambda_k2", "da_subln_w",
       "gdn_conv_w", "gdn_a_log", "gdn_dt_bias", "gdn_norm_w", "w_out", "ffn_norm_w", "ffn_w_up", "ffn_conv_w",
       "ffn_w_down"]


def make_in_maps(inputs, n_cores, nseq, T):
    consts = host_consts(T)
    base = {k: np.ascontiguousarray(np.asarray(inputs[k], np.float32)[0]) for k in _W1}
    base["final_norm_w"] = np.ascontiguousarray(np.asarray(inputs["final_norm_w"], np.float32))
    base["gdn_conv_w"] = np.ascontiguousarray(base["gdn_conv_w"].reshape(4, 12, 128).transpose(2, 1, 0))
    base["ffn_conv_w"] = np.ascontiguousarray(base["ffn_conv_w"].reshape(3, 44, 128).transpose(2, 1, 0))
    base.update(consts)
    x = np.asarray(inputs["x"], np.float32)
    maps = []
    for c in range(n_cores):
        m = dict(base)
        m["x"] = np.ascontiguousarray(x[c * nseq:(c + 1) * nseq])
        maps.append(m)
    return maps


def kernel(**inputs):
    x = inputs["x"]
    B, T, _ = x.shape
    n = 8
    nseq = B // n
    nc = build(T, nseq)
    maps = make_in_maps(inputs, n, nseq, T)
    res = run_bass_kernel_spmd(nc, maps, core_ids=list(range(n)))
    return np.concatenate([r["out"] for r in res.results], axis=0)
```
